# Optimizing a Trainium2 kernel written in Bass

```python
import jax, jax.numpy as jnp
from jax import lax
import numpy as np

D_MODEL = 1024
BATCH = 2
SEQ = 8192
DEPTH = 2

N_A_LAYERS = DEPTH // 2
N_B_LAYERS = DEPTH - N_A_LAYERS

D_FF = ((8 * D_MODEL // 3 + 255) // 256) * 256
N_FFN_PER_LAYER = 2

POOL_WINDOWS = (2, 4, 8, 16)
POOL_GROUPS = len(POOL_WINDOWS)
POOL_GROUP_DIM = D_MODEL // POOL_GROUPS

ATTN_GROUPS = ((128, 1), (512, 4), (2048, 16))
N_ATTN_GROUPS = len(ATTN_GROUPS)
HEAD_DIM = 64
HEADS_PER_GROUP = D_MODEL // (2 * HEAD_DIM)
QKV_WIDTH = N_ATTN_GROUPS * HEADS_PER_GROUP * HEAD_DIM
O_WIDTH = HEADS_PER_GROUP * HEAD_DIM
ROPE_THETA = 10000.0

N_NORMS_PER_LAYER = 6
RMS_EPS = 1e-6
NEG_BIG = -1e30

kernel_name = "yoco_pool_dilated_attn_macaron"


def rms_norm(x, gain):
    x32 = x.astype(jnp.float32)
    y = x32 * lax.rsqrt(jnp.mean(x32 * x32, axis=-1, keepdims=True) + RMS_EPS) * gain.astype(jnp.float32)
    return y.astype(x.dtype)


def swiglu(h, w_gate, w_up, w_down):
    return (jax.nn.silu(h @ w_gate) * (h @ w_up)) @ w_down


def rope_tables(positions):
    inv_freq = ROPE_THETA ** (-jnp.arange(0, HEAD_DIM, 2, dtype=jnp.float32) / HEAD_DIM)
    ang = positions.astype(jnp.float32)[..., None] * inv_freq
    return jnp.cos(ang), jnp.sin(ang)


def apply_rope(x, cos, sin):
    extra = x.ndim - 3
    shape = cos.shape[:2] + (1,) * extra + cos.shape[-1:]
    c = cos.reshape(shape)
    s = sin.reshape(shape)
    x32 = x.astype(jnp.float32)
    x1, x2 = jnp.split(x32, 2, axis=-1)
    out = jnp.concatenate([x1 * c - x2 * s, x2 * c + x1 * s], axis=-1)
    return out.astype(x.dtype)


def pool_mixer(h, w_in, w_group, scale, w_out):
    B, S, _ = h.shape
    u = h @ w_in
    u32 = u.astype(jnp.float32)
    csum = jnp.cumsum(u32, axis=1)
    t = jnp.arange(S)
    pooled = []
    for g, w in enumerate(POOL_WINDOWS):
        lo, hi = g * POOL_GROUP_DIM, (g + 1) * POOL_GROUP_DIM
        c = csum[..., lo:hi]
        c_prev = jnp.pad(c, ((0, 0), (w, 0), (0, 0)))[:, :S]
        count = jnp.minimum(t + 1, w).astype(jnp.float32)[None, :, None]
        pooled.append((c - c_prev) / count - u32[..., lo:hi])
    p = jnp.stack(pooled, axis=2).astype(u.dtype)
    y = jnp.einsum('bsgc,gcd->bsgd', p, w_group).reshape(B, S, D_MODEL) * scale
    return y @ w_out


def dilated_window_attention(q, k, v, window, dilation):
    B, S, H, Dh = q.shape
    blk = window // dilation
    span = blk * dilation
    s_pad = -(-S // span) * span
    nb = s_pad // span

    def split(a):
        a = jnp.pad(a, ((0, 0), (0, s_pad - S), (0, 0), (0, 0)))
        return a.reshape(B, nb, blk, dilation, H, Dh)

    def with_prev(a):
        prev = jnp.pad(a[:, :-1], ((0, 0), (1, 0), (0, 0), (0, 0), (0, 0), (0, 0)))
        return jnp.concatenate([prev, a], axis=2)

    qb = split(q)
    kc = with_prev(split(k))
    vc = with_prev(split(v))
    s = jnp.einsum('bnqrhd,bnkrhd->bnrhqk', qb, kc, preferred_element_type=jnp.float32) * (Dh ** -0.5)
    qi = jnp.arange(blk)[:, None]
    kj = jnp.arange(2 * blk)[None, :]
    band = (kj >= qi) & (kj <= qi + blk)
    valid_prev = (jnp.arange(nb)[:, None, None] > 0) | (kj[None] >= blk)
    mask = band[None] & valid_prev
    s = jnp.where(mask[None, :, None, None], s, NEG_BIG)
    m = jnp.max(s, axis=-1, keepdims=True)
    p = jnp.exp(s - m)
    den = jnp.sum(p, axis=-1, keepdims=True)
    o = jnp.einsum('bnrhqk,bnkrhd->bnqrhd', (p / den).astype(v.dtype), vc)
    lse = (m + jnp.log(den))[..., 0]
    o = o.reshape(B, s_pad, H, Dh)[:, :S]
    lse = lse.transpose(0, 1, 4, 2, 3).reshape(B, s_pad, H)[:, :S]
    return o, lse


def shared_kv(h, kv_gain, w_k, w_v, cos, sin):
    B, S, _ = h.shape
    hk = rms_norm(h, kv_gain)
    k = (hk @ w_k).reshape(B, S, N_ATTN_GROUPS, HEADS_PER_GROUP, HEAD_DIM)
    v = (hk @ w_v).reshape(B, S, N_ATTN_GROUPS, HEADS_PER_GROUP, HEAD_DIM)
    return apply_rope(k, cos, sin), v


def dilated_mixer(h, cos, sin, w_q, k_sh, v_sh, w_o):
    B, S, _ = h.shape
    q = (h @ w_q).reshape(B, S, N_ATTN_GROUPS, HEADS_PER_GROUP, HEAD_DIM)
    q = apply_rope(q, cos, sin)
    outs, lses = [], []
    for g, (window, dilation) in enumerate(ATTN_GROUPS):
        o, l = dilated_window_attention(q[:, :, g], k_sh[:, :, g], v_sh[:, :, g], window, dilation)
        outs.append(o)
        lses.append(l)
    wts = jax.nn.softmax(jnp.stack(lses, axis=0), axis=0)
    o = jnp.sum(wts[..., None] * jnp.stack(outs, axis=0).astype(jnp.float32), axis=0).astype(h.dtype)
    return o.reshape(B, S, O_WIDTH) @ w_o


def setup_inputs(seed: int = 0) -> dict:
    key = jax.random.key(seed)
    ks = jax.random.split(key, 16)
    f32 = jnp.float32

    def w(k, shape, fan_in):
        return jax.random.normal(k, shape, f32) * (fan_in ** -0.5)

    x = jax.random.normal(ks[0], (BATCH, SEQ, D_MODEL), f32)
    offset = jax.random.randint(ks[1], (BATCH, 1), 0, 1024, dtype=jnp.int32)
    positions = (jnp.arange(SEQ, dtype=jnp.int32)[None, :] + offset).astype(jnp.int32)
    norm_gain = 1.0 + 0.05 * jax.random.normal(ks[2], (DEPTH, N_NORMS_PER_LAYER, D_MODEL), f32)
    ffn_w_gate = w(ks[3], (DEPTH, N_FFN_PER_LAYER, D_MODEL, D_FF), D_MODEL)
    ffn_w_up = w(ks[4], (DEPTH, N_FFN_PER_LAYER, D_MODEL, D_FF), D_MODEL)
    ffn_w_down = w(ks[5], (DEPTH, N_FFN_PER_LAYER, D_FF, D_MODEL), D_FF)
    pool_w_in = w(ks[6], (N_A_LAYERS, D_MODEL, D_MODEL), D_MODEL)
    pool_w_group = w(ks[7], (N_A_LAYERS, POOL_GROUPS, POOL_GROUP_DIM, POOL_GROUP_DIM), POOL_GROUP_DIM)
    pool_scale = 1.0 + 0.1 * jax.random.normal(ks[8], (N_A_LAYERS, D_MODEL), f32)
    pool_w_out = w(ks[9], (N_A_LAYERS, D_MODEL, D_MODEL), D_MODEL)
    kv_norm_gain = 1.0 + 0.05 * jax.random.normal(ks[10], (D_MODEL,), f32)
    w_k = w(ks[11], (D_MODEL, QKV_WIDTH), D_MODEL)
    w_v = w(ks[12], (D_MODEL, QKV_WIDTH), D_MODEL)
    attn_w_q = w(ks[13], (N_B_LAYERS, D_MODEL, QKV_WIDTH), D_MODEL)
    attn_w_o = w(ks[14], (N_B_LAYERS, O_WIDTH, D_MODEL), O_WIDTH)
    return {
        "x": x, "positions": positions, "norm_gain": norm_gain,
        "ffn_w_gate": ffn_w_gate, "ffn_w_up": ffn_w_up, "ffn_w_down": ffn_w_down,
        "pool_w_in": pool_w_in, "pool_w_group": pool_w_group, "pool_scale": pool_scale,
        "pool_w_out": pool_w_out, "kv_norm_gain": kv_norm_gain, "w_k": w_k, "w_v": w_v,
        "attn_w_q": attn_w_q, "attn_w_o": attn_w_o,
    }


def reference(x, positions, norm_gain, ffn_w_gate, ffn_w_up, ffn_w_down, pool_w_in, pool_w_group,
              pool_scale, pool_w_out, kv_norm_gain, w_k, w_v, attn_w_q, attn_w_o):
    cos, sin = rope_tables(positions)
    h = x
    k_sh = None
    v_sh = None
    for layer in range(DEPTH):
        g = norm_gain[layer]
        f = swiglu(rms_norm(h, g[0]), ffn_w_gate[layer, 0], ffn_w_up[layer, 0], ffn_w_down[layer, 0])
        h = h + 0.5 * rms_norm(f, g[1])
        hm = rms_norm(h, g[2])
        if layer < N_A_LAYERS:
            mix = pool_mixer(hm, pool_w_in[layer], pool_w_group[layer], pool_scale[layer], pool_w_out[layer])
        else:
            b = layer - N_A_LAYERS
            mix = dilated_mixer(hm, cos, sin, attn_w_q[b], k_sh, v_sh, attn_w_o[b])
        h = h + rms_norm(mix, g[3])
        f = swiglu(rms_norm(h, g[4]), ffn_w_gate[layer, 1], ffn_w_up[layer, 1], ffn_w_down[layer, 1])
        h = h + 0.5 * rms_norm(f, g[5])
        if layer == N_A_LAYERS - 1:
            k_sh, v_sh = shared_kv(h, kv_norm_gain, w_k, w_v, cos, sin)
    return h
```

```python
import contextlib
import os
KSTEP = int(os.environ.get('KSTEP', '9'))
import numpy as np
import ml_dtypes
import concourse.bass as bass
import concourse.mybir as mybir
from concourse.bass_utils import run_bass_kernel_spmd

F32 = mybir.dt.float32
BF16 = mybir.dt.bfloat16
I32 = mybir.dt.int32
AF = mybir.ActivationFunctionType
ALU = mybir.AluOpType

D = 1024
DFF = 2816
NT = 2048
HALO = 128
HC = HALO + NT
DC = 8
FC = 22
QKV = 1536
EPS = 1e-6
NEG = -30000.0
NWG = 2
GROUPS = ((128, 1), (512, 4), (2048, 16))
ENGS = ("pe", "act", "dve", "pool", "sp")
SAME_ENGINE_FREE = ("pe", "sp")
MAGIC = 12582912.0
TWO_PI = 6.283185307179586
C1 = 6.28125
C2 = TWO_PI - C1

TILES = [(0, HALO)] + [(HALO + 512 * i, 512) for i in range(4)]


class Sched:
    def __init__(self):
        self.q = {e: [] for e in ENGS}
        self.cnt = {e: 0 for e in ENGS}
        self.dcnt = {}
        self.seen = {e: {} for e in ENGS}
        self.last_w = {}
        self.readers = {}
        self.fence_tok = None
        self.fence_done = set(ENGS)

    def fence(self):
        tok = {("e", e): self.cnt[e] for e in ENGS}
        for k, v in self.dcnt.items():
            tok[("d", k)] = v
        self.fence_tok = tok
        self.fence_done = set()

    def op(self, eng, fn, reads=(), writes=(), dma=None):
        waits = {}
        writes = list(writes) + [k for k in reads if isinstance(k, tuple) and k[0] == "ps" and k not in writes]

        def need(dep):
            if dep is None:
                return
            sk, val = dep
            if sk == ("e", eng) and eng in SAME_ENGINE_FREE:
                return
            if sk[0] == "d":
                val = self.dcnt[sk[1]]
            if val <= 0 or self.seen[eng].get(sk, 0) >= val:
                return
            if waits.get(sk, 0) < val:
                waits[sk] = val

        if eng not in self.fence_done:
            for sk, v in self.fence_tok.items():
                need((sk, v))
            self.fence_done.add(eng)
        for k in reads:
            need(self.last_w.get(k))
        for k in writes:
            need(self.last_w.get(k))
            for r in self.readers.get(k, ()):
                need(r)
        for sk, v in waits.items():
            self.seen[eng][sk] = v
        if dma is None:
            self.cnt[eng] += 1
            tok = (("e", eng), self.cnt[eng])
        else:
            self.dcnt[dma] = self.dcnt.get(dma, 0) + 16
            tok = (("d", dma), self.dcnt[dma])
        for k in writes:
            self.last_w[k] = tok
            self.readers[k] = []
        for k in reads:
            self.readers.setdefault(k, []).append(tok)
        self.q[eng].append((sorted(waits.items(), key=str), fn, tok))


class Arena:
    def __init__(self, ap, nwords):
        self.ap = ap
        self.n = nwords
        self.off = 0

    def alloc(self, free, dtype=F32):
        if isinstance(free, int):
            free = (free,)
        n = int(np.prod(free))
        words = n if dtype != BF16 else (n + 1) // 2
        words = (words + 7) // 8 * 8
        assert self.off + words <= self.n, ("arena overflow", self.off, words, self.n)
        v = self.ap[:, self.off:self.off + words]
        self.off += words
        if dtype == BF16:
            v = v.bitcast(BF16)
        elif dtype == I32:
            v = v.bitcast(I32)
        v = v[:, 0:n]
        if len(free) == 2:
            v = v.rearrange("p (a b) -> p a b", a=free[0])
        elif len(free) == 3:
            v = v.rearrange("p (a b c) -> p a b c", a=free[0], b=free[1])
        return v


class Pool:
    def __init__(self, name, bufs):
        self.name = name
        self.bufs = bufs
        self.i = -1

    def next(self):
        self.i = (self.i + 1) % len(self.bufs)
        return self.bufs[self.i], (self.name, self.i)


def build(upto=99, dbg=False):
    nc = bass.Bass("TRN2", target_bir_lowering=False)
    S = Sched()
    A_ = True

    def din(name, shape, dt=F32):
        return nc.dram_tensor(name, list(shape), dt, kind="ExternalInput").ap()

    def dout(name, shape, dt=F32):
        return nc.dram_tensor(name, list(shape), dt, kind="ExternalOutput").ap()

    xT_prev = din("xT_prev", [128, DC, HC])
    xT_own = din("xT_own", [128, DC, HC])
    wg = din("wg", [4, FC, 128, DC * 128])
    wu = din("wu", [4, FC, 128, DC * 128])
    wd = din("wd", [4, DC, 128, FC * 128])
    NV = 14 * 8
    vecs = din("vecs", [128, NV])
    pos_prev = din("pos_prev", [1, NT], I32)
    pos_own = din("pos_own", [1, NT], I32)
    rconst = din("rconst", [128, 2])
    permM = din("permM", [128, 128], BF16)
    w_in = din("w_in", [128, DC, D])
    w_grp = din("w_grp", [128, 4, 2, 256])
    w_out = din("w_out", [128, DC, D])
    w_k = din("w_k", [128, DC, QKV])
    w_v = din("w_v", [128, DC, QKV])
    corr_prev = din("corr_prev", [128, 4, 128])
    corr_own = din("corr_own", [128, 4, 128])
    w_q = din("w_q", [128, DC, QKV])
    w_o = din("w_o", [128, 4, D])
    masks = din("masks", [128, 2, 512], BF16)
    masks_id = din("masks_id", [128, 128], BF16)
    hT_out = dout("hT", [128, DC, NT])
    skind = "ExternalOutput" if dbg else "Internal"
    kin = [nc.dram_tensor("kin%d" % g, [4, 128, (GROUPS[g][1] + 16) * 128], BF16, kind=skind).ap() for g in range(3)]
    vin = [nc.dram_tensor("vin%d" % g, [4, 128, (GROUPS[g][1] + 16), 128], BF16, kind=skind).ap() for g in range(3)]
    tab_prev = nc.dram_tensor("tab_prev", [128, 2, NT], F32).ap()
    tab_own = nc.dram_tensor("tab_own", [128, 2, NT], F32).ap()

    es = contextlib.ExitStack()
    with es:
        AW = 53000
        arena_t = es.enter_context(nc.sbuf_tensor("arena", [128, AW], F32))
        AR = Arena(arena_t[:], AW)
        banks = [es.enter_context(nc.psum_tensor("bank%d" % i, [128, 512], F32)) for i in range(8)]
        PS = Pool("ps", [b[:] for b in banks])

        H = AR.alloc((DC, HC))
        G = AR.alloc(NV)
        RC = AR.alloc(2)
        onesD = AR.alloc(128, BF16)
        ones4D = AR.alloc(128, BF16)
        ones1 = AR.alloc(128, BF16)
        ident = AR.alloc(128, BF16)
        perm_sb = AR.alloc(128, BF16)
        epsb = AR.alloc(2)
        tabs = {}

        def Hk(c, ti):
            return ("H", c, ti)

        S.op("sp", lambda e: e.dma_start(out=G, in_=vecs), writes=["G"], dma="ld")
        S.op("sp", lambda e: e.dma_start(out=RC, in_=rconst), writes=["RC"], dma="ld")
        S.op("sp", lambda e: e.dma_start(out=perm_sb, in_=permM), writes=["perm"], dma="ld")
        S.op("dve", lambda e: e.memset(onesD, 1.0 / D), writes=["onesD"])
        S.op("dve", lambda e: e.memset(ones4D, 4.0 / D), writes=["ones4D"])
        S.op("dve", lambda e: e.memset(ones1, 1.0), writes=["ones1"])
        S.op("dve", lambda e: e.memset(epsb[:, 0:1], EPS), writes=["epsb"])
        S.op("dve", lambda e: e.memset(epsb[:, 1:2], 4 * EPS), writes=["epsb"])

        def rope_precompute(pos, dst):
            m = AR.off
            CT = AR.alloc(NT)
            ST = AR.alloc(NT)
            posi = AR.alloc(NT, I32)
            ang = AR.alloc(NT)
            t1 = AR.alloc(NT)
            t2 = AR.alloc(NT)
            S.op("sp", lambda e: e.dma_start(out=posi, in_=pos.to_broadcast([128, NT])), writes=["posi"], dma="ld")
            S.op("dve", lambda e: e.tensor_copy(out=ang, in_=posi), reads=["posi"], writes=["ang"])
            S.op("dve", lambda e: e.tensor_scalar(ang, ang, RC[:, 0:1], None, ALU.mult), reads=["ang", "RC"],
                 writes=["ang"])
            for which, dst_sb, shift in (("s", ST, 0.0), ("c", CT, np.pi / 2)):
                S.op("dve", lambda e, shift=shift: e.tensor_scalar(t1, ang, float(shift), None, ALU.add),
                     reads=["ang"], writes=["t1"])
                S.op("dve", lambda e: e.tensor_scalar(t2, t1, 1.0 / TWO_PI, MAGIC, ALU.mult, ALU.add),
                     reads=["t1"], writes=["t2"])
                S.op("dve", lambda e: e.tensor_scalar(t2, t2, MAGIC, None, ALU.subtract), reads=["t2"], writes=["t2"])
                S.op("dve", lambda e: e.scalar_tensor_tensor(out=t1, in0=t2, scalar=-C1, in1=t1, op0=ALU.mult,
                                                             op1=ALU.add), reads=["t1", "t2"], writes=["t1"])
                S.op("dve", lambda e: e.scalar_tensor_tensor(out=t1, in0=t2, scalar=-C2, in1=t1, op0=ALU.mult,
                                                             op1=ALU.add), reads=["t1", "t2"], writes=["t1"])
                S.op("dve", lambda e: e.tensor_scalar(t1, t1, 3.1415925, -3.1415925, ALU.min, ALU.max),
                     reads=["t1"], writes=["t1"])
                if which == "s":
                    S.op("act", lambda e, dst_sb=dst_sb: e.activation(out=dst_sb, in_=t1, func=AF.Sin,
                                                                      scale=RC[:, 1:2]),
                         reads=["t1", "RC"], writes=["ptab" + which])
                else:
                    S.op("act", lambda e, dst_sb=dst_sb: e.activation(out=dst_sb, in_=t1, func=AF.Sin),
                         reads=["t1"], writes=["ptab" + which])
            S.op("sp", lambda e: e.dma_start(out=dst[:, 0, :], in_=CT), reads=["ptabc"], dma="st")
            S.op("sp", lambda e: e.dma_start(out=dst[:, 1, :], in_=ST), reads=["ptabs"], dma="st")
            S.fence()
            AR.off = m

        def rope_tables(src):
            CT = AR.alloc(NT)
            ST = AR.alloc(NT)
            tabs["CT"], tabs["ST"] = CT, ST
            S.op("sp", lambda e: e.dma_start(out=CT, in_=src[:, 0, :]), writes=["tabc"], dma="ld")
            S.op("sp", lambda e: e.dma_start(out=ST, in_=src[:, 1, :]), writes=["tabs"], dma="ld")

        rope_precompute(pos_prev, tab_prev)
        rope_precompute(pos_own, tab_own)

        sq_pool = None
        misc = {}
        NTMP = [2]

        def alloc_common():
            misc["sq"] = Pool("sq", [AR.alloc(512, BF16) for _ in range(2)])
            misc["rstd"] = Pool("rstd", [AR.alloc(512) for _ in range(2)])
            misc["tmp"] = Pool("tmp", [AR.alloc(512) for _ in range(NTMP[0])])

        def gcol(slot, c):
            return G[:, slot * 8 + c: slot * 8 + c + 1]

        def rms_stats(src_fn, src_keys, n, onesm, eps, sq_eng="act"):
            ps, psk = PS.next()
            for c in range(DC):
                sq, sqk = misc["sq"].next()
                if sq_eng == "act":
                    S.op("act", lambda e, c=c, sq=sq: e.activation(out=sq[:, :n], in_=src_fn(c), func=AF.Square),
                         reads=[src_keys(c)], writes=[sqk])
                else:
                    S.op("dve", lambda e, c=c, sq=sq: e.tensor_tensor(out=sq[:, :n], in0=src_fn(c), in1=src_fn(c),
                                                                       op=ALU.mult),
                         reads=[src_keys(c)], writes=[sqk])
                S.op("pe", lambda e, c=c, sq=sq, ps=ps: e.matmul(ps[:, :n], onesm, sq[:, :n], start=(c == 0),
                                                                  stop=(c == DC - 1)),
                     reads=[sqk, "onesD", "ones4D"], writes=[psk])
            rstd, rk = misc["rstd"].next()
            S.op("act", lambda e, ps=ps, rstd=rstd: e.activation(out=rstd[:, :n], in_=ps[:, :n], func=AF.Ln,
                                                                 bias=epsb[:, 1:2] if eps > 2e-6 else epsb[:, 0:1]),
                 reads=[psk, "epsb"], writes=[rk])
            S.op("act", lambda e, rstd=rstd: e.activation(out=rstd[:, :n], in_=rstd[:, :n], func=AF.Exp, scale=-0.5),
                 reads=[rk], writes=[rk])
            return rstd, rk

        def prenorm(ti, slot, dst_fn, dst_key):
            c0, n = TILES[ti]
            rstd, rk = rms_stats(lambda c: H[:, c, c0:c0 + n], lambda c: Hk(c, ti), n, onesD, EPS)
            for c in range(DC):
                S.op("dve", lambda e, c=c: e.scalar_tensor_tensor(out=dst_fn(c), in0=H[:, c, c0:c0 + n],
                                                                  scalar=gcol(slot, c), in1=rstd[:, :n],
                                                                  op0=ALU.mult, op1=ALU.mult),
                     reads=[Hk(c, ti), rk, "G"], writes=[dst_key(c)])

        def postnorm_add(ti, slot, Y_fn, Y_key, half):
            c0, n = TILES[ti]
            rstd, rk = rms_stats(Y_fn, Y_key, n, ones4D if half else onesD, 4 * EPS if half else EPS, sq_eng="act")
            for c in range(DC):
                tmp, tk = misc["tmp"].next()
                S.op("dve", lambda e, c=c, tmp=tmp: e.scalar_tensor_tensor(out=tmp[:, :n], in0=Y_fn(c),
                                                                           scalar=gcol(slot, c), in1=rstd[:, :n],
                                                                           op0=ALU.mult, op1=ALU.mult),
                     reads=[Y_key(c), rk, "G"], writes=[tk])
                S.op("dve", lambda e, c=c, tmp=tmp: e.tensor_tensor(out=H[:, c, c0:c0 + n], in0=H[:, c, c0:c0 + n],
                                                                    in1=tmp[:, :n], op=ALU.add),
                     reads=[tk, Hk(c, ti)], writes=[Hk(c, ti)])

        def load_dense(dst, src, key, nsplit=1):
            a = dst.shape[1]
            step = (a + nsplit - 1) // nsplit
            for i in range(0, a, step):
                S.op("pool", lambda e, i=i: e.dma_start(out=dst[:, i:i + step], in_=src[:, i:i + step]),
                     writes=[key], dma="wdense")

        def interleave(main, side):
            nm, ns = len(main), len(side)
            si = 0
            for i, t in enumerate(main):
                t()
                want = ((i + 1) * ns) // nm
                while si < want:
                    side[si]()
                    si += 1
            while si < ns:
                side[si]()
                si += 1

        def ffn(j, slot_pre, slot_post, passes):
            S.fence()
            m = AR.off
            alloc_common()
            W = HALO + 1024
            xn = AR.alloc((DC, W), BF16)
            act = AR.alloc((FC, W), BF16)
            Y = AR.alloc((DC, W))
            sgp = Pool("sg", [AR.alloc(512) for _ in range(2)])
            wgu = [(AR.alloc((DC, 128), BF16), AR.alloc((DC, 128), BF16)) for _ in range(NWG)]
            wdn = [AR.alloc((FC, 128), BF16) for _ in range(2)]
            fcount = [0, 0]

            def pre_thunks(tl):
                p0 = TILES[tl[0]][0]
                out = []
                for ti in tl:
                    c0, n = TILES[ti]
                    lc = c0 - p0
                    out.append(lambda ti=ti, lc=lc, n=n: prenorm(
                        ti, slot_pre, lambda c: xn[:, c, lc:lc + n], lambda c: ("xn", c, ti)))
                return out

            def gateup_thunks(tl):
                p0 = TILES[tl[0]][0]

                def one(f):
                    s = fcount[0] % NWG
                    fcount[0] += 1
                    S.op("pool", lambda e: e.dma_start(out=wgu[s][0], in_=wg[j, f].rearrange(
                        "p (k n) -> p k n", k=DC)), writes=[("wg", s)], dma="wg%d" % s)
                    S.op("pool", lambda e: e.dma_start(out=wgu[s][1], in_=wu[j, f].rearrange(
                        "p (k n) -> p k n", k=DC)), writes=[("wu", s)], dma="wu%d" % s)
                    for ti in tl:
                        c0, n = TILES[ti]
                        lc = c0 - p0
                        pg, pgk = PS.next()
                        pu, puk = PS.next()

                        def mmg(e, lc=lc, n=n, pp=pg, which=0):
                            for k in range(DC):
                                ins = e.matmul(pp[:, :n], wgu[s][which][:, k, :], xn[:, k, lc:lc + n], start=(k == 0),
                                               stop=(k == DC - 1))
                            return ins

                        S.op("pe", mmg, reads=[("wg", s)] + [("xn", c, ti) for c in range(DC)], writes=[pgk])
                        S.op("pe", lambda e, lc=lc, n=n, pu=pu, mmg=mmg: mmg(e, lc, n, pu, 1),
                             reads=[("wu", s)] + [("xn", c, ti) for c in range(DC)], writes=[puk])
                        sg, sgk = sgp.next()
                        S.op("act", lambda e, sg=sg, pg=pg, n=n: e.activation(out=sg[:, :n], in_=pg[:, :n],
                                                                               func=AF.Silu),
                             reads=[pgk], writes=[sgk])
                        S.op("dve", lambda e, sg=sg, pu=pu, n=n, lc=lc: e.tensor_tensor(
                            out=act[:, f, lc:lc + n], in0=sg[:, :n], in1=pu[:, :n], op=ALU.mult),
                             reads=[sgk, puk], writes=[("act", f, ti)])

                return [lambda f=f: one(f) for f in range(FC)]

            def down_thunks(tl):
                p0 = TILES[tl[0]][0]

                def one(dc):
                    s = fcount[1] % 2
                    fcount[1] += 1
                    S.op("pool", lambda e: e.dma_start(out=wdn[s], in_=wd[j, dc].rearrange(
                        "p (k n) -> p k n", k=FC)), writes=[("wd", s)], dma="wd%d" % s)
                    for ti in tl:
                        c0, n = TILES[ti]
                        lc = c0 - p0
                        py, pyk = PS.next()

                        def mmd(e, lc=lc, n=n, py=py):
                            for k in range(FC):
                                ins = e.matmul(py[:, :n], wdn[s][:, k, :], act[:, k, lc:lc + n], start=(k == 0),
                                               stop=(k == FC - 1))
                            return ins

                        S.op("pe", mmd, reads=[("wd", s)] + [("act", f, ti) for f in range(FC)], writes=[pyk])
                        S.op("act", lambda e, py=py, lc=lc, n=n: e.activation(out=Y[:, dc, lc:lc + n],
                                                                              in_=py[:, :n], func=AF.Copy),
                             reads=[pyk], writes=[("Y", dc, ti)])

                return [lambda dc=dc: one(dc) for dc in range(DC)]

            def post_thunks(tl):
                p0 = TILES[tl[0]][0]
                out = []
                for ti in tl:
                    c0, n = TILES[ti]
                    lc = c0 - p0
                    out.append(lambda ti=ti, lc=lc, n=n: postnorm_add(
                        ti, slot_post, lambda c: Y[:, c, lc:lc + n], lambda c: ("Y", c, ti), True))
                return out

            def run(ths):
                for t in ths:
                    t()

            np_ = len(passes)
            run(pre_thunks(passes[0]))
            for p_i in range(np_):
                if p_i == 0:
                    run(gateup_thunks(passes[0]))
                if p_i + 1 < np_:
                    interleave(down_thunks(passes[p_i]), pre_thunks(passes[p_i + 1]))
                    interleave(gateup_thunks(passes[p_i + 1]), post_thunks(passes[p_i]))
                else:
                    run(down_thunks(passes[p_i]))
                    run(post_thunks(passes[p_i]))
            AR.off = m

        def rope_stage1(ps, psk, n):
            kb, kbk = misc["kb"].next()
            S.op("act", lambda e: e.activation(out=kb[:, :n], in_=ps[:, :n], func=AF.Copy), reads=[psk], writes=[kbk])
            return kb, kbk

        def rope_stage2(ps, psk, kb, kbk, c0t, n, dst_fn, dkey):
            CT_, ST_ = tabs["CT"], tabs["ST"]
            p2, p2k = PS.next()
            S.op("pe", lambda e: e.matmul(p2[:, :n], perm_sb, kb[:, :n], start=True, stop=True),
                 reads=[kbk, "perm"], writes=[p2k])
            t1, t1k = misc["tmp"].next()
            t2, t2k = misc["tmp"].next()
            S.op("dve", lambda e: e.tensor_tensor(out=t1[:, :n], in0=ps[:, :n], in1=CT_[:, c0t:c0t + n], op=ALU.mult),
                 reads=[psk, "tabc"], writes=[t1k])
            S.op("dve", lambda e: e.tensor_tensor(out=t2[:, :n], in0=p2[:, :n], in1=ST_[:, c0t:c0t + n], op=ALU.mult),
                 reads=[p2k, "tabs"], writes=[t2k])
            S.op("dve", lambda e: dst_fn(e, t1[:, :n], t2[:, :n]), reads=[t1k, t2k], writes=[dkey])

        def class_views(dst3, g, tt, full):
            d = GROUPS[g][1]
            base = dst3[:, 512 * tt:512 * tt + 512] if full else dst3
            if d == 1:
                return base, None
            if d == 4:
                return base.rearrange("p (r i) -> p r i", r=4), 4
            if full:
                return dst3.rearrange("p (r i) -> p r i", r=16)[:, :, 32 * tt:32 * tt + 32], 16
            return dst3.rearrange("p (r i) -> p r i", r=16), 16

        def proj_rope(xn_tile, xn_keys, wres, wkey, tt, dst_all, dname, full):
            pend = None
            for c in range(12):
                g = c // 4
                ps, psk = PS.next()

                def mm(e, c=c, ps=ps):
                    for k in range(DC):
                        ins = e.matmul(ps[:, :512], wres[:, k, c * 128:(c + 1) * 128], xn_tile[:, k, :],
                                       start=(k == 0), stop=(k == DC - 1))
                    return ins

                S.op("pe", mm, reads=[wkey] + xn_keys, writes=[psk])
                kb, kbk = rope_stage1(ps, psk, 512)
                ov, r = class_views(dst_all[:, c, :], g, tt, full)

                def fin(e, a, b, ov=ov, r=r):
                    if r is not None:
                        a = a.rearrange("p (i r) -> p r i", r=r)
                        b = b.rearrange("p (i r) -> p r i", r=r)
                    return e.tensor_tensor(out=ov, in0=a, in1=b, op=ALU.add)

                if pend is not None:
                    rope_stage2(*pend)
                pend = (ps, psk, kb, kbk, 512 * tt, 512, fin, (dname, c, tt if full else 0))
            rope_stage2(*pend)

        def layer0_pass(xsrc, possrc, corrsrc, prev):
            S.fence()
            for c in range(DC):
                S.op("sp", lambda e, c=c: e.dma_start(out=H[:, c, :], in_=xsrc[:, c, :]),
                     writes=[Hk(c, t) for t in range(5)], dma="ld")
            if A_:
                if upto >= 1:
                    ffn(0, 0, 1, [[0, 1, 2], [3, 4]])

            if A_ and upto >= 2:
                S.fence()
                m = AR.off
                alloc_common()
                win_sb = AR.alloc((DC, D), BF16)
                wgr_sb = AR.alloc((4, 2, 256), BF16)
                wout_sb = AR.alloc((DC, D), BF16)
                corr_sb = AR.alloc((4, 128))
                load_dense(win_sb, w_in, "w_in", 2)
                S.op("pool", lambda e: e.dma_start(out=wgr_sb, in_=w_grp), writes=["w_grp"], dma="wdense")
                load_dense(wout_sb, w_out, "w_out", 2)
                S.op("sp", lambda e: e.dma_start(out=corr_sb, in_=corrsrc), writes=["corr"], dma="ld")
                hm = AR.alloc((DC, 512), BF16)
                U = [AR.alloc((DC, 528)) for _ in range(2)]
                Ta = AR.alloc(528)
                Tb = AR.alloc(528)
                pT = AR.alloc((DC, 512), BF16)
                yT = AR.alloc((DC, 512), BF16)
                Ym = AR.alloc((DC, 512))
                for ti in range(5):
                    c0, n = TILES[ti]
                    cur = U[ti % 2]
                    nxt = U[(ti + 1) % 2]
                    ck = "U%d" % (ti % 2)
                    nk = "U%d" % ((ti + 1) % 2)
                    prenorm(ti, 2, lambda c, n=n: hm[:, c, :n], lambda c: ("hm", c))
                    for c in range(DC):
                        ps, psk = PS.next()

                        def mm(e, c=c, ps=ps, n=n):
                            for k in range(DC):
                                ins = e.matmul(ps[:, :n], win_sb[:, k, c * 128:(c + 1) * 128], hm[:, k, :n],
                                               start=(k == 0), stop=(k == DC - 1))
                            return ins

                        S.op("pe", mm, reads=["w_in"] + [("hm", k) for k in range(DC)], writes=[psk])
                        if ti == 0:
                            S.op("act", lambda e, c=c, ps=ps, nxt=nxt: e.activation(out=nxt[:, c, 0:16], in_=ps[:, HALO - 16:HALO],
                                                                           func=AF.Copy),
                                 reads=[psk], writes=[(nk, c)])
                            continue
                        S.op("act", lambda e, c=c, ps=ps, cur=cur: e.activation(out=cur[:, c, 16:528], in_=ps[:, :512],
                                                                       func=AF.Copy),
                             reads=[psk], writes=[(ck, c)])
                        if ti < 4:
                            S.op("act", lambda e, c=c, cur=cur, nxt=nxt: e.activation(out=nxt[:, c, 0:16], in_=cur[:, c, 512:528],
                                                                    func=AF.Copy),
                                 reads=[(ck, c)], writes=[(nk, c)])
                        gi = c // 2
                        w = 2 << gi
                        Uc = cur[:, c, :]
                        src = Uc
                        srck = (ck, c)
                        sh = 1
                        bufs = [(Ta, "Ta"), (Tb, "Tb")]
                        bi = 0
                        while sh < w:
                            dstb, dk = bufs[bi]
                            S.op("dve", lambda e, src=src, dstb=dstb, sh=sh, c=c: e.tensor_tensor(
                                out=dstb[:, sh:528], in0=src[:, sh:528], in1=src[:, 0:528 - sh], op=ALU.add),
                                 reads=[srck], writes=[dk])
                            src, srck = dstb, dk
                            bi ^= 1
                            sh *= 2
                        if ti == 1:
                            S.op("dve", lambda e, src=src, gi=gi: e.tensor_tensor(out=src[:, 16:144], in0=src[:, 16:144],
                                                                                  in1=corr_sb[:, gi, :], op=ALU.mult),
                                 reads=[srck, "corr"], writes=[srck])
                        S.op("dve", lambda e, src=src, w=w, Uc=Uc, c=c: e.scalar_tensor_tensor(
                            out=pT[:, c, :], in0=src[:, 16:528], scalar=1.0 / w, in1=Uc[:, 16:528], op0=ALU.mult,
                            op1=ALU.subtract), reads=[srck, (ck, c)], writes=[("pT", c)])
                    if ti == 0:
                        continue
                    for dc in range(DC):
                        gi = dc // 2
                        ps, psk = PS.next()

                        def mm(e, dc=dc, gi=gi, ps=ps):
                            for cc in range(2):
                                ins = e.matmul(ps[:, :512], wgr_sb[:, gi, cc, (dc % 2) * 128:(dc % 2) * 128 + 128],
                                               pT[:, 2 * gi + cc, :], start=(cc == 0), stop=(cc == 1))
                            return ins

                        S.op("pe", mm, reads=["w_grp", ("pT", 2 * gi), ("pT", 2 * gi + 1)], writes=[psk])
                        S.op("act", lambda e, dc=dc, ps=ps: e.activation(out=yT[:, dc, :], in_=ps[:, :512], func=AF.Copy,
                                                                         scale=G[:, 56 + dc:57 + dc]),
                             reads=[psk, "G"], writes=[("yT", dc)])
                    for dc in range(DC):
                        ps, psk = PS.next()

                        def mm(e, dc=dc, ps=ps):
                            for k in range(DC):
                                ins = e.matmul(ps[:, :512], wout_sb[:, k, dc * 128:(dc + 1) * 128], yT[:, k, :],
                                               start=(k == 0), stop=(k == DC - 1))
                            return ins

                        S.op("pe", mm, reads=["w_out"] + [("yT", k) for k in range(DC)], writes=[psk])
                        S.op("act", lambda e, dc=dc, ps=ps: e.activation(out=Ym[:, dc, :], in_=ps[:, :512], func=AF.Copy),
                             reads=[psk], writes=[("Ym", dc)])
                    postnorm_add(ti, 3, lambda c: Ym[:, c, :], lambda c: ("Ym", c), False)
                AR.off = m

            if A_ and upto >= 3:
                ffn(1, 4, 5, [[1, 2], [3, 4]])

            if upto < 5 and not prev:
                S.fence()
                for c in range(DC):
                    S.op("sp", lambda e, c=c: e.dma_start(out=hT_out[:, c, :], in_=H[:, c, HALO:]),
                         reads=[Hk(c, t) for t in range(5)], dma="st")
            if A_ and upto >= 4:
                S.fence()
                m = AR.off
                alloc_common()
                misc["kb"] = Pool("kb", [AR.alloc(512, BF16) for _ in range(2)])
                rope_tables(tab_prev if prev else tab_own)
                wk_sb = AR.alloc((DC, QKV), BF16)
                wv_sb = AR.alloc((DC, QKV), BF16)
                load_dense(wk_sb, w_k, "w_k", 2)
                load_dense(wv_sb, w_v, "w_v", 2)
                Kall = AR.alloc((12, 512), BF16)
                hkb = [AR.alloc((DC, 512), BF16) for _ in range(2)]
                hk16 = AR.alloc((DC, 512), BF16)
                vsb = Pool("vsb", [AR.alloc(512, BF16) for _ in range(3)])
                prenorm(1, 6, lambda c: hkb[0][:, c, :], lambda c: ("hk0", c))
                for tt in range(4 if upto >= 4.1 else 0):
                    ti = tt + 1
                    hk = hkb[tt % 2]
                    hkn = "hk%d" % (tt % 2)
                    if tt < 3:
                        prenorm(ti + 1, 6, lambda c, tt=tt: hkb[(tt + 1) % 2][:, c, :],
                                lambda c, tt=tt: ("hk%d" % ((tt + 1) % 2), c))
                    hkk = [(hkn, c) for c in range(DC)]
                    for c in range(DC if upto >= 4.3 else 0):
                        S.op("act", lambda e, c=c, hk=hk: e.activation(
                            out=hk16[:, c, :].rearrange("p (r i) -> p r i", r=16),
                            in_=hk[:, c, :].rearrange("p (i r) -> p r i", r=16), func=AF.Copy),
                             reads=[(hkn, c)], writes=[("hk16", c)])
                    proj_rope(hk, hkk, wk_sb, "w_k", tt, Kall, "Kall", False)
                    for g in range(3):
                        R = GROUPS[g][1]
                        kd = kin[g].rearrange("j p x -> p j x")
                        ksrc = Kall[:, 4 * g:4 * g + 4, :]
                        rk = [("Kall", c, 0) for c in range(4 * g, 4 * g + 4)]
                        if g < 2:
                            if prev and tt < 3:
                                continue
                            if prev and g == 0:
                                S.op("sp", lambda e, kd=kd, ksrc=ksrc: e.dma_start(out=kd[:, :, 0:128], in_=ksrc[:, :, 384:512]),
                                     reads=rk, dma="st")
                            elif prev:
                                S.op("sp", lambda e, kd=kd, ksrc=ksrc: e.dma_start(out=kd[:, :, 0:512], in_=ksrc), reads=rk, dma="st")
                            else:
                                o = R * 128 + 512 * tt
                                S.op("sp", lambda e, kd=kd, ksrc=ksrc, o=o: e.dma_start(out=kd[:, :, o:o + 512], in_=ksrc),
                                     reads=rk, dma="st")
                        else:
                            o = 0 if prev else 2048
                            for jj in range(4):
                                S.op("sp", lambda e, jj=jj, o=o, tt=tt: e.dma_start(
                                    out=kin[2][jj][:, o:o + 2048].rearrange("p (r i) -> p r i", r=16)[:, :, 32 * tt:32 * tt + 32],
                                    in_=Kall[:, 8 + jj, :].rearrange("p (r i) -> p r i", r=16)),
                                     reads=[("Kall", 8 + jj, 0)], dma="st")
                    for g in range(3 if upto >= 4.3 else 0):
                        d = GROUPS[g][1]
                        for b in range(4):
                            R = d
                            vd = vin[g].rearrange("j p b f -> p j b f")
                            if d == 1:
                                cols = lambda k, b=b, hk=hk: hk[:, k, 128 * b:128 * b + 128]
                                cls = [4 * tt + b]
                            elif d == 4:
                                cols = lambda k, b=b, hk=hk: hk[:, k, :].rearrange("p (i r) -> p r i", r=4)[:, b, :]
                                cls = [4 * tt + b]
                            else:
                                cols = lambda k, b=b: hk16[:, k, 128 * b:128 * b + 128]
                                cls = [4 * b + rr for rr in range(4)]
                            blks = [(k_ - (16 - R)) if prev else (R + k_) for k_ in cls]
                            if blks[0] < 0:
                                continue
                            ps, psk = PS.next()

                            def mm(e, cols=cols, ps=ps, g=g):
                                for k in range(DC):
                                    ins = e.matmul(ps[:, :512], cols(k), wv_sb[:, k, g * 512:(g + 1) * 512],
                                                   start=(k == 0), stop=(k == DC - 1))
                                return ins

                            S.op("pe", mm, reads=["w_v"] + hkk + [("hk16", c) for c in range(DC)], writes=[psk])
                            vb, vbk = vsb.next()
                            S.op("act", lambda e, vb=vb, ps=ps: e.activation(out=vb, in_=ps[:, :512], func=AF.Copy),
                                 reads=[psk], writes=[vbk])
                            if d == 16:
                                for rr in range(4):
                                    S.op("sp", lambda e, vb=vb, vd=vd, rr=rr, tt=tt, blk=blks[rr]: e.dma_start(
                                        out=vd[32 * tt:32 * tt + 32, :, blk, :],
                                        in_=vb[32 * rr:32 * rr + 32, :].rearrange("p (j f) -> p j f", j=4)), reads=[vbk], dma="st")
                            else:
                                S.op("sp", lambda e, vb=vb, vd=vd, blk=blks[0]: e.dma_start(
                                    out=vd[:, :, blk, :], in_=vb.rearrange("p (j f) -> p j f", j=4)), reads=[vbk], dma="st")
                AR.off = m

        layer0_pass(xT_prev, pos_prev, corr_prev, True)
        layer0_pass(xT_own, pos_own, corr_own, False)

        if upto >= 5:
            ffn(2, 8, 9, [[1, 2], [3, 4]])

        if upto >= 5.15:
            S.fence()
            m = AR.off
            alloc_common()
            misc["kb"] = Pool("kb", [AR.alloc(512, BF16) for _ in range(2)])
            Qall = AR.alloc((12, NT), BF16)
            m2 = AR.off
            rope_tables(tab_own)
            wq_sb = AR.alloc((DC, QKV), BF16)
            load_dense(wq_sb, w_q, "w_q", 2)
            hmqb = [AR.alloc((DC, 512), BF16) for _ in range(2)]
            prenorm(1, 10, lambda c: hmqb[0][:, c, :], lambda c: ("hmq0", c))
            for tt in range(4):
                ti = tt + 1
                if tt < 3:
                    prenorm(ti + 1, 10, lambda c, tt=tt: hmqb[(tt + 1) % 2][:, c, :],
                            lambda c, tt=tt: ("hmq%d" % ((tt + 1) % 2), c))
                proj_rope(hmqb[tt % 2], [("hmq%d" % (tt % 2), c) for c in range(DC)], wq_sb, "w_q", tt, Qall, "Qall",
                          True)
            S.fence()
            AR.off = m2
            mk = AR.alloc((2, 512), BF16)
            S.op("sp", lambda e: e.dma_start(out=mk, in_=masks), writes=["mk"], dma="ld")
            S.op("sp", lambda e: e.dma_start(out=ident, in_=masks_id), writes=["ident"], dma="ld")
            wo_sb = AR.alloc((4, D), BF16)
            load_dense(wo_sb, w_o, "w_o", 1)
            OT = AR.alloc((4, NT), BF16)
            m3 = AR.off
            ND = AR.alloc((2, NT))
            kbuf = [AR.alloc(32 * 128, BF16) for _ in range(2)]
            vbuf = [AR.alloc((32, 128), BF16) for _ in range(2)]
            ptp = Pool("pt", [AR.alloc(512, BF16) for _ in range(3)])
            Qk = [("Qall", c, tt) for c in range(12) for tt in range(4)]
            slot_ctr = [0]

            def begin_group(jj, g):
                R = GROUPS[g][1]
                nb = R + 16
                s_ = slot_ctr[0] % 2
                slot_ctr[0] += 1
                S.op("sp", lambda e: e.dma_start(out=kbuf[s_][:, :nb * 128], in_=kin[g][jj]),
                     writes=[("kbuf", s_)], dma="kb%d" % s_)
                S.op("sp", lambda e: e.dma_start(out=vbuf[s_][:, :nb, :], in_=vin[g][jj]),
                     writes=[("vbuf", s_)], dma="vb%d" % s_)
                return dict(jj=jj, g=g, R=R, s=s_, Qc=Qall[:, 4 * g + jj, :])

            def S_stage(ctx, k):
                jj, g, R, s_, Qc = ctx["jj"], ctx["g"], ctx["R"], ctx["s"], ctx["Qc"]
                first = k < R
                pt, ptk = ptp.next()
                for hh in range(2):
                    pss, pssk = PS.next()

                    def mms(e, pss=pss, hh=hh):
                        e.matmul(pss[:, :256], ident, mk[:, 1 if first else 0, 0:256], start=True, stop=False)
                        q = Qc[:, 128 * k:128 * k + 128]
                        lo = 64 * hh
                        for w_, kb_ in enumerate((k, k + R)):
                            ins = e.matmul(pss[:, w_ * 128:(w_ + 1) * 128],
                                           kbuf[s_][lo:lo + 64, kb_ * 128:(kb_ + 1) * 128], q[lo:lo + 64, :],
                                           start=False, stop=(w_ == 1))
                        return ins

                    S.op("pe", mms, reads=[("kbuf", s_), "mk", "ident"] + [("Qall", 4 * g + jj, tt) for tt in
                                                                            range(4)], writes=[pssk])
                    S.op("act", lambda e, pss=pss, hh=hh: e.activation(
                        out=pt[:, 256 * hh:256 * hh + 256], in_=pss[:, :256], func=AF.Exp, scale=0.125),
                         reads=[pssk], writes=[(ptk, hh)])
                return pt, ptk

            def PV_stage(ctx, k, pt, ptk):
                g, R, s_ = ctx["g"], ctx["R"], ctx["s"]
                pso, psok = PS.next()

                def mmo(e):
                    for hh in range(2):
                        lo = 64 * hh
                        for w_, kb_ in enumerate((k, k + R)):
                            e.matmul(pso[lo:lo + 64, 0:128], vbuf[s_][:, kb_, lo:lo + 64],
                                     pt[:, (2 * hh + w_) * 128:(2 * hh + w_ + 1) * 128], start=(w_ == 0),
                                     stop=(w_ == 1))
                        for w_ in range(2):
                            ins = e.matmul(pso[lo:lo + 64, 128:256], ones1[:, 0:64],
                                           pt[:, (2 * hh + w_) * 128:(2 * hh + w_ + 1) * 128], start=(w_ == 0),
                                           stop=(w_ == 1))
                    return ins

                S.op("pe", mmo, reads=[("vbuf", s_), (ptk, 0), (ptk, 1), "ones1"], writes=[psok])
                if R == 1:
                    ndv = ND[:, :, 128 * k:128 * k + 128]
                elif R == 4:
                    n_, r_ = k // 4, k % 4
                    ndv = ND[:, :, 512 * n_:512 * n_ + 512].rearrange("p x (i r) -> p x r i", r=4)[:, :, r_, :]
                else:
                    ndv = ND.rearrange("p x (i r) -> p x r i", r=16)[:, :, k, :]
                psv = pso[:, 0:256].rearrange("p (x i) -> p x i", x=2)
                if g == 0:
                    S.op("dve", lambda e: e.tensor_copy(out=ndv, in_=psv), reads=[psok], writes=["NDall"])
                else:
                    S.op("dve", lambda e: e.tensor_tensor(out=ndv, in0=psv, in1=ndv, op=ALU.add),
                         reads=[psok, "NDall"], writes=["NDall"])

            def end_jj(jj):
                for tt in range(4):
                    S.op("dve", lambda e, tt=tt: e.reciprocal(out=ND[:, 1, 512 * tt:512 * tt + 512],
                                                              in_=ND[:, 1, 512 * tt:512 * tt + 512]),
                         reads=["NDall"], writes=["NDall"])
                    S.op("dve", lambda e, tt=tt: e.tensor_tensor(out=OT[:, jj, 512 * tt:512 * tt + 512],
                                                                 in0=ND[:, 0, 512 * tt:512 * tt + 512],
                                                                 in1=ND[:, 1, 512 * tt:512 * tt + 512],
                                                                 op=ALU.mult),
                         reads=["NDall"], writes=[("OT", jj, tt)])

            steps = [(jj, g, k) for jj in range(4) for g in range(3) for k in range(16)]
            ctxs = {}

            def get_ctx(jj, g):
                if (jj, g) not in ctxs:
                    ctxs[(jj, g)] = begin_group(jj, g)
                return ctxs[(jj, g)]

            cur = S_stage(get_ctx(0, 0), 0)
            for i, (jj, g, k) in enumerate(steps):
                nxt = None
                if i + 1 < len(steps):
                    j2, g2, k2 = steps[i + 1]
                    nxt = S_stage(get_ctx(j2, g2), k2)
                PV_stage(get_ctx(jj, g), k, *cur)
                if g == 2 and k == 15:
                    end_jj(jj)
                cur = nxt
            S.fence()
            AR.off = m3
            Ym = AR.alloc((DC, 512))
            for tt in range(4):
                ti = tt + 1
                for dc in range(DC):
                    ps, psk = PS.next()

                    def mm(e, dc=dc, ps=ps, tt=tt):
                        for k in range(4):
                            ins = e.matmul(ps[:, :512], wo_sb[:, k, dc * 128:(dc + 1) * 128],
                                           OT[:, k, 512 * tt:512 * tt + 512], start=(k == 0), stop=(k == 3))
                        return ins

                    S.op("pe", mm, reads=["w_o"] + [("OT", k, tt) for k in range(4)], writes=[psk])
                    S.op("act", lambda e, dc=dc, ps=ps: e.activation(out=Ym[:, dc, :], in_=ps[:, :512], func=AF.Copy),
                         reads=[psk], writes=[("Ym", dc)])
                postnorm_add(ti, 11, lambda c: Ym[:, c, :], lambda c: ("Ym", c), False)
            AR.off = m

        if upto >= 5.25:
            ffn(3, 12, 13, [[1, 2], [3, 4]])
        if upto >= 5:
            S.fence()
            for c in range(DC):
                S.op("sp", lambda e, c=c: e.dma_start(out=hT_out[:, c, :], in_=H[:, c, HALO:]),
                     reads=[Hk(c, t) for t in range(5)], dma="st")

        S.fence()
        final_waits = dict(S.fence_tok)

        sems = {}
        for e in ENGS:
            sems[("e", e)] = es.enter_context(nc.semaphore("sem_" + e))
        for k in S.dcnt:
            sems[("d", k)] = es.enter_context(nc.semaphore("dsem_" + k))
        engmap = {"pe": "tensor", "act": "scalar", "dve": "vector", "pool": "gpsimd", "sp": "sync"}
        with nc.Block() as block:
            def make(eng):
                def body(e):
                    for waits, fn, tok in S.q[eng]:
                        for sk, v in waits:
                            e.wait_ge(sems[sk], v)
                        ins = fn(e)
                        ins.then_inc(sems[tok[0]], 16 if tok[0][0] == "d" else 1)
                    if eng == "sp":
                        for sk, v in final_waits.items():
                            if v > 0:
                                e.wait_ge(sems[sk], v)
                return body

            for eng in ENGS:
                getattr(block, engmap[eng])(make(eng))
    return nc


def _fm(a):
    T, F = a.shape
    return np.ascontiguousarray(a.T.reshape(F // 128, 128, T).transpose(1, 0, 2))


def _dense(w):
    k, n = w.shape
    return np.ascontiguousarray(w.reshape(k // 128, 128, n).transpose(1, 0, 2))


def _ffn_w(gate, up, down):
    gate = gate.reshape(4, DC, 128, FC, 128)
    up = up.reshape(4, DC, 128, FC, 128)
    down = down.reshape(4, FC, 128, DC, 128)
    wg = np.ascontiguousarray(gate.transpose(0, 3, 2, 1, 4)).reshape(4, FC, 128, DC * 128)
    wu = np.ascontiguousarray(up.transpose(0, 3, 2, 1, 4)).reshape(4, FC, 128, DC * 128)
    wd = np.ascontiguousarray(down.transpose(0, 3, 2, 1, 4)).reshape(4, DC, 128, FC * 128)
    return wg, wu, wd


def _vecs(norm_gain, kv_gain, pool_scale):
    v = np.zeros((128, 14 * 8), np.float32)
    for i in range(6):
        v[:, i * 8:(i + 1) * 8] = norm_gain[0, i].reshape(8, 128).T
        v[:, (8 + i) * 8:(9 + i) * 8] = norm_gain[1, i].reshape(8, 128).T
    v[:, 48:56] = kv_gain.reshape(8, 128).T
    v[:, 56:64] = pool_scale.reshape(8, 128).T
    return v


_NC_CACHE = {}


def _get_nc():
    if "F" not in _NC_CACHE:
        _NC_CACHE["F"] = build()
    return _NC_CACHE["F"]


def _corr(is_seq_start):
    corr = np.ones((128, 4, 128), np.float32)
    if is_seq_start:
        t = np.arange(128)
        for g, w in enumerate((2, 4, 8, 16)):
            corr[:, g, :] = (w / np.minimum(t + 1, w)).astype(np.float32)[None, :]
    return corr


def make_in_maps(x, positions, norm_gain, ffn_w_gate, ffn_w_up, ffn_w_down, pool_w_in, pool_w_group, pool_scale,
                 pool_w_out, kv_norm_gain, w_k, w_v, attn_w_q, attn_w_o):
    x = np.asarray(x, np.float32)
    positions = np.asarray(positions, np.int32)
    bf = ml_dtypes.bfloat16
    p = np.arange(128)
    inv_freq = (np.float32(10000.0) ** (-(np.arange(0, 64, 2, dtype=np.float32)) / np.float32(64))).astype(np.float32)
    rconst = np.stack([inv_freq[p % 32], np.where((p % 64) < 32, -1.0, 1.0)], axis=1).astype(np.float32)
    partner = np.where((p % 64) < 32, p + 32, p - 32)
    permM = np.zeros((128, 128), np.float32)
    permM[partner, p] = 1.0
    permM = permM.astype(bf)
    wg, wu, wd = _ffn_w(np.asarray(ffn_w_gate, np.float32), np.asarray(ffn_w_up, np.float32),
                        np.asarray(ffn_w_down, np.float32))
    vecs = _vecs(np.asarray(norm_gain, np.float32), np.asarray(kv_norm_gain, np.float32),
                 np.asarray(pool_scale[0], np.float32))
    common = dict(
        wg=wg, wu=wu, wd=wd, vecs=vecs, rconst=rconst, permM=permM,
        w_in=_dense(np.asarray(pool_w_in[0], np.float32)), w_out=_dense(np.asarray(pool_w_out[0], np.float32)),
        w_grp=np.ascontiguousarray(np.asarray(pool_w_group[0], np.float32).reshape(4, 2, 128, 256).transpose(2, 0, 1, 3)),
        w_k=_dense(np.asarray(w_k, np.float32)), w_v=_dense(np.asarray(w_v, np.float32)),
        w_q=_dense(np.asarray(attn_w_q[0], np.float32)), w_o=_dense(np.asarray(attn_w_o[0], np.float32)),
        masks_id=np.eye(128, dtype=np.float32).astype(bf))
    kk = np.arange(128)[:, None]
    qq = np.arange(128)[None, :]
    mprev = np.where(kk >= qq, 0.0, NEG).astype(np.float32)
    mcur = np.where(kk <= qq, 0.0, NEG).astype(np.float32)
    mall = np.full((128, 128), NEG, np.float32)
    MN = np.concatenate([mprev, mcur, mprev, mcur], axis=1)
    MF0 = np.concatenate([mall, mcur, mall, mcur], axis=1)
    in_maps = []
    for c in range(8):
        b, q = divmod(c, 4)
        s0 = q * NT

        def xslice(start):
            xs = np.zeros((HC, D), np.float32)
            lo = start - HALO
            if start >= 0:
                a = max(lo, 0)
                xs[a - lo:] = x[b, a:start + NT]
            return _fm(xs)

        pp = s0 - NT
        pos_prev = positions[b:b + 1, pp:pp + NT] if q > 0 else positions[b:b + 1, 0:NT]
        d = dict(common)
        d.update(xT_prev=xslice(s0 - NT if q > 0 else -10 ** 9), xT_own=xslice(s0),
                 pos_prev=np.ascontiguousarray(pos_prev), pos_own=np.ascontiguousarray(positions[b:b + 1, s0:s0 + NT]),
                 corr_prev=_corr(q == 1), corr_own=_corr(q == 0),
                 masks=np.stack([MN, MF0 if q == 0 else MN], axis=1).astype(bf))
        in_maps.append(d)
    return in_maps


def kernel(x, positions, norm_gain, ffn_w_gate, ffn_w_up, ffn_w_down, pool_w_in, pool_w_group, pool_scale,
           pool_w_out, kv_norm_gain, w_k, w_v, attn_w_q, attn_w_o):
    in_maps = make_in_maps(x, positions, norm_gain, ffn_w_gate, ffn_w_up, ffn_w_down, pool_w_in, pool_w_group,
                           pool_scale, pool_w_out, kv_norm_gain, w_k, w_v, attn_w_q, attn_w_o)
    res = run_bass_kernel_spmd(_get_nc(), in_maps, core_ids=list(range(8))).results
    out = np.zeros((2, 8192, D), np.float32)
    for c in range(8):
        b, q = divmod(c, 4)
        hT = np.asarray(res[c]["hT"])
        out[b, q * NT:(q + 1) * NT] = hT.transpose(2, 1, 0).reshape(NT, D)
    return out
```

```python
import contextlib
import os
KSTEP = int(os.environ.get('KSTEP', '9'))
import numpy as np
import ml_dtypes
import concourse.bass as bass
import concourse.mybir as mybir
from concourse.bass_utils import run_bass_kernel_spmd

F32 = mybir.dt.float32
BF16 = mybir.dt.bfloat16
I32 = mybir.dt.int32
AF = mybir.ActivationFunctionType
ALU = mybir.AluOpType

D = 1024
DFF = 2816
NT = 2048
HALO = 128
HC = HALO + NT
DC = 8
FC = 22
QKV = 1536
EPS = 1e-6
NEG = -30000.0
NWG = 2
GROUPS = ((128, 1), (512, 4), (2048, 16))
ENGS = ("pe", "act", "dve", "pool", "sp")
SAME_ENGINE_FREE = ("pe", "sp")
MAGIC = 12582912.0
TWO_PI = 6.283185307179586
C1 = 6.28125
C2 = TWO_PI - C1

TILES = [(0, HALO)] + [(HALO + 512 * i, 512) for i in range(4)]


class Sched:
    def __init__(self):
        self.q = {e: [] for e in ENGS}
        self.cnt = {e: 0 for e in ENGS}
        self.dcnt = {}
        self.seen = {e: {} for e in ENGS}
        self.last_w = {}
        self.readers = {}
        self.fence_tok = None
        self.fence_done = set(ENGS)

    def fence(self):
        tok = {("e", e): self.cnt[e] for e in ENGS}
        for k, v in self.dcnt.items():
            tok[("d", k)] = v
        self.fence_tok = tok
        self.fence_done = set()

    def op(self, eng, fn, reads=(), writes=(), dma=None):
        waits = {}
        writes = list(writes) + [k for k in reads if isinstance(k, tuple) and k[0] == "ps" and k not in writes]

        def need(dep):
            if dep is None:
                return
            sk, val = dep
            if sk == ("e", eng) and eng in SAME_ENGINE_FREE:
                return
            if sk[0] == "d":
                val = self.dcnt[sk[1]]
            if val <= 0 or self.seen[eng].get(sk, 0) >= val:
                return
            if waits.get(sk, 0) < val:
                waits[sk] = val

        if eng not in self.fence_done:
            for sk, v in self.fence_tok.items():
                need((sk, v))
            self.fence_done.add(eng)
        for k in reads:
            need(self.last_w.get(k))
        for k in writes:
            need(self.last_w.get(k))
            for r in self.readers.get(k, ()):
                need(r)
        for sk, v in waits.items():
            self.seen[eng][sk] = v
        if dma is None:
            self.cnt[eng] += 1
            tok = (("e", eng), self.cnt[eng])
        else:
            self.dcnt[dma] = self.dcnt.get(dma, 0) + 16
            tok = (("d", dma), self.dcnt[dma])
        for k in writes:
            self.last_w[k] = tok
            self.readers[k] = []
        for k in reads:
            self.readers.setdefault(k, []).append(tok)
        self.q[eng].append((sorted(waits.items(), key=str), fn, tok))


class Arena:
    def __init__(self, ap, nwords):
        self.ap = ap
        self.n = nwords
        self.off = 0

    def alloc(self, free, dtype=F32):
        if isinstance(free, int):
            free = (free,)
        n = int(np.prod(free))
        words = n if dtype != BF16 else (n + 1) // 2
        words = (words + 7) // 8 * 8
        assert self.off + words <= self.n, ("arena overflow", self.off, words, self.n)
        v = self.ap[:, self.off:self.off + words]
        self.off += words
        if dtype == BF16:
            v = v.bitcast(BF16)
        elif dtype == I32:
            v = v.bitcast(I32)
        v = v[:, 0:n]
        if len(free) == 2:
            v = v.rearrange("p (a b) -> p a b", a=free[0])
        elif len(free) == 3:
            v = v.rearrange("p (a b c) -> p a b c", a=free[0], b=free[1])
        return v


class Pool:
    def __init__(self, name, bufs):
        self.name = name
        self.bufs = bufs
        self.i = -1

    def next(self):
        self.i = (self.i + 1) % len(self.bufs)
        return self.bufs[self.i], (self.name, self.i)


def build(upto=99, dbg=False):
    nc = bass.Bass("TRN2", target_bir_lowering=False)
    S = Sched()
    A_ = True

    def din(name, shape, dt=F32):
        return nc.dram_tensor(name, list(shape), dt, kind="ExternalInput").ap()

    def dout(name, shape, dt=F32):
        return nc.dram_tensor(name, list(shape), dt, kind="ExternalOutput").ap()

    xT_prev = din("xT_prev", [128, DC, HC])
    xT_own = din("xT_own", [128, DC, HC])
    wg = din("wg", [4, FC, 128, DC * 128])
    wu = din("wu", [4, FC, 128, DC * 128])
    wd = din("wd", [4, DC, 128, FC * 128])
    NV = 14 * 8
    vecs = din("vecs", [128, NV])
    pos_prev = din("pos_prev", [1, NT], I32)
    pos_own = din("pos_own", [1, NT], I32)
    rconst = din("rconst", [128, 2])
    permM = din("permM", [128, 128], BF16)
    w_in = din("w_in", [128, DC, D])
    w_grp = din("w_grp", [128, 4, 2, 256])
    w_out = din("w_out", [128, DC, D])
    w_k = din("w_k", [128, DC, QKV])
    w_v = din("w_v", [128, DC, QKV])
    corr_prev = din("corr_prev", [128, 4, 128])
    corr_own = din("corr_own", [128, 4, 128])
    w_q = din("w_q", [128, DC, QKV])
    w_o = din("w_o", [128, 4, D])
    masks = din("masks", [128, 2, 512], BF16)
    masks_id = din("masks_id", [128, 128], BF16)
    hT_out = dout("hT", [128, DC, NT])
    skind = "ExternalOutput" if dbg else "Internal"
    kin = [nc.dram_tensor("kin%d" % g, [4, 128, (GROUPS[g][1] + 16) * 128], BF16, kind=skind).ap() for g in range(3)]
    vin = [nc.dram_tensor("vin%d" % g, [4, 128, (GROUPS[g][1] + 16), 128], BF16, kind=skind).ap() for g in range(3)]
    tab_prev = nc.dram_tensor("tab_prev", [128, 2, NT], F32).ap()
    tab_own = nc.dram_tensor("tab_own", [128, 2, NT], F32).ap()

    es = contextlib.ExitStack()
    with es:
        AW = 53000
        arena_t = es.enter_context(nc.sbuf_tensor("arena", [128, AW], F32))
        AR = Arena(arena_t[:], AW)
        banks = [es.enter_context(nc.psum_tensor("bank%d" % i, [128, 512], F32)) for i in range(8)]
        PS = Pool("ps", [b[:] for b in banks])

        H = AR.alloc((DC, HC))
        G = AR.alloc(NV)
        RC = AR.alloc(2)
        onesD = AR.alloc(128, BF16)
        ones4D = AR.alloc(128, BF16)
        ones1 = AR.alloc(128, BF16)
        ident = AR.alloc(128, BF16)
        perm_sb = AR.alloc(128, BF16)
        epsb = AR.alloc(2)
        tabs = {}

        def Hk(c, ti):
            return ("H", c, ti)

        S.op("sp", lambda e: e.dma_start(out=G, in_=vecs), writes=["G"], dma="ldc")
        S.op("sp", lambda e: e.dma_start(out=RC, in_=rconst), writes=["RC"], dma="ldc")
        S.op("sp", lambda e: e.dma_start(out=perm_sb, in_=permM), writes=["perm"], dma="ldc")
        S.op("dve", lambda e: e.memset(onesD, 1.0 / D), writes=["onesD"])
        S.op("dve", lambda e: e.memset(ones4D, 4.0 / D), writes=["ones4D"])
        S.op("dve", lambda e: e.memset(ones1, 1.0), writes=["ones1"])
        S.op("dve", lambda e: e.memset(epsb[:, 0:1], EPS), writes=["epsb"])
        S.op("dve", lambda e: e.memset(epsb[:, 1:2], 4 * EPS), writes=["epsb"])

        def rope_precompute(pos, dst):
            m = AR.off
            CT = AR.alloc(NT)
            ST = AR.alloc(NT)
            posi = AR.alloc(NT, I32)
            ang = AR.alloc(NT)
            t1 = AR.alloc(NT)
            t2 = AR.alloc(NT)
            S.op("sp", lambda e: e.dma_start(out=posi, in_=pos.to_broadcast([128, NT])), writes=["posi"], dma="ldp")
            S.op("dve", lambda e: e.tensor_copy(out=ang, in_=posi), reads=["posi"], writes=["ang"])
            S.op("dve", lambda e: e.tensor_scalar(ang, ang, RC[:, 0:1], None, ALU.mult), reads=["ang", "RC"],
                 writes=["ang"])
            for which, dst_sb, shift in (("s", ST, 0.0), ("c", CT, np.pi / 2)):
                S.op("dve", lambda e, shift=shift: e.tensor_scalar(t1, ang, float(shift), None, ALU.add),
                     reads=["ang"], writes=["t1"])
                S.op("dve", lambda e: e.tensor_scalar(t2, t1, 1.0 / TWO_PI, MAGIC, ALU.mult, ALU.add),
                     reads=["t1"], writes=["t2"])
                S.op("dve", lambda e: e.tensor_scalar(t2, t2, MAGIC, None, ALU.subtract), reads=["t2"], writes=["t2"])
                S.op("dve", lambda e: e.scalar_tensor_tensor(out=t1, in0=t2, scalar=-C1, in1=t1, op0=ALU.mult,
                                                             op1=ALU.add), reads=["t1", "t2"], writes=["t1"])
                S.op("dve", lambda e: e.scalar_tensor_tensor(out=t1, in0=t2, scalar=-C2, in1=t1, op0=ALU.mult,
                                                             op1=ALU.add), reads=["t1", "t2"], writes=["t1"])
                S.op("dve", lambda e: e.tensor_scalar(t1, t1, 3.1415925, -3.1415925, ALU.min, ALU.max),
                     reads=["t1"], writes=["t1"])
                if which == "s":
                    S.op("act", lambda e, dst_sb=dst_sb: e.activation(out=dst_sb, in_=t1, func=AF.Sin,
                                                                      scale=RC[:, 1:2]),
                         reads=["t1", "RC"], writes=["ptab" + which])
                else:
                    S.op("act", lambda e, dst_sb=dst_sb: e.activation(out=dst_sb, in_=t1, func=AF.Sin),
                         reads=["t1"], writes=["ptab" + which])
            S.op("sp", lambda e: e.dma_start(out=dst[:, 0, :], in_=CT), reads=["ptabc"], dma="stT")
            S.op("sp", lambda e: e.dma_start(out=dst[:, 1, :], in_=ST), reads=["ptabs"], dma="stT")
            S.fence()
            AR.off = m

        def rope_tables(src):
            CT = AR.alloc(NT)
            ST = AR.alloc(NT)
            tabs["CT"], tabs["ST"] = CT, ST
            S.op("sp", lambda e: e.dma_start(out=CT, in_=src[:, 0, :]), writes=["tabc"], dma="ldt")
            S.op("sp", lambda e: e.dma_start(out=ST, in_=src[:, 1, :]), writes=["tabs"], dma="ldt")

        rope_precompute(pos_prev, tab_prev)
        rope_precompute(pos_own, tab_own)

        sq_pool = None
        misc = {}
        NTMP = [2]

        def alloc_common():
            misc["sq"] = Pool("sq", [AR.alloc(512, BF16) for _ in range(2)])
            misc["rstd"] = Pool("rstd", [AR.alloc(512) for _ in range(2)])
            misc["tmp"] = Pool("tmp", [AR.alloc(512) for _ in range(NTMP[0])])

        def aslist(k):
            return list(k) if isinstance(k, list) else [k]

        def gcol(slot, c):
            return G[:, slot * 8 + c: slot * 8 + c + 1]

        def rms_stats(src_fn, src_keys, n, onesm, eps, sq_eng="act"):
            ps, psk = PS.next()
            for c in range(DC):
                sq, sqk = misc["sq"].next()
                if sq_eng == "act":
                    S.op("act", lambda e, c=c, sq=sq: e.activation(out=sq[:, :n], in_=src_fn(c), func=AF.Square),
                         reads=aslist(src_keys(c)), writes=[sqk])
                else:
                    S.op("dve", lambda e, c=c, sq=sq: e.tensor_tensor(out=sq[:, :n], in0=src_fn(c), in1=src_fn(c),
                                                                       op=ALU.mult),
                         reads=aslist(src_keys(c)), writes=[sqk])
                S.op("pe", lambda e, c=c, sq=sq, ps=ps: e.matmul(ps[:, :n], onesm, sq[:, :n], start=(c == 0),
                                                                  stop=(c == DC - 1)),
                     reads=[sqk, "onesD", "ones4D"], writes=[psk])
            rstd, rk = misc["rstd"].next()
            S.op("act", lambda e, ps=ps, rstd=rstd: e.activation(out=rstd[:, :n], in_=ps[:, :n], func=AF.Ln,
                                                                 bias=epsb[:, 1:2] if eps > 2e-6 else epsb[:, 0:1]),
                 reads=[psk, "epsb"], writes=[rk])
            S.op("act", lambda e, rstd=rstd: e.activation(out=rstd[:, :n], in_=rstd[:, :n], func=AF.Exp, scale=-0.5),
                 reads=[rk], writes=[rk])
            return rstd, rk

        def prenorm(ti, slot, dst_fn, dst_key):
            c0, n = TILES[ti]
            rstd, rk = rms_stats(lambda c: H[:, c, c0:c0 + n], lambda c: Hk(c, ti), n, onesD, EPS)
            for c in range(DC):
                S.op("dve", lambda e, c=c: e.scalar_tensor_tensor(out=dst_fn(c), in0=H[:, c, c0:c0 + n],
                                                                  scalar=gcol(slot, c), in1=rstd[:, :n],
                                                                  op0=ALU.mult, op1=ALU.mult),
                     reads=[Hk(c, ti), rk, "G"], writes=aslist(dst_key(c)))

        def postnorm_add(ti, slot, Y_fn, Y_key, half):
            c0, n = TILES[ti]
            rstd, rk = rms_stats(Y_fn, Y_key, n, ones4D if half else onesD, 4 * EPS if half else EPS, sq_eng="act")
            for c in range(DC):
                tmp, tk = misc["tmp"].next()
                S.op("dve", lambda e, c=c, tmp=tmp: e.scalar_tensor_tensor(out=tmp[:, :n], in0=Y_fn(c),
                                                                           scalar=gcol(slot, c), in1=rstd[:, :n],
                                                                           op0=ALU.mult, op1=ALU.mult),
                     reads=aslist(Y_key(c)) + [rk, "G"], writes=[tk])
                S.op("dve", lambda e, c=c, tmp=tmp: e.tensor_tensor(out=H[:, c, c0:c0 + n], in0=H[:, c, c0:c0 + n],
                                                                    in1=tmp[:, :n], op=ALU.add),
                     reads=[tk, Hk(c, ti)], writes=[Hk(c, ti)])

        def load_dense(dst, src, key, nsplit=1):
            a = dst.shape[1]
            step = (a + nsplit - 1) // nsplit
            for i in range(0, a, step):
                S.op("pool", lambda e, i=i: e.dma_start(out=dst[:, i:i + step], in_=src[:, i:i + step]),
                     writes=[key], dma="wdense")

        def blkkeys(name, idx, lc, n):
            return [(name, idx, b) for b in range(lc // 128, (lc + n + 127) // 128)]

        def interleave(main, side):
            nm, ns = len(main), len(side)
            si = 0
            for i, t in enumerate(main):
                t()
                want = ((i + 1) * ns) // nm
                while si < want:
                    side[si]()
                    si += 1
            while si < ns:
                side[si]()
                si += 1

        def ffn(j, slot_pre, slot_post, passes):
            S.fence()
            m = AR.off
            alloc_common()
            W = HALO + 1024
            xn = AR.alloc((DC, W), BF16)
            act = AR.alloc((FC, W), BF16)
            Y = AR.alloc((DC, W))
            sgp = Pool("sg", [AR.alloc(512) for _ in range(2)])
            wgu = [(AR.alloc((DC, 128), BF16), AR.alloc((DC, 128), BF16)) for _ in range(NWG)]
            wdn = [AR.alloc((FC, 128), BF16) for _ in range(2)]
            fcount = [0, 0]

            def pre_thunks(tl):
                p0 = TILES[tl[0]][0]
                out = []
                for ti in tl:
                    c0, n = TILES[ti]
                    lc = c0 - p0
                    out.append(lambda ti=ti, lc=lc, n=n: prenorm(
                        ti, slot_pre, lambda c: xn[:, c, lc:lc + n], lambda c: blkkeys("xn", c, lc, n)))
                return out

            def gateup_thunks(tl):
                p0 = TILES[tl[0]][0]

                def one(f):
                    s = fcount[0] % NWG
                    fcount[0] += 1
                    S.op("pool", lambda e: e.dma_start(out=wgu[s][0], in_=wg[j, f].rearrange(
                        "p (k n) -> p k n", k=DC)), writes=[("wg", s)], dma="wg%d" % s)
                    S.op("pool", lambda e: e.dma_start(out=wgu[s][1], in_=wu[j, f].rearrange(
                        "p (k n) -> p k n", k=DC)), writes=[("wu", s)], dma="wu%d" % s)
                    for ti in tl:
                        c0, n = TILES[ti]
                        lc = c0 - p0
                        pg, pgk = PS.next()
                        pu, puk = PS.next()

                        def mmg(e, lc=lc, n=n, pp=pg, which=0):
                            for k in range(DC):
                                ins = e.matmul(pp[:, :n], wgu[s][which][:, k, :], xn[:, k, lc:lc + n], start=(k == 0),
                                               stop=(k == DC - 1))
                            return ins

                        xk = [k_ for c in range(DC) for k_ in blkkeys("xn", c, lc, n)]
                        S.op("pe", mmg, reads=[("wg", s)] + xk, writes=[pgk])
                        S.op("pe", lambda e, lc=lc, n=n, pu=pu, mmg=mmg: mmg(e, lc, n, pu, 1),
                             reads=[("wu", s)] + xk, writes=[puk])
                        sg, sgk = sgp.next()
                        S.op("act", lambda e, sg=sg, pg=pg, n=n: e.activation(out=sg[:, :n], in_=pg[:, :n],
                                                                               func=AF.Silu),
                             reads=[pgk], writes=[sgk])
                        S.op("dve", lambda e, sg=sg, pu=pu, n=n, lc=lc: e.tensor_tensor(
                            out=act[:, f, lc:lc + n], in0=sg[:, :n], in1=pu[:, :n], op=ALU.mult),
                             reads=[sgk, puk], writes=blkkeys("act", f, lc, n))

                return [lambda f=f: one(f) for f in range(FC)]

            def down_thunks(tl):
                p0 = TILES[tl[0]][0]

                def one(dc):
                    s = fcount[1] % 2
                    fcount[1] += 1
                    S.op("pool", lambda e: e.dma_start(out=wdn[s], in_=wd[j, dc].rearrange(
                        "p (k n) -> p k n", k=FC)), writes=[("wd", s)], dma="wd%d" % s)
                    for ti in tl:
                        c0, n = TILES[ti]
                        lc = c0 - p0
                        py, pyk = PS.next()

                        def mmd(e, lc=lc, n=n, py=py):
                            for k in range(FC):
                                ins = e.matmul(py[:, :n], wdn[s][:, k, :], act[:, k, lc:lc + n], start=(k == 0),
                                               stop=(k == FC - 1))
                            return ins

                        S.op("pe", mmd, reads=[("wd", s)] + [k_ for f in range(FC) for k_ in blkkeys("act", f, lc, n)],
                             writes=[pyk])
                        S.op("act", lambda e, py=py, lc=lc, n=n: e.activation(out=Y[:, dc, lc:lc + n],
                                                                              in_=py[:, :n], func=AF.Copy),
                             reads=[pyk], writes=blkkeys("Y", dc, lc, n))

                return [lambda dc=dc: one(dc) for dc in range(DC)]

            def post_thunks(tl):
                p0 = TILES[tl[0]][0]
                out = []
                for ti in tl:
                    c0, n = TILES[ti]
                    lc = c0 - p0
                    out.append(lambda ti=ti, lc=lc, n=n: postnorm_add(
                        ti, slot_post, lambda c: Y[:, c, lc:lc + n], lambda c: blkkeys("Y", c, lc, n), True))
                return out

            def run(ths):
                for t in ths:
                    t()

            np_ = len(passes)
            run(pre_thunks(passes[0]))
            for p_i in range(np_):
                if p_i == 0:
                    run(gateup_thunks(passes[0]))
                if p_i + 1 < np_:
                    interleave(down_thunks(passes[p_i]), pre_thunks(passes[p_i + 1]))
                    interleave(gateup_thunks(passes[p_i + 1]), post_thunks(passes[p_i]))
                else:
                    run(down_thunks(passes[p_i]))
                    run(post_thunks(passes[p_i]))
            AR.off = m

        def rope_stage1(ps, psk, n):
            kb, kbk = misc["kb"].next()
            S.op("act", lambda e: e.activation(out=kb[:, :n], in_=ps[:, :n], func=AF.Copy), reads=[psk], writes=[kbk])
            return kb, kbk

        def rope_stage2(ps, psk, kb, kbk, c0t, n, dst_fn, dkey):
            CT_, ST_ = tabs["CT"], tabs["ST"]
            p2, p2k = PS.next()
            S.op("pe", lambda e: e.matmul(p2[:, :n], perm_sb, kb[:, :n], start=True, stop=True),
                 reads=[kbk, "perm"], writes=[p2k])
            t1, t1k = misc["tmp"].next()
            t2, t2k = misc["tmp"].next()
            S.op("dve", lambda e: e.tensor_tensor(out=t1[:, :n], in0=ps[:, :n], in1=CT_[:, c0t:c0t + n], op=ALU.mult),
                 reads=[psk, "tabc"], writes=[t1k])
            S.op("dve", lambda e: e.tensor_tensor(out=t2[:, :n], in0=p2[:, :n], in1=ST_[:, c0t:c0t + n], op=ALU.mult),
                 reads=[p2k, "tabs"], writes=[t2k])
            S.op("dve", lambda e: dst_fn(e, t1[:, :n], t2[:, :n]), reads=[t1k, t2k], writes=[dkey])

        def class_views(dst3, g, tt, full):
            d = GROUPS[g][1]
            base = dst3[:, 512 * tt:512 * tt + 512] if full else dst3
            if d == 1:
                return base, None
            if d == 4:
                return base.rearrange("p (r i) -> p r i", r=4), 4
            if full:
                return dst3.rearrange("p (r i) -> p r i", r=16)[:, :, 32 * tt:32 * tt + 32], 16
            return dst3.rearrange("p (r i) -> p r i", r=16), 16

        def proj_rope(xn_tile, xn_keys, wres, wkey, tt, dst_all, dname, full):
            pend = None
            for c in range(12):
                g = c // 4
                ps, psk = PS.next()

                def mm(e, c=c, ps=ps):
                    for k in range(DC):
                        ins = e.matmul(ps[:, :512], wres[:, k, c * 128:(c + 1) * 128], xn_tile[:, k, :],
                                       start=(k == 0), stop=(k == DC - 1))
                    return ins

                S.op("pe", mm, reads=[wkey] + xn_keys, writes=[psk])
                kb, kbk = rope_stage1(ps, psk, 512)
                ov, r = class_views(dst_all[:, c, :], g, tt, full)

                def fin(e, a, b, ov=ov, r=r):
                    if r is not None:
                        a = a.rearrange("p (i r) -> p r i", r=r)
                        b = b.rearrange("p (i r) -> p r i", r=r)
                    return e.tensor_tensor(out=ov, in0=a, in1=b, op=ALU.add)

                if pend is not None:
                    rope_stage2(*pend)
                pend = (ps, psk, kb, kbk, 512 * tt, 512, fin, (dname, c, tt if full else 0))
            rope_stage2(*pend)

        def layer0_pass(xsrc, possrc, corrsrc, prev):
            S.fence()
            for c in range(DC):
                S.op("sp", lambda e, c=c: e.dma_start(out=H[:, c, :], in_=xsrc[:, c, :]),
                     writes=[Hk(c, t) for t in range(5)], dma="ldx")
            if A_:
                if upto >= 1:
                    ffn(0, 0, 1, [[0, 1, 2], [3, 4]])

            if A_ and upto >= 2:
                S.fence()
                m = AR.off
                alloc_common()
                win_sb = AR.alloc((DC, D), BF16)
                wgr_sb = AR.alloc((4, 2, 256), BF16)
                wout_sb = AR.alloc((DC, D), BF16)
                corr_sb = AR.alloc((4, 128))
                load_dense(win_sb, w_in, "w_in", 2)
                S.op("pool", lambda e: e.dma_start(out=wgr_sb, in_=w_grp), writes=["w_grp"], dma="wdense")
                load_dense(wout_sb, w_out, "w_out", 2)
                S.op("sp", lambda e: e.dma_start(out=corr_sb, in_=corrsrc), writes=["corr"], dma="ldm")
                hm = AR.alloc((DC, 512), BF16)
                U = [AR.alloc((DC, 528)) for _ in range(2)]
                Ta = AR.alloc(528)
                Tb = AR.alloc(528)
                pT = AR.alloc((DC, 512), BF16)
                yT = AR.alloc((DC, 512), BF16)
                Ym = AR.alloc((DC, 512))
                for ti in range(5):
                    c0, n = TILES[ti]
                    cur = U[ti % 2]
                    nxt = U[(ti + 1) % 2]
                    ck = "U%d" % (ti % 2)
                    nk = "U%d" % ((ti + 1) % 2)
                    prenorm(ti, 2, lambda c, n=n: hm[:, c, :n], lambda c: ("hm", c))
                    for c in range(DC):
                        ps, psk = PS.next()

                        def mm(e, c=c, ps=ps, n=n):
                            for k in range(DC):
                                ins = e.matmul(ps[:, :n], win_sb[:, k, c * 128:(c + 1) * 128], hm[:, k, :n],
                                               start=(k == 0), stop=(k == DC - 1))
                            return ins

                        S.op("pe", mm, reads=["w_in"] + [("hm", k) for k in range(DC)], writes=[psk])
                        if ti == 0:
                            S.op("act", lambda e, c=c, ps=ps, nxt=nxt: e.activation(out=nxt[:, c, 0:16], in_=ps[:, HALO - 16:HALO],
                                                                           func=AF.Copy),
                                 reads=[psk], writes=[(nk, c)])
                            continue
                        S.op("act", lambda e, c=c, ps=ps, cur=cur: e.activation(out=cur[:, c, 16:528], in_=ps[:, :512],
                                                                       func=AF.Copy),
                             reads=[psk], writes=[(ck, c)])
                        if ti < 4:
                            S.op("act", lambda e, c=c, cur=cur, nxt=nxt: e.activation(out=nxt[:, c, 0:16], in_=cur[:, c, 512:528],
                                                                    func=AF.Copy),
                                 reads=[(ck, c)], writes=[(nk, c)])
                        gi = c // 2
                        w = 2 << gi
                        Uc = cur[:, c, :]
                        src = Uc
                        srck = (ck, c)
                        sh = 1
                        bufs = [(Ta, "Ta"), (Tb, "Tb")]
                        bi = 0
                        while sh < w:
                            dstb, dk = bufs[bi]
                            S.op("dve", lambda e, src=src, dstb=dstb, sh=sh, c=c: e.tensor_tensor(
                                out=dstb[:, sh:528], in0=src[:, sh:528], in1=src[:, 0:528 - sh], op=ALU.add),
                                 reads=[srck], writes=[dk])
                            src, srck = dstb, dk
                            bi ^= 1
                            sh *= 2
                        if ti == 1:
                            S.op("dve", lambda e, src=src, gi=gi: e.tensor_tensor(out=src[:, 16:144], in0=src[:, 16:144],
                                                                                  in1=corr_sb[:, gi, :], op=ALU.mult),
                                 reads=[srck, "corr"], writes=[srck])
                        S.op("dve", lambda e, src=src, w=w, Uc=Uc, c=c: e.scalar_tensor_tensor(
                            out=pT[:, c, :], in0=src[:, 16:528], scalar=1.0 / w, in1=Uc[:, 16:528], op0=ALU.mult,
                            op1=ALU.subtract), reads=[srck, (ck, c)], writes=[("pT", c)])
                    if ti == 0:
                        continue
                    for dc in range(DC):
                        gi = dc // 2
                        ps, psk = PS.next()

                        def mm(e, dc=dc, gi=gi, ps=ps):
                            for cc in range(2):
                                ins = e.matmul(ps[:, :512], wgr_sb[:, gi, cc, (dc % 2) * 128:(dc % 2) * 128 + 128],
                                               pT[:, 2 * gi + cc, :], start=(cc == 0), stop=(cc == 1))
                            return ins

                        S.op("pe", mm, reads=["w_grp", ("pT", 2 * gi), ("pT", 2 * gi + 1)], writes=[psk])
                        S.op("act", lambda e, dc=dc, ps=ps: e.activation(out=yT[:, dc, :], in_=ps[:, :512], func=AF.Copy,
                                                                         scale=G[:, 56 + dc:57 + dc]),
                             reads=[psk, "G"], writes=[("yT", dc)])
                    for dc in range(DC):
                        ps, psk = PS.next()

                        def mm(e, dc=dc, ps=ps):
                            for k in range(DC):
                                ins = e.matmul(ps[:, :512], wout_sb[:, k, dc * 128:(dc + 1) * 128], yT[:, k, :],
                                               start=(k == 0), stop=(k == DC - 1))
                            return ins

                        S.op("pe", mm, reads=["w_out"] + [("yT", k) for k in range(DC)], writes=[psk])
                        S.op("act", lambda e, dc=dc, ps=ps: e.activation(out=Ym[:, dc, :], in_=ps[:, :512], func=AF.Copy),
                             reads=[psk], writes=[("Ym", dc)])
                    postnorm_add(ti, 3, lambda c: Ym[:, c, :], lambda c: ("Ym", c), False)
                AR.off = m

            if A_ and upto >= 3:
                ffn(1, 4, 5, [[1, 2], [3, 4]])

            if upto < 5 and not prev:
                S.fence()
                for c in range(DC):
                    S.op("sp", lambda e, c=c: e.dma_start(out=hT_out[:, c, :], in_=H[:, c, HALO:]),
                         reads=[Hk(c, t) for t in range(5)], dma="stH")
            if A_ and upto >= 4:
                S.fence()
                m = AR.off
                alloc_common()
                misc["kb"] = Pool("kb", [AR.alloc(512, BF16) for _ in range(2)])
                rope_tables(tab_prev if prev else tab_own)
                wk_sb = AR.alloc((DC, QKV), BF16)
                wv_sb = AR.alloc((DC, QKV), BF16)
                load_dense(wk_sb, w_k, "w_k", 2)
                load_dense(wv_sb, w_v, "w_v", 2)
                Kall = AR.alloc((12, 512), BF16)
                hkb = [AR.alloc((DC, 512), BF16) for _ in range(2)]
                hk16 = AR.alloc((DC, 512), BF16)
                vsb = Pool("vsb", [AR.alloc(512, BF16) for _ in range(3)])
                prenorm(1, 6, lambda c: hkb[0][:, c, :], lambda c: ("hk0", c))
                for tt in range(4 if upto >= 4.1 else 0):
                    ti = tt + 1
                    hk = hkb[tt % 2]
                    hkn = "hk%d" % (tt % 2)
                    if tt < 3:
                        prenorm(ti + 1, 6, lambda c, tt=tt: hkb[(tt + 1) % 2][:, c, :],
                                lambda c, tt=tt: ("hk%d" % ((tt + 1) % 2), c))
                    hkk = [(hkn, c) for c in range(DC)]
                    for c in range(DC if upto >= 4.3 else 0):
                        S.op("act", lambda e, c=c, hk=hk: e.activation(
                            out=hk16[:, c, :].rearrange("p (r i) -> p r i", r=16),
                            in_=hk[:, c, :].rearrange("p (i r) -> p r i", r=16), func=AF.Copy),
                             reads=[(hkn, c)], writes=[("hk16", c)])
                    proj_rope(hk, hkk, wk_sb, "w_k", tt, Kall, "Kall", False)
                    for g in range(3):
                        R = GROUPS[g][1]
                        kd = kin[g].rearrange("j p x -> p j x")
                        ksrc = Kall[:, 4 * g:4 * g + 4, :]
                        rk = [("Kall", c, 0) for c in range(4 * g, 4 * g + 4)]
                        if g < 2:
                            if prev and tt < 3:
                                continue
                            if prev and g == 0:
                                S.op("sp", lambda e, kd=kd, ksrc=ksrc: e.dma_start(out=kd[:, :, 0:128], in_=ksrc[:, :, 384:512]),
                                     reads=rk, dma="stK%d" % g)
                            elif prev:
                                S.op("sp", lambda e, kd=kd, ksrc=ksrc: e.dma_start(out=kd[:, :, 0:512], in_=ksrc), reads=rk, dma="stK%d" % g)
                            else:
                                o = R * 128 + 512 * tt
                                S.op("sp", lambda e, kd=kd, ksrc=ksrc, o=o: e.dma_start(out=kd[:, :, o:o + 512], in_=ksrc),
                                     reads=rk, dma="stK%d" % g)
                        else:
                            o = 0 if prev else 2048
                            for jj in range(4):
                                S.op("sp", lambda e, jj=jj, o=o, tt=tt: e.dma_start(
                                    out=kin[2][jj][:, o:o + 2048].rearrange("p (r i) -> p r i", r=16)[:, :, 32 * tt:32 * tt + 32],
                                    in_=Kall[:, 8 + jj, :].rearrange("p (r i) -> p r i", r=16)),
                                     reads=[("Kall", 8 + jj, 0)], dma="stK2%d" % jj)
                    for g in range(3 if upto >= 4.3 else 0):
                        d = GROUPS[g][1]
                        for b in range(4):
                            R = d
                            vd = vin[g].rearrange("j p b f -> p j b f")
                            if d == 1:
                                cols = lambda k, b=b, hk=hk: hk[:, k, 128 * b:128 * b + 128]
                                cls = [4 * tt + b]
                            elif d == 4:
                                cols = lambda k, b=b, hk=hk: hk[:, k, :].rearrange("p (i r) -> p r i", r=4)[:, b, :]
                                cls = [4 * tt + b]
                            else:
                                cols = lambda k, b=b: hk16[:, k, 128 * b:128 * b + 128]
                                cls = [4 * b + rr for rr in range(4)]
                            blks = [(k_ - (16 - R)) if prev else (R + k_) for k_ in cls]
                            if blks[0] < 0:
                                continue
                            ps, psk = PS.next()

                            def mm(e, cols=cols, ps=ps, g=g):
                                for k in range(DC):
                                    ins = e.matmul(ps[:, :512], cols(k), wv_sb[:, k, g * 512:(g + 1) * 512],
                                                   start=(k == 0), stop=(k == DC - 1))
                                return ins

                            S.op("pe", mm, reads=["w_v"] + hkk + [("hk16", c) for c in range(DC)], writes=[psk])
                            vb, vbk = vsb.next()
                            S.op("act", lambda e, vb=vb, ps=ps: e.activation(out=vb, in_=ps[:, :512], func=AF.Copy),
                                 reads=[psk], writes=[vbk])
                            if d == 16:
                                for rr in range(4):
                                    S.op("sp", lambda e, vb=vb, vd=vd, rr=rr, tt=tt, blk=blks[rr]: e.dma_start(
                                        out=vd[32 * tt:32 * tt + 32, :, blk, :],
                                        in_=vb[32 * rr:32 * rr + 32, :].rearrange("p (j f) -> p j f", j=4)), reads=[vbk], dma="stV%d" % vbk[1])
                            else:
                                S.op("sp", lambda e, vb=vb, vd=vd, blk=blks[0]: e.dma_start(
                                    out=vd[:, :, blk, :], in_=vb.rearrange("p (j f) -> p j f", j=4)), reads=[vbk], dma="stV%d" % vbk[1])
                AR.off = m

        layer0_pass(xT_prev, pos_prev, corr_prev, True)
        layer0_pass(xT_own, pos_own, corr_own, False)

        if upto >= 5:
            ffn(2, 8, 9, [[1, 2], [3, 4]])

        if upto >= 5.15:
            S.fence()
            m = AR.off
            alloc_common()
            misc["kb"] = Pool("kb", [AR.alloc(512, BF16) for _ in range(2)])
            Qall = AR.alloc((12, NT), BF16)
            m2 = AR.off
            rope_tables(tab_own)
            wq_sb = AR.alloc((DC, QKV), BF16)
            load_dense(wq_sb, w_q, "w_q", 2)
            hmqb = [AR.alloc((DC, 512), BF16) for _ in range(2)]
            prenorm(1, 10, lambda c: hmqb[0][:, c, :], lambda c: ("hmq0", c))
            for tt in range(4):
                ti = tt + 1
                if tt < 3:
                    prenorm(ti + 1, 10, lambda c, tt=tt: hmqb[(tt + 1) % 2][:, c, :],
                            lambda c, tt=tt: ("hmq%d" % ((tt + 1) % 2), c))
                proj_rope(hmqb[tt % 2], [("hmq%d" % (tt % 2), c) for c in range(DC)], wq_sb, "w_q", tt, Qall, "Qall",
                          True)
            S.fence()
            AR.off = m2
            mk = AR.alloc((2, 512), BF16)
            S.op("sp", lambda e: e.dma_start(out=mk, in_=masks), writes=["mk"], dma="ldm")
            S.op("sp", lambda e: e.dma_start(out=ident, in_=masks_id), writes=["ident"], dma="ldm")
            wo_sb = AR.alloc((4, D), BF16)
            load_dense(wo_sb, w_o, "w_o", 1)
            OT = AR.alloc((4, NT), BF16)
            m3 = AR.off
            ND = AR.alloc((2, NT))
            kbuf = [AR.alloc(32 * 128, BF16) for _ in range(2)]
            vbuf = [AR.alloc((32, 128), BF16) for _ in range(2)]
            ptp = Pool("pt", [AR.alloc(512, BF16) for _ in range(3)])
            Qk = [("Qall", c, tt) for c in range(12) for tt in range(4)]
            slot_ctr = [0]

            def begin_group(jj, g):
                R = GROUPS[g][1]
                nb = R + 16
                s_ = slot_ctr[0] % 2
                slot_ctr[0] += 1
                S.op("sp", lambda e: e.dma_start(out=kbuf[s_][:, :nb * 128], in_=kin[g][jj]),
                     writes=[("kbuf", s_)], dma="kb%d" % s_)
                S.op("sp", lambda e: e.dma_start(out=vbuf[s_][:, :nb, :], in_=vin[g][jj]),
                     writes=[("vbuf", s_)], dma="vb%d" % s_)
                return dict(jj=jj, g=g, R=R, s=s_, Qc=Qall[:, 4 * g + jj, :])

            def S_stage(ctx, k):
                jj, g, R, s_, Qc = ctx["jj"], ctx["g"], ctx["R"], ctx["s"], ctx["Qc"]
                first = k < R
                pt, ptk = ptp.next()
                for hh in range(2):
                    pss, pssk = PS.next()

                    def mms(e, pss=pss, hh=hh):
                        e.matmul(pss[:, :256], ident, mk[:, 1 if first else 0, 0:256], start=True, stop=False)
                        q = Qc[:, 128 * k:128 * k + 128]
                        lo = 64 * hh
                        for w_, kb_ in enumerate((k, k + R)):
                            ins = e.matmul(pss[:, w_ * 128:(w_ + 1) * 128],
                                           kbuf[s_][lo:lo + 64, kb_ * 128:(kb_ + 1) * 128], q[lo:lo + 64, :],
                                           start=False, stop=(w_ == 1))
                        return ins

                    S.op("pe", mms, reads=[("kbuf", s_), "mk", "ident"] + [("Qall", 4 * g + jj, tt) for tt in
                                                                            range(4)], writes=[pssk])
                    S.op("act", lambda e, pss=pss, hh=hh: e.activation(
                        out=pt[:, 256 * hh:256 * hh + 256], in_=pss[:, :256], func=AF.Exp, scale=0.125),
                         reads=[pssk], writes=[(ptk, hh)])
                return pt, ptk

            def PV_stage(ctx, k, pt, ptk):
                g, R, s_ = ctx["g"], ctx["R"], ctx["s"]
                pso, psok = PS.next()

                def mmo(e):
                    for hh in range(2):
                        lo = 64 * hh
                        for w_, kb_ in enumerate((k, k + R)):
                            e.matmul(pso[lo:lo + 64, 0:128], vbuf[s_][:, kb_, lo:lo + 64],
                                     pt[:, (2 * hh + w_) * 128:(2 * hh + w_ + 1) * 128], start=(w_ == 0),
                                     stop=(w_ == 1))
                        for w_ in range(2):
                            ins = e.matmul(pso[lo:lo + 64, 128:256], ones1[:, 0:64],
                                           pt[:, (2 * hh + w_) * 128:(2 * hh + w_ + 1) * 128], start=(w_ == 0),
                                           stop=(w_ == 1))
                    return ins

                S.op("pe", mmo, reads=[("vbuf", s_), (ptk, 0), (ptk, 1), "ones1"], writes=[psok])
                if R == 1:
                    ndv = ND[:, :, 128 * k:128 * k + 128]
                elif R == 4:
                    n_, r_ = k // 4, k % 4
                    ndv = ND[:, :, 512 * n_:512 * n_ + 512].rearrange("p x (i r) -> p x r i", r=4)[:, :, r_, :]
                else:
                    ndv = ND.rearrange("p x (i r) -> p x r i", r=16)[:, :, k, :]
                psv = pso[:, 0:256].rearrange("p (x i) -> p x i", x=2)
                if g == 0:
                    S.op("dve", lambda e: e.tensor_copy(out=ndv, in_=psv), reads=[psok], writes=["NDall"])
                else:
                    S.op("dve", lambda e: e.tensor_tensor(out=ndv, in0=psv, in1=ndv, op=ALU.add),
                         reads=[psok, "NDall"], writes=["NDall"])

            def end_jj(jj):
                for tt in range(4):
                    S.op("dve", lambda e, tt=tt: e.reciprocal(out=ND[:, 1, 512 * tt:512 * tt + 512],
                                                              in_=ND[:, 1, 512 * tt:512 * tt + 512]),
                         reads=["NDall"], writes=["NDall"])
                    S.op("dve", lambda e, tt=tt: e.tensor_tensor(out=OT[:, jj, 512 * tt:512 * tt + 512],
                                                                 in0=ND[:, 0, 512 * tt:512 * tt + 512],
                                                                 in1=ND[:, 1, 512 * tt:512 * tt + 512],
                                                                 op=ALU.mult),
                         reads=["NDall"], writes=[("OT", jj, tt)])

            steps = [(jj, g, k) for jj in range(4) for g in range(3) for k in range(16)]
            ctxs = {}

            def get_ctx(jj, g):
                if (jj, g) not in ctxs:
                    ctxs[(jj, g)] = begin_group(jj, g)
                return ctxs[(jj, g)]

            cur = S_stage(get_ctx(0, 0), 0)
            for i, (jj, g, k) in enumerate(steps):
                nxt = None
                if i + 1 < len(steps):
                    j2, g2, k2 = steps[i + 1]
                    nxt = S_stage(get_ctx(j2, g2), k2)
                PV_stage(get_ctx(jj, g), k, *cur)
                if g == 2 and k == 15:
                    end_jj(jj)
                cur = nxt
            S.fence()
            AR.off = m3
            Ym = AR.alloc((DC, 512))
            for tt in range(4):
                ti = tt + 1
                for dc in range(DC):
                    ps, psk = PS.next()

                    def mm(e, dc=dc, ps=ps, tt=tt):
                        for k in range(4):
                            ins = e.matmul(ps[:, :512], wo_sb[:, k, dc * 128:(dc + 1) * 128],
                                           OT[:, k, 512 * tt:512 * tt + 512], start=(k == 0), stop=(k == 3))
                        return ins

                    S.op("pe", mm, reads=["w_o"] + [("OT", k, tt) for k in range(4)], writes=[psk])
                    S.op("act", lambda e, dc=dc, ps=ps: e.activation(out=Ym[:, dc, :], in_=ps[:, :512], func=AF.Copy),
                         reads=[psk], writes=[("Ym", dc)])
                postnorm_add(ti, 11, lambda c: Ym[:, c, :], lambda c: ("Ym", c), False)
            AR.off = m

        if upto >= 5.25:
            ffn(3, 12, 13, [[1, 2], [3, 4]])
        if upto >= 5:
            S.fence()
            for c in range(DC):
                S.op("sp", lambda e, c=c: e.dma_start(out=hT_out[:, c, :], in_=H[:, c, HALO:]),
                     reads=[Hk(c, t) for t in range(5)], dma="stH")

        S.fence()
        final_waits = dict(S.fence_tok)

        sems = {}
        for e in ENGS:
            sems[("e", e)] = es.enter_context(nc.semaphore("sem_" + e))
        for k in S.dcnt:
            sems[("d", k)] = es.enter_context(nc.semaphore("dsem_" + k))
        engmap = {"pe": "tensor", "act": "scalar", "dve": "vector", "pool": "gpsimd", "sp": "sync"}
        with nc.Block() as block:
            def make(eng):
                def body(e):
                    for waits, fn, tok in S.q[eng]:
                        for sk, v in waits:
                            e.wait_ge(sems[sk], v)
                        ins = fn(e)
                        ins.then_inc(sems[tok[0]], 16 if tok[0][0] == "d" else 1)
                    if eng == "sp":
                        for sk, v in final_waits.items():
                            if v > 0:
                                e.wait_ge(sems[sk], v)
                return body

            for eng in ENGS:
                getattr(block, engmap[eng])(make(eng))
    return nc


def _fm(a):
    T, F = a.shape
    return np.ascontiguousarray(a.T.reshape(F // 128, 128, T).transpose(1, 0, 2))


def _dense(w):
    k, n = w.shape
    return np.ascontiguousarray(w.reshape(k // 128, 128, n).transpose(1, 0, 2))


def _ffn_w(gate, up, down):
    gate = gate.reshape(4, DC, 128, FC, 128)
    up = up.reshape(4, DC, 128, FC, 128)
    down = down.reshape(4, FC, 128, DC, 128)
    wg = np.ascontiguousarray(gate.transpose(0, 3, 2, 1, 4)).reshape(4, FC, 128, DC * 128)
    wu = np.ascontiguousarray(up.transpose(0, 3, 2, 1, 4)).reshape(4, FC, 128, DC * 128)
    wd = np.ascontiguousarray(down.transpose(0, 3, 2, 1, 4)).reshape(4, DC, 128, FC * 128)
    return wg, wu, wd


def _vecs(norm_gain, kv_gain, pool_scale):
    v = np.zeros((128, 14 * 8), np.float32)
    for i in range(6):
        v[:, i * 8:(i + 1) * 8] = norm_gain[0, i].reshape(8, 128).T
        v[:, (8 + i) * 8:(9 + i) * 8] = norm_gain[1, i].reshape(8, 128).T
    v[:, 48:56] = kv_gain.reshape(8, 128).T
    v[:, 56:64] = pool_scale.reshape(8, 128).T
    return v


_NC_CACHE = {}


def _get_nc():
    if "F" not in _NC_CACHE:
        _NC_CACHE["F"] = build()
    return _NC_CACHE["F"]


def _corr(is_seq_start):
    corr = np.ones((128, 4, 128), np.float32)
    if is_seq_start:
        t = np.arange(128)
        for g, w in enumerate((2, 4, 8, 16)):
            corr[:, g, :] = (w / np.minimum(t + 1, w)).astype(np.float32)[None, :]
    return corr


def make_in_maps(x, positions, norm_gain, ffn_w_gate, ffn_w_up, ffn_w_down, pool_w_in, pool_w_group, pool_scale,
                 pool_w_out, kv_norm_gain, w_k, w_v, attn_w_q, attn_w_o):
    x = np.asarray(x, np.float32)
    positions = np.asarray(positions, np.int32)
    bf = ml_dtypes.bfloat16
    p = np.arange(128)
    inv_freq = (np.float32(10000.0) ** (-(np.arange(0, 64, 2, dtype=np.float32)) / np.float32(64))).astype(np.float32)
    rconst = np.stack([inv_freq[p % 32], np.where((p % 64) < 32, -1.0, 1.0)], axis=1).astype(np.float32)
    partner = np.where((p % 64) < 32, p + 32, p - 32)
    permM = np.zeros((128, 128), np.float32)
    permM[partner, p] = 1.0
    permM = permM.astype(bf)
    wg, wu, wd = _ffn_w(np.asarray(ffn_w_gate, np.float32), np.asarray(ffn_w_up, np.float32),
                        np.asarray(ffn_w_down, np.float32))
    vecs = _vecs(np.asarray(norm_gain, np.float32), np.asarray(kv_norm_gain, np.float32),
                 np.asarray(pool_scale[0], np.float32))
    common = dict(
        wg=wg, wu=wu, wd=wd, vecs=vecs, rconst=rconst, permM=permM,
        w_in=_dense(np.asarray(pool_w_in[0], np.float32)), w_out=_dense(np.asarray(pool_w_out[0], np.float32)),
        w_grp=np.ascontiguousarray(np.asarray(pool_w_group[0], np.float32).reshape(4, 2, 128, 256).transpose(2, 0, 1, 3)),
        w_k=_dense(np.asarray(w_k, np.float32)), w_v=_dense(np.asarray(w_v, np.float32)),
        w_q=_dense(np.asarray(attn_w_q[0], np.float32)), w_o=_dense(np.asarray(attn_w_o[0], np.float32)),
        masks_id=np.eye(128, dtype=np.float32).astype(bf))
    kk = np.arange(128)[:, None]
    qq = np.arange(128)[None, :]
    mprev = np.where(kk >= qq, 0.0, NEG).astype(np.float32)
    mcur = np.where(kk <= qq, 0.0, NEG).astype(np.float32)
    mall = np.full((128, 128), NEG, np.float32)
    MN = np.concatenate([mprev, mcur, mprev, mcur], axis=1)
    MF0 = np.concatenate([mall, mcur, mall, mcur], axis=1)
    in_maps = []
    for c in range(8):
        b, q = divmod(c, 4)
        s0 = q * NT

        def xslice(start):
            xs = np.zeros((HC, D), np.float32)
            lo = start - HALO
            if start >= 0:
                a = max(lo, 0)
                xs[a - lo:] = x[b, a:start + NT]
            return _fm(xs)

        pp = s0 - NT
        pos_prev = positions[b:b + 1, pp:pp + NT] if q > 0 else positions[b:b + 1, 0:NT]
        d = dict(common)
        d.update(xT_prev=xslice(s0 - NT if q > 0 else -10 ** 9), xT_own=xslice(s0),
                 pos_prev=np.ascontiguousarray(pos_prev), pos_own=np.ascontiguousarray(positions[b:b + 1, s0:s0 + NT]),
                 corr_prev=_corr(q == 1), corr_own=_corr(q == 0),
                 masks=np.stack([MN, MF0 if q == 0 else MN], axis=1).astype(bf))
        in_maps.append(d)
    return in_maps


def kernel(x, positions, norm_gain, ffn_w_gate, ffn_w_up, ffn_w_down, pool_w_in, pool_w_group, pool_scale,
           pool_w_out, kv_norm_gain, w_k, w_v, attn_w_q, attn_w_o):
    in_maps = make_in_maps(x, positions, norm_gain, ffn_w_gate, ffn_w_up, ffn_w_down, pool_w_in, pool_w_group,
                           pool_scale, pool_w_out, kv_norm_gain, w_k, w_v, attn_w_q, attn_w_o)
    res = run_bass_kernel_spmd(_get_nc(), in_maps, core_ids=list(range(8))).results
    out = np.zeros((2, 8192, D), np.float32)
    for c in range(8):
        b, q = divmod(c, 4)
        hT = np.asarray(res[c]["hT"])
        out[b, q * NT:(q + 1) * NT] = hT.transpose(2, 1, 0).reshape(NT, D)
    return out
```

```python
import contextlib
import os
KSTEP = int(os.environ.get('KSTEP', '9'))
import numpy as np
import ml_dtypes
import concourse.bass as bass
import concourse.mybir as mybir
from concourse.bass_utils import run_bass_kernel_spmd

F32 = mybir.dt.float32
BF16 = mybir.dt.bfloat16
I32 = mybir.dt.int32
AF = mybir.ActivationFunctionType
ALU = mybir.AluOpType

D = 1024
DFF = 2816
NT = 2048
HALO = 128
HC = HALO + NT
DC = 8
FC = 22
QKV = 1536
EPS = 1e-6
NEG = -30000.0
NWG = 2
GROUPS = ((128, 1), (512, 4), (2048, 16))
ENGS = ("pe", "act", "dve", "pool", "sp")
SAME_ENGINE_FREE = ("pe", "sp")
MAGIC = 12582912.0
TWO_PI = 6.283185307179586
C1 = 6.28125
C2 = TWO_PI - C1

TILES = [(0, HALO)] + [(HALO + 512 * i, 512) for i in range(4)]


class Sched:
    def __init__(self):
        self.q = {e: [] for e in ENGS}
        self.cnt = {e: 0 for e in ENGS}
        self.dcnt = {}
        self.seen = {e: {} for e in ENGS}
        self.last_w = {}
        self.readers = {}
        self.fence_tok = None
        self.fence_done = set(ENGS)

    def fence(self):
        tok = {("e", e): self.cnt[e] for e in ENGS}
        for k, v in self.dcnt.items():
            tok[("d", k)] = v
        self.fence_tok = tok
        self.fence_done = set()

    def op(self, eng, fn, reads=(), writes=(), dma=None):
        waits = {}
        writes = list(writes) + [k for k in reads if isinstance(k, tuple) and k[0] == "ps" and k not in writes]

        def need(dep):
            if dep is None:
                return
            sk, val = dep
            if sk == ("e", eng) and eng in SAME_ENGINE_FREE:
                return
            if sk[0] == "d":
                val = self.dcnt[sk[1]]
            if val <= 0 or self.seen[eng].get(sk, 0) >= val:
                return
            if waits.get(sk, 0) < val:
                waits[sk] = val

        if eng not in self.fence_done:
            for sk, v in self.fence_tok.items():
                need((sk, v))
            self.fence_done.add(eng)
        for k in reads:
            need(self.last_w.get(k))
        for k in writes:
            need(self.last_w.get(k))
            for r in self.readers.get(k, ()):
                need(r)
        for sk, v in waits.items():
            self.seen[eng][sk] = v
        if dma is None:
            self.cnt[eng] += 1
            tok = (("e", eng), self.cnt[eng])
        else:
            self.dcnt[dma] = self.dcnt.get(dma, 0) + 16
            tok = (("d", dma), self.dcnt[dma])
        for k in writes:
            self.last_w[k] = tok
            self.readers[k] = []
        for k in reads:
            self.readers.setdefault(k, []).append(tok)
        self.q[eng].append((sorted(waits.items(), key=str), fn, tok))


class Arena:
    def __init__(self, ap, nwords):
        self.ap = ap
        self.n = nwords
        self.off = 0

    def alloc(self, free, dtype=F32):
        if isinstance(free, int):
            free = (free,)
        n = int(np.prod(free))
        words = n if dtype != BF16 else (n + 1) // 2
        words = (words + 7) // 8 * 8
        assert self.off + words <= self.n, ("arena overflow", self.off, words, self.n)
        v = self.ap[:, self.off:self.off + words]
        self.off += words
        if dtype == BF16:
            v = v.bitcast(BF16)
        elif dtype == I32:
            v = v.bitcast(I32)
        v = v[:, 0:n]
        if len(free) == 2:
            v = v.rearrange("p (a b) -> p a b", a=free[0])
        elif len(free) == 3:
            v = v.rearrange("p (a b c) -> p a b c", a=free[0], b=free[1])
        return v


class Pool:
    def __init__(self, name, bufs):
        self.name = name
        self.bufs = bufs
        self.i = -1

    def next(self):
        self.i = (self.i + 1) % len(self.bufs)
        return self.bufs[self.i], (self.name, self.i)


def build(upto=99, dbg=False):
    nc = bass.Bass("TRN2", target_bir_lowering=False)
    S = Sched()
    A_ = True

    def din(name, shape, dt=F32):
        return nc.dram_tensor(name, list(shape), dt, kind="ExternalInput").ap()

    def dout(name, shape, dt=F32):
        return nc.dram_tensor(name, list(shape), dt, kind="ExternalOutput").ap()

    xT_prev = din("xT_prev", [128, DC, HC])
    xT_own = din("xT_own", [128, DC, HC])
    wg = din("wg", [4, FC, 128, DC * 128])
    wu = din("wu", [4, FC, 128, DC * 128])
    wd = din("wd", [4, DC, 128, FC * 128])
    NV = 14 * 8
    vecs = din("vecs", [128, NV])
    pos_prev = din("pos_prev", [1, NT], I32)
    pos_own = din("pos_own", [1, NT], I32)
    rconst = din("rconst", [128, 2])
    permM = din("permM", [128, 128], BF16)
    w_in = din("w_in", [128, DC, D])
    w_grp = din("w_grp", [128, 4, 2, 256])
    w_out = din("w_out", [128, DC, D])
    w_k = din("w_k", [128, DC, QKV])
    w_v = din("w_v", [128, DC, QKV])
    corr_prev = din("corr_prev", [128, 4, 128])
    corr_own = din("corr_own", [128, 4, 128])
    w_q = din("w_q", [128, DC, QKV])
    w_o = din("w_o", [128, 4, D])
    masks = din("masks", [128, 2, 512], BF16)
    masks_id = din("masks_id", [128, 128], BF16)
    hT_out = dout("hT", [128, DC, NT])
    skind = "ExternalOutput" if dbg else "Internal"
    kin = [nc.dram_tensor("kin%d" % g, [4, 128, (GROUPS[g][1] + 16) * 128], BF16, kind=skind).ap() for g in range(3)]
    vin = [nc.dram_tensor("vin%d" % g, [4, 128, (GROUPS[g][1] + 16), 128], BF16, kind=skind).ap() for g in range(3)]
    tab_prev = nc.dram_tensor("tab_prev", [128, 2, NT], F32).ap()
    tab_own = nc.dram_tensor("tab_own", [128, 2, NT], F32).ap()

    es = contextlib.ExitStack()
    with es:
        AW = 53000
        arena_t = es.enter_context(nc.sbuf_tensor("arena", [128, AW], F32))
        AR = Arena(arena_t[:], AW)
        banks = [es.enter_context(nc.psum_tensor("bank%d" % i, [128, 512], F32)) for i in range(8)]
        PS = Pool("ps", [b[:] for b in banks])

        H = AR.alloc((DC, HC))
        G = AR.alloc(NV)
        RC = AR.alloc(2)
        onesD = AR.alloc(128, BF16)
        ones4D = AR.alloc(128, BF16)
        ones1 = AR.alloc(128, BF16)
        ident = AR.alloc(128, BF16)
        perm_sb = AR.alloc(128, BF16)
        epsb = AR.alloc(2)
        tabs = {}

        def Hk(c, ti):
            return ("H", c, ti)

        S.op("sp", lambda e: e.dma_start(out=G, in_=vecs), writes=["G"], dma="ldc")
        S.op("sp", lambda e: e.dma_start(out=RC, in_=rconst), writes=["RC"], dma="ldc")
        S.op("sp", lambda e: e.dma_start(out=perm_sb, in_=permM), writes=["perm"], dma="ldc")
        S.op("dve", lambda e: e.memset(onesD, 1.0 / D), writes=["onesD"])
        S.op("dve", lambda e: e.memset(ones4D, 4.0 / D), writes=["ones4D"])
        S.op("dve", lambda e: e.memset(ones1, 1.0), writes=["ones1"])
        S.op("dve", lambda e: e.memset(epsb[:, 0:1], EPS), writes=["epsb"])
        S.op("dve", lambda e: e.memset(epsb[:, 1:2], 4 * EPS), writes=["epsb"])

        def rope_precompute(pos, dst):
            m = AR.off
            CT = AR.alloc(NT)
            ST = AR.alloc(NT)
            posi = AR.alloc(NT, I32)
            ang = AR.alloc(NT)
            t1 = AR.alloc(NT)
            t2 = AR.alloc(NT)
            S.op("sp", lambda e: e.dma_start(out=posi, in_=pos.to_broadcast([128, NT])), writes=["posi"], dma="ldp")
            S.op("dve", lambda e: e.tensor_copy(out=ang, in_=posi), reads=["posi"], writes=["ang"])
            S.op("dve", lambda e: e.tensor_scalar(ang, ang, RC[:, 0:1], None, ALU.mult), reads=["ang", "RC"],
                 writes=["ang"])
            for which, dst_sb, shift in (("s", ST, 0.0), ("c", CT, np.pi / 2)):
                S.op("dve", lambda e, shift=shift: e.tensor_scalar(t1, ang, float(shift), None, ALU.add),
                     reads=["ang"], writes=["t1"])
                S.op("dve", lambda e: e.tensor_scalar(t2, t1, 1.0 / TWO_PI, MAGIC, ALU.mult, ALU.add),
                     reads=["t1"], writes=["t2"])
                S.op("dve", lambda e: e.tensor_scalar(t2, t2, MAGIC, None, ALU.subtract), reads=["t2"], writes=["t2"])
                S.op("dve", lambda e: e.scalar_tensor_tensor(out=t1, in0=t2, scalar=-C1, in1=t1, op0=ALU.mult,
                                                             op1=ALU.add), reads=["t1", "t2"], writes=["t1"])
                S.op("dve", lambda e: e.scalar_tensor_tensor(out=t1, in0=t2, scalar=-C2, in1=t1, op0=ALU.mult,
                                                             op1=ALU.add), reads=["t1", "t2"], writes=["t1"])
                S.op("dve", lambda e: e.tensor_scalar(t1, t1, 3.1415925, -3.1415925, ALU.min, ALU.max),
                     reads=["t1"], writes=["t1"])
                if which == "s":
                    S.op("act", lambda e, dst_sb=dst_sb: e.activation(out=dst_sb, in_=t1, func=AF.Sin,
                                                                      scale=RC[:, 1:2]),
                         reads=["t1", "RC"], writes=["ptab" + which])
                else:
                    S.op("act", lambda e, dst_sb=dst_sb: e.activation(out=dst_sb, in_=t1, func=AF.Sin),
                         reads=["t1"], writes=["ptab" + which])
            S.op("sp", lambda e: e.dma_start(out=dst[:, 0, :], in_=CT), reads=["ptabc"], dma="stT")
            S.op("sp", lambda e: e.dma_start(out=dst[:, 1, :], in_=ST), reads=["ptabs"], dma="stT")
            S.fence()
            AR.off = m

        def rope_tables(src):
            CT = AR.alloc(NT)
            ST = AR.alloc(NT)
            tabs["CT"], tabs["ST"] = CT, ST
            S.op("sp", lambda e: e.dma_start(out=CT, in_=src[:, 0, :]), writes=["tabc"], dma="ldt")
            S.op("sp", lambda e: e.dma_start(out=ST, in_=src[:, 1, :]), writes=["tabs"], dma="ldt")

        rope_precompute(pos_prev, tab_prev)
        rope_precompute(pos_own, tab_own)

        sq_pool = None
        misc = {}
        NTMP = [2]

        def alloc_common():
            misc["sq"] = Pool("sq", [AR.alloc(512, BF16) for _ in range(2)])
            misc["rstd"] = Pool("rstd", [AR.alloc(512) for _ in range(2)])
            misc["tmp"] = Pool("tmp", [AR.alloc(512) for _ in range(NTMP[0])])

        def aslist(k):
            return list(k) if isinstance(k, list) else [k]

        def gcol(slot, c):
            return G[:, slot * 8 + c: slot * 8 + c + 1]

        def rms_stats(src_fn, src_keys, n, onesm, eps, sq_eng="act"):
            ps, psk = PS.next()
            for c in range(DC):
                sq, sqk = misc["sq"].next()
                if sq_eng == "act":
                    S.op("act", lambda e, c=c, sq=sq: e.activation(out=sq[:, :n], in_=src_fn(c), func=AF.Square),
                         reads=aslist(src_keys(c)), writes=[sqk])
                else:
                    S.op("dve", lambda e, c=c, sq=sq: e.tensor_tensor(out=sq[:, :n], in0=src_fn(c), in1=src_fn(c),
                                                                       op=ALU.mult),
                         reads=aslist(src_keys(c)), writes=[sqk])
                S.op("pe", lambda e, c=c, sq=sq, ps=ps: e.matmul(ps[:, :n], onesm, sq[:, :n], start=(c == 0),
                                                                  stop=(c == DC - 1)),
                     reads=[sqk, "onesD", "ones4D"], writes=[psk])
            rstd, rk = misc["rstd"].next()
            S.op("act", lambda e, ps=ps, rstd=rstd: e.activation(out=rstd[:, :n], in_=ps[:, :n], func=AF.Ln,
                                                                 bias=epsb[:, 1:2] if eps > 2e-6 else epsb[:, 0:1]),
                 reads=[psk, "epsb"], writes=[rk])
            S.op("act", lambda e, rstd=rstd: e.activation(out=rstd[:, :n], in_=rstd[:, :n], func=AF.Exp, scale=-0.5),
                 reads=[rk], writes=[rk])
            return rstd, rk

        def prenorm(ti, slot, dst_fn, dst_key):
            c0, n = TILES[ti]
            rstd, rk = rms_stats(lambda c: H[:, c, c0:c0 + n], lambda c: Hk(c, ti), n, onesD, EPS)
            for c in range(DC):
                S.op("dve", lambda e, c=c: e.scalar_tensor_tensor(out=dst_fn(c), in0=H[:, c, c0:c0 + n],
                                                                  scalar=gcol(slot, c), in1=rstd[:, :n],
                                                                  op0=ALU.mult, op1=ALU.mult),
                     reads=[Hk(c, ti), rk, "G"], writes=aslist(dst_key(c)))

        def postnorm_add(ti, slot, Y_fn, Y_key, half):
            c0, n = TILES[ti]
            rstd, rk = rms_stats(Y_fn, Y_key, n, ones4D if half else onesD, 4 * EPS if half else EPS, sq_eng="act")
            for c in range(DC):
                tmp, tk = misc["tmp"].next()
                S.op("dve", lambda e, c=c, tmp=tmp: e.scalar_tensor_tensor(out=tmp[:, :n], in0=Y_fn(c),
                                                                           scalar=gcol(slot, c), in1=rstd[:, :n],
                                                                           op0=ALU.mult, op1=ALU.mult),
                     reads=aslist(Y_key(c)) + [rk, "G"], writes=[tk])
                S.op("dve", lambda e, c=c, tmp=tmp: e.tensor_tensor(out=H[:, c, c0:c0 + n], in0=H[:, c, c0:c0 + n],
                                                                    in1=tmp[:, :n], op=ALU.add),
                     reads=[tk, Hk(c, ti)], writes=[Hk(c, ti)])

        def load_dense(dst, src, key, nsplit=1):
            a = dst.shape[1]
            step = (a + nsplit - 1) // nsplit
            for i in range(0, a, step):
                S.op("pool", lambda e, i=i: e.dma_start(out=dst[:, i:i + step], in_=src[:, i:i + step]),
                     writes=[key], dma="wdense")

        def blkkeys(name, idx, lc, n):
            return [(name, idx, b) for b in range(lc // 128, (lc + n + 127) // 128)]

        def interleave(main, side):
            nm, ns = len(main), len(side)
            si = 0
            for i, t in enumerate(main):
                t()
                want = ((i + 1) * ns) // nm
                while si < want:
                    side[si]()
                    si += 1
            while si < ns:
                side[si]()
                si += 1

        def ffn(j, slot_pre, slot_post, passes):
            S.fence()
            m = AR.off
            alloc_common()
            W = HALO + 1024
            xn = AR.alloc((DC, W), BF16)
            act = AR.alloc((FC, W), BF16)
            Y = AR.alloc((DC, W))
            sgp = Pool("sg", [AR.alloc(512) for _ in range(2)])
            wgu = [(AR.alloc((DC, 128), BF16), AR.alloc((DC, 128), BF16)) for _ in range(NWG)]
            wdn = [AR.alloc((FC, 128), BF16) for _ in range(2)]
            fcount = [0, 0]

            def pre_thunks(tl):
                p0 = TILES[tl[0]][0]
                out = []
                for ti in tl:
                    c0, n = TILES[ti]
                    lc = c0 - p0
                    out.append(lambda ti=ti, lc=lc, n=n: prenorm(
                        ti, slot_pre, lambda c: xn[:, c, lc:lc + n], lambda c: blkkeys("xn", c, lc, n)))
                return out

            def gateup_thunks(tl):
                p0 = TILES[tl[0]][0]

                def one(f):
                    s = fcount[0] % NWG
                    fcount[0] += 1
                    S.op("pool", lambda e: e.dma_start(out=wgu[s][0], in_=wg[j, f].rearrange(
                        "p (k n) -> p k n", k=DC)), writes=[("wg", s)], dma="wg%d" % s)
                    S.op("pool", lambda e: e.dma_start(out=wgu[s][1], in_=wu[j, f].rearrange(
                        "p (k n) -> p k n", k=DC)), writes=[("wu", s)], dma="wu%d" % s)
                    for ti in tl:
                        c0, n = TILES[ti]
                        lc = c0 - p0
                        pg, pgk = PS.next()
                        pu, puk = PS.next()

                        def mmg(e, lc=lc, n=n, pp=pg, which=0):
                            for k in range(DC):
                                ins = e.matmul(pp[:, :n], wgu[s][which][:, k, :], xn[:, k, lc:lc + n], start=(k == 0),
                                               stop=(k == DC - 1))
                            return ins

                        xk = [k_ for c in range(DC) for k_ in blkkeys("xn", c, lc, n)]
                        S.op("pe", mmg, reads=[("wg", s)] + xk, writes=[pgk])
                        S.op("pe", lambda e, lc=lc, n=n, pu=pu, mmg=mmg: mmg(e, lc, n, pu, 1),
                             reads=[("wu", s)] + xk, writes=[puk])
                        sg, sgk = sgp.next()
                        S.op("act", lambda e, sg=sg, pg=pg, n=n: e.activation(out=sg[:, :n], in_=pg[:, :n],
                                                                               func=AF.Silu),
                             reads=[pgk], writes=[sgk])
                        S.op("dve", lambda e, sg=sg, pu=pu, n=n, lc=lc: e.tensor_tensor(
                            out=act[:, f, lc:lc + n], in0=sg[:, :n], in1=pu[:, :n], op=ALU.mult),
                             reads=[sgk, puk], writes=blkkeys("act", f, lc, n))

                return [lambda f=f: one(f) for f in range(FC)]

            def down_thunks(tl):
                p0 = TILES[tl[0]][0]

                def one(dc):
                    s = fcount[1] % 2
                    fcount[1] += 1
                    S.op("pool", lambda e: e.dma_start(out=wdn[s], in_=wd[j, dc].rearrange(
                        "p (k n) -> p k n", k=FC)), writes=[("wd", s)], dma="wd%d" % s)
                    for ti in tl:
                        c0, n = TILES[ti]
                        lc = c0 - p0
                        py, pyk = PS.next()

                        def mmd(e, lc=lc, n=n, py=py):
                            for k in range(FC):
                                ins = e.matmul(py[:, :n], wdn[s][:, k, :], act[:, k, lc:lc + n], start=(k == 0),
                                               stop=(k == FC - 1))
                            return ins

                        S.op("pe", mmd, reads=[("wd", s)] + [k_ for f in range(FC) for k_ in blkkeys("act", f, lc, n)],
                             writes=[pyk])
                        S.op("act", lambda e, py=py, lc=lc, n=n: e.activation(out=Y[:, dc, lc:lc + n],
                                                                              in_=py[:, :n], func=AF.Copy),
                             reads=[pyk], writes=blkkeys("Y", dc, lc, n))

                return [lambda dc=dc: one(dc) for dc in range(DC)]

            def post_thunks(tl):
                p0 = TILES[tl[0]][0]
                out = []
                for ti in tl:
                    c0, n = TILES[ti]
                    lc = c0 - p0
                    out.append(lambda ti=ti, lc=lc, n=n: postnorm_add(
                        ti, slot_post, lambda c: Y[:, c, lc:lc + n], lambda c: blkkeys("Y", c, lc, n), True))
                return out

            def run(ths):
                for t in ths:
                    t()

            np_ = len(passes)
            run(pre_thunks(passes[0]))
            for p_i in range(np_):
                if p_i == 0:
                    run(gateup_thunks(passes[0]))
                if p_i + 1 < np_:
                    interleave(down_thunks(passes[p_i]), pre_thunks(passes[p_i + 1]))
                    interleave(gateup_thunks(passes[p_i + 1]), post_thunks(passes[p_i]))
                else:
                    run(down_thunks(passes[p_i]))
                    run(post_thunks(passes[p_i]))
            AR.off = m

        def rope_stage1(ps, psk, n):
            kb, kbk = misc["kb"].next()
            S.op("act", lambda e: e.activation(out=kb[:, :n], in_=ps[:, :n], func=AF.Copy), reads=[psk], writes=[kbk])
            return kb, kbk

        def rope_stage2(ps, psk, kb, kbk, c0t, n, dst_fn, dkey):
            CT_, ST_ = tabs["CT"], tabs["ST"]
            p2, p2k = PS.next()
            S.op("pe", lambda e: e.matmul(p2[:, :n], perm_sb, kb[:, :n], start=True, stop=True),
                 reads=[kbk, "perm"], writes=[p2k])
            t1, t1k = misc["tmp"].next()
            t2, t2k = misc["tmp"].next()
            S.op("dve", lambda e: e.tensor_tensor(out=t1[:, :n], in0=ps[:, :n], in1=CT_[:, c0t:c0t + n], op=ALU.mult),
                 reads=[psk, "tabc"], writes=[t1k])
            S.op("dve", lambda e: e.tensor_tensor(out=t2[:, :n], in0=p2[:, :n], in1=ST_[:, c0t:c0t + n], op=ALU.mult),
                 reads=[p2k, "tabs"], writes=[t2k])
            S.op("dve", lambda e: dst_fn(e, t1[:, :n], t2[:, :n]), reads=[t1k, t2k], writes=[dkey])

        def class_views(dst3, g, tt, full):
            d = GROUPS[g][1]
            base = dst3[:, 512 * tt:512 * tt + 512] if full else dst3
            if d == 1:
                return base, None
            if d == 4:
                return base.rearrange("p (r i) -> p r i", r=4), 4
            if full:
                return dst3.rearrange("p (r i) -> p r i", r=16)[:, :, 32 * tt:32 * tt + 32], 16
            return dst3.rearrange("p (r i) -> p r i", r=16), 16

        def proj_rope(xn_tile, xn_keys, wres, wkey, tt, dst_all, dname, full):
            pend = None
            for c in range(12):
                g = c // 4
                ps, psk = PS.next()

                def mm(e, c=c, ps=ps):
                    for k in range(DC):
                        ins = e.matmul(ps[:, :512], wres[:, k, c * 128:(c + 1) * 128], xn_tile[:, k, :],
                                       start=(k == 0), stop=(k == DC - 1))
                    return ins

                S.op("pe", mm, reads=[wkey] + xn_keys, writes=[psk])
                kb, kbk = rope_stage1(ps, psk, 512)
                ov, r = class_views(dst_all[:, c, :], g, tt, full)

                def fin(e, a, b, ov=ov, r=r):
                    if r is not None:
                        a = a.rearrange("p (i r) -> p r i", r=r)
                        b = b.rearrange("p (i r) -> p r i", r=r)
                    return e.tensor_tensor(out=ov, in0=a, in1=b, op=ALU.add)

                if pend is not None:
                    rope_stage2(*pend)
                pend = (ps, psk, kb, kbk, 512 * tt, 512, fin, (dname, c, tt if full else 0))
            rope_stage2(*pend)

        def layer0_pass(xsrc, possrc, corrsrc, prev):
            S.fence()
            for c in range(DC):
                S.op("sp", lambda e, c=c: e.dma_start(out=H[:, c, :], in_=xsrc[:, c, :]),
                     writes=[Hk(c, t) for t in range(5)], dma="ldx")
            if A_:
                if upto >= 1:
                    ffn(0, 0, 1, [[0, 1, 2], [3, 4]])

            if A_ and upto >= 2:
                S.fence()
                m = AR.off
                alloc_common()
                win_sb = AR.alloc((DC, D), BF16)
                wgr_sb = AR.alloc((4, 2, 256), BF16)
                wout_sb = AR.alloc((DC, D), BF16)
                corr_sb = AR.alloc((4, 128))
                load_dense(win_sb, w_in, "w_in", 2)
                S.op("pool", lambda e: e.dma_start(out=wgr_sb, in_=w_grp), writes=["w_grp"], dma="wdense")
                load_dense(wout_sb, w_out, "w_out", 2)
                S.op("sp", lambda e: e.dma_start(out=corr_sb, in_=corrsrc), writes=["corr"], dma="ldm")
                hm = AR.alloc((DC, 512), BF16)
                U = [AR.alloc((DC, 528)) for _ in range(2)]
                Ta = AR.alloc(528)
                Tb = AR.alloc(528)
                pTb = [AR.alloc((DC, 512), BF16) for _ in range(2)]
                yT = AR.alloc((DC, 512), BF16)
                Ym = AR.alloc((DC, 512))

                def stage_A(ti):
                    n = TILES[ti][1]
                    prenorm(ti, 2, lambda c: hm[:, c, :n], lambda c: ("hm", c))

                def stage_B(ti):
                    c0, n = TILES[ti]
                    cur = U[ti % 2]
                    nxt = U[(ti + 1) % 2]
                    ck = "U%d" % (ti % 2)
                    nk = "U%d" % ((ti + 1) % 2)
                    pT = pTb[ti % 2]
                    pk = "pT%d" % (ti % 2)
                    for c in range(DC):
                        ps, psk = PS.next()

                        def mm(e, c=c, ps=ps):
                            for k in range(DC):
                                ins = e.matmul(ps[:, :n], win_sb[:, k, c * 128:(c + 1) * 128], hm[:, k, :n],
                                               start=(k == 0), stop=(k == DC - 1))
                            return ins

                        S.op("pe", mm, reads=["w_in"] + [("hm", k) for k in range(DC)], writes=[psk])
                        if ti == 0:
                            S.op("act", lambda e, c=c, ps=ps: e.activation(out=nxt[:, c, 0:16],
                                                                           in_=ps[:, HALO - 16:HALO], func=AF.Copy),
                                 reads=[psk], writes=[(nk, c)])
                            continue
                        S.op("act", lambda e, c=c, ps=ps: e.activation(out=cur[:, c, 16:528], in_=ps[:, :512],
                                                                       func=AF.Copy),
                             reads=[psk], writes=[(ck, c)])
                        if ti < 4:
                            S.op("act", lambda e, c=c: e.activation(out=nxt[:, c, 0:16], in_=cur[:, c, 512:528],
                                                                    func=AF.Copy),
                                 reads=[(ck, c)], writes=[(nk, c)])
                        gi = c // 2
                        w = 2 << gi
                        Uc = cur[:, c, :]
                        src = Uc
                        srck = (ck, c)
                        sh = 1
                        bufs = [(Ta, "Ta"), (Tb, "Tb")]
                        bi = 0
                        while sh < w:
                            dstb, dk = bufs[bi]
                            S.op("dve", lambda e, src=src, dstb=dstb, sh=sh: e.tensor_tensor(
                                out=dstb[:, sh:528], in0=src[:, sh:528], in1=src[:, 0:528 - sh], op=ALU.add),
                                 reads=[srck], writes=[dk])
                            src, srck = dstb, dk
                            bi ^= 1
                            sh *= 2
                        if ti == 1:
                            S.op("dve", lambda e, src=src, gi=gi: e.tensor_tensor(out=src[:, 16:144], in0=src[:, 16:144],
                                                                                  in1=corr_sb[:, gi, :], op=ALU.mult),
                                 reads=[srck, "corr"], writes=[srck])
                        S.op("dve", lambda e, src=src, w=w, Uc=Uc, c=c: e.scalar_tensor_tensor(
                            out=pT[:, c, :], in0=src[:, 16:528], scalar=1.0 / w, in1=Uc[:, 16:528], op0=ALU.mult,
                            op1=ALU.subtract), reads=[srck, (ck, c)], writes=[(pk, c)])

                def stage_C(ti):
                    pT = pTb[ti % 2]
                    pk = "pT%d" % (ti % 2)
                    for dc in range(DC):
                        gi = dc // 2
                        ps, psk = PS.next()

                        def mm(e, dc=dc, gi=gi, ps=ps):
                            for cc in range(2):
                                ins = e.matmul(ps[:, :512], wgr_sb[:, gi, cc, (dc % 2) * 128:(dc % 2) * 128 + 128],
                                               pT[:, 2 * gi + cc, :], start=(cc == 0), stop=(cc == 1))
                            return ins

                        S.op("pe", mm, reads=["w_grp", (pk, 2 * gi), (pk, 2 * gi + 1)], writes=[psk])
                        S.op("act", lambda e, dc=dc, ps=ps: e.activation(out=yT[:, dc, :], in_=ps[:, :512], func=AF.Copy,
                                                                         scale=G[:, 56 + dc:57 + dc]),
                             reads=[psk, "G"], writes=[("yT", dc)])
                    for dc in range(DC):
                        ps, psk = PS.next()

                        def mm(e, dc=dc, ps=ps):
                            for k in range(DC):
                                ins = e.matmul(ps[:, :512], wout_sb[:, k, dc * 128:(dc + 1) * 128], yT[:, k, :],
                                               start=(k == 0), stop=(k == DC - 1))
                            return ins

                        S.op("pe", mm, reads=["w_out"] + [("yT", k) for k in range(DC)], writes=[psk])
                        S.op("act", lambda e, dc=dc, ps=ps: e.activation(out=Ym[:, dc, :], in_=ps[:, :512], func=AF.Copy),
                             reads=[psk], writes=[("Ym", dc)])
                    postnorm_add(ti, 3, lambda c: Ym[:, c, :], lambda c: ("Ym", c), False)

                stage_A(0)
                stage_B(0)
                stage_A(1)
                stage_B(1)
                for ti in range(1, 5):
                    if ti < 4:
                        stage_A(ti + 1)
                        stage_B(ti + 1)
                    stage_C(ti)
                AR.off = m

            if A_ and upto >= 3:
                ffn(1, 4, 5, [[1, 2], [3, 4]])

            if upto < 5 and not prev:
                S.fence()
                for c in range(DC):
                    S.op("sp", lambda e, c=c: e.dma_start(out=hT_out[:, c, :], in_=H[:, c, HALO:]),
                         reads=[Hk(c, t) for t in range(5)], dma="stH")
            if A_ and upto >= 4:
                S.fence()
                m = AR.off
                alloc_common()
                misc["kb"] = Pool("kb", [AR.alloc(512, BF16) for _ in range(2)])
                rope_tables(tab_prev if prev else tab_own)
                wk_sb = AR.alloc((DC, QKV), BF16)
                wv_sb = AR.alloc((DC, QKV), BF16)
                load_dense(wk_sb, w_k, "w_k", 2)
                load_dense(wv_sb, w_v, "w_v", 2)
                Kall = AR.alloc((12, 512), BF16)
                hkb = [AR.alloc((DC, 512), BF16) for _ in range(2)]
                hk16 = AR.alloc((DC, 512), BF16)
                vsb = Pool("vsb", [AR.alloc(512, BF16) for _ in range(3)])
                prenorm(1, 6, lambda c: hkb[0][:, c, :], lambda c: ("hk0", c))
                for tt in range(4 if upto >= 4.1 else 0):
                    ti = tt + 1
                    hk = hkb[tt % 2]
                    hkn = "hk%d" % (tt % 2)
                    if tt < 3:
                        prenorm(ti + 1, 6, lambda c, tt=tt: hkb[(tt + 1) % 2][:, c, :],
                                lambda c, tt=tt: ("hk%d" % ((tt + 1) % 2), c))
                    hkk = [(hkn, c) for c in range(DC)]
                    for c in range(DC if upto >= 4.3 else 0):
                        S.op("act", lambda e, c=c, hk=hk: e.activation(
                            out=hk16[:, c, :].rearrange("p (r i) -> p r i", r=16),
                            in_=hk[:, c, :].rearrange("p (i r) -> p r i", r=16), func=AF.Copy),
                             reads=[(hkn, c)], writes=[("hk16", c)])
                    proj_rope(hk, hkk, wk_sb, "w_k", tt, Kall, "Kall", False)
                    for g in range(3):
                        R = GROUPS[g][1]
                        kd = kin[g].rearrange("j p x -> p j x")
                        ksrc = Kall[:, 4 * g:4 * g + 4, :]
                        rk = [("Kall", c, 0) for c in range(4 * g, 4 * g + 4)]
                        if g < 2:
                            if prev and tt < 3:
                                continue
                            if prev and g == 0:
                                S.op("sp", lambda e, kd=kd, ksrc=ksrc: e.dma_start(out=kd[:, :, 0:128], in_=ksrc[:, :, 384:512]),
                                     reads=rk, dma="stK%d" % g)
                            elif prev:
                                S.op("sp", lambda e, kd=kd, ksrc=ksrc: e.dma_start(out=kd[:, :, 0:512], in_=ksrc), reads=rk, dma="stK%d" % g)
                            else:
                                o = R * 128 + 512 * tt
                                S.op("sp", lambda e, kd=kd, ksrc=ksrc, o=o: e.dma_start(out=kd[:, :, o:o + 512], in_=ksrc),
                                     reads=rk, dma="stK%d" % g)
                        else:
                            o = 0 if prev else 2048
                            for jj in range(4):
                                S.op("sp", lambda e, jj=jj, o=o, tt=tt: e.dma_start(
                                    out=kin[2][jj][:, o:o + 2048].rearrange("p (r i) -> p r i", r=16)[:, :, 32 * tt:32 * tt + 32],
                                    in_=Kall[:, 8 + jj, :].rearrange("p (r i) -> p r i", r=16)),
                                     reads=[("Kall", 8 + jj, 0)], dma="stK2%d" % jj)
                    for g in range(3 if upto >= 4.3 else 0):
                        d = GROUPS[g][1]
                        for b in range(4):
                            R = d
                            vd = vin[g].rearrange("j p b f -> p j b f")
                            if d == 1:
                                cols = lambda k, b=b, hk=hk: hk[:, k, 128 * b:128 * b + 128]
                                cls = [4 * tt + b]
                            elif d == 4:
                                cols = lambda k, b=b, hk=hk: hk[:, k, :].rearrange("p (i r) -> p r i", r=4)[:, b, :]
                                cls = [4 * tt + b]
                            else:
                                cols = lambda k, b=b: hk16[:, k, 128 * b:128 * b + 128]
                                cls = [4 * b + rr for rr in range(4)]
                            blks = [(k_ - (16 - R)) if prev else (R + k_) for k_ in cls]
                            if blks[0] < 0:
                                continue
                            ps, psk = PS.next()

                            def mm(e, cols=cols, ps=ps, g=g):
                                for k in range(DC):
                                    ins = e.matmul(ps[:, :512], cols(k), wv_sb[:, k, g * 512:(g + 1) * 512],
                                                   start=(k == 0), stop=(k == DC - 1))
                                return ins

                            S.op("pe", mm, reads=["w_v"] + hkk + [("hk16", c) for c in range(DC)], writes=[psk])
                            vb, vbk = vsb.next()
                            S.op("act", lambda e, vb=vb, ps=ps: e.activation(out=vb, in_=ps[:, :512], func=AF.Copy),
                                 reads=[psk], writes=[vbk])
                            if d == 16:
                                for rr in range(4):
                                    S.op("sp", lambda e, vb=vb, vd=vd, rr=rr, tt=tt, blk=blks[rr]: e.dma_start(
                                        out=vd[32 * tt:32 * tt + 32, :, blk, :],
                                        in_=vb[32 * rr:32 * rr + 32, :].rearrange("p (j f) -> p j f", j=4)), reads=[vbk], dma="stV%d" % vbk[1])
                            else:
                                S.op("sp", lambda e, vb=vb, vd=vd, blk=blks[0]: e.dma_start(
                                    out=vd[:, :, blk, :], in_=vb.rearrange("p (j f) -> p j f", j=4)), reads=[vbk], dma="stV%d" % vbk[1])
                AR.off = m

        layer0_pass(xT_prev, pos_prev, corr_prev, True)
        layer0_pass(xT_own, pos_own, corr_own, False)

        if upto >= 5:
            ffn(2, 8, 9, [[1, 2], [3, 4]])

        if upto >= 5.15:
            S.fence()
            m = AR.off
            alloc_common()
            misc["kb"] = Pool("kb", [AR.alloc(512, BF16) for _ in range(2)])
            Qall = AR.alloc((12, NT), BF16)
            m2 = AR.off
            rope_tables(tab_own)
            wq_sb = AR.alloc((DC, QKV), BF16)
            load_dense(wq_sb, w_q, "w_q", 2)
            hmqb = [AR.alloc((DC, 512), BF16) for _ in range(2)]
            prenorm(1, 10, lambda c: hmqb[0][:, c, :], lambda c: ("hmq0", c))
            for tt in range(4):
                ti = tt + 1
                if tt < 3:
                    prenorm(ti + 1, 10, lambda c, tt=tt: hmqb[(tt + 1) % 2][:, c, :],
                            lambda c, tt=tt: ("hmq%d" % ((tt + 1) % 2), c))
                proj_rope(hmqb[tt % 2], [("hmq%d" % (tt % 2), c) for c in range(DC)], wq_sb, "w_q", tt, Qall, "Qall",
                          True)
            S.fence()
            AR.off = m2
            mk = AR.alloc((2, 512), BF16)
            S.op("sp", lambda e: e.dma_start(out=mk, in_=masks), writes=["mk"], dma="ldm")
            S.op("sp", lambda e: e.dma_start(out=ident, in_=masks_id), writes=["ident"], dma="ldm")
            wo_sb = AR.alloc((4, D), BF16)
            load_dense(wo_sb, w_o, "w_o", 1)
            OT = AR.alloc((4, NT), BF16)
            m3 = AR.off
            ND = AR.alloc((2, NT))
            kbuf = [AR.alloc(32 * 128, BF16) for _ in range(2)]
            vbuf = [AR.alloc((32, 128), BF16) for _ in range(2)]
            ptp = Pool("pt", [AR.alloc(512, BF16) for _ in range(3)])
            Qk = [("Qall", c, tt) for c in range(12) for tt in range(4)]
            slot_ctr = [0]

            def begin_group(jj, g):
                R = GROUPS[g][1]
                nb = R + 16
                s_ = slot_ctr[0] % 2
                slot_ctr[0] += 1
                S.op("sp", lambda e: e.dma_start(out=kbuf[s_][:, :nb * 128], in_=kin[g][jj]),
                     writes=[("kbuf", s_)], dma="kb%d" % s_)
                S.op("sp", lambda e: e.dma_start(out=vbuf[s_][:, :nb, :], in_=vin[g][jj]),
                     writes=[("vbuf", s_)], dma="vb%d" % s_)
                return dict(jj=jj, g=g, R=R, s=s_, Qc=Qall[:, 4 * g + jj, :])

            def S_stage(ctx, k):
                jj, g, R, s_, Qc = ctx["jj"], ctx["g"], ctx["R"], ctx["s"], ctx["Qc"]
                first = k < R
                pt, ptk = ptp.next()
                for hh in range(2):
                    pss, pssk = PS.next()

                    def mms(e, pss=pss, hh=hh):
                        e.matmul(pss[:, :256], ident, mk[:, 1 if first else 0, 0:256], start=True, stop=False)
                        q = Qc[:, 128 * k:128 * k + 128]
                        lo = 64 * hh
                        for w_, kb_ in enumerate((k, k + R)):
                            ins = e.matmul(pss[:, w_ * 128:(w_ + 1) * 128],
                                           kbuf[s_][lo:lo + 64, kb_ * 128:(kb_ + 1) * 128], q[lo:lo + 64, :],
                                           start=False, stop=(w_ == 1))
                        return ins

                    S.op("pe", mms, reads=[("kbuf", s_), "mk", "ident"] + [("Qall", 4 * g + jj, tt) for tt in
                                                                            range(4)], writes=[pssk])
                    S.op("act", lambda e, pss=pss, hh=hh: e.activation(
                        out=pt[:, 256 * hh:256 * hh + 256], in_=pss[:, :256], func=AF.Exp, scale=0.125),
                         reads=[pssk], writes=[(ptk, hh)])
                return pt, ptk

            def PV_stage(ctx, k, pt, ptk):
                g, R, s_ = ctx["g"], ctx["R"], ctx["s"]
                pso, psok = PS.next()

                def mmo(e):
                    for hh in range(2):
                        lo = 64 * hh
                        for w_, kb_ in enumerate((k, k + R)):
                            e.matmul(pso[lo:lo + 64, 0:128], vbuf[s_][:, kb_, lo:lo + 64],
                                     pt[:, (2 * hh + w_) * 128:(2 * hh + w_ + 1) * 128], start=(w_ == 0),
                                     stop=(w_ == 1))
                        for w_ in range(2):
                            ins = e.matmul(pso[lo:lo + 64, 128:256], ones1[:, 0:64],
                                           pt[:, (2 * hh + w_) * 128:(2 * hh + w_ + 1) * 128], start=(w_ == 0),
                                           stop=(w_ == 1))
                    return ins

                S.op("pe", mmo, reads=[("vbuf", s_), (ptk, 0), (ptk, 1), "ones1"], writes=[psok])
                if R == 1:
                    ndv = ND[:, :, 128 * k:128 * k + 128]
                elif R == 4:
                    n_, r_ = k // 4, k % 4
                    ndv = ND[:, :, 512 * n_:512 * n_ + 512].rearrange("p x (i r) -> p x r i", r=4)[:, :, r_, :]
                else:
                    ndv = ND.rearrange("p x (i r) -> p x r i", r=16)[:, :, k, :]
                psv = pso[:, 0:256].rearrange("p (x i) -> p x i", x=2)
                if g == 0:
                    S.op("dve", lambda e: e.tensor_copy(out=ndv, in_=psv), reads=[psok], writes=["NDall"])
                else:
                    S.op("dve", lambda e: e.tensor_tensor(out=ndv, in0=psv, in1=ndv, op=ALU.add),
                         reads=[psok, "NDall"], writes=["NDall"])

            def end_jj(jj):
                for tt in range(4):
                    S.op("dve", lambda e, tt=tt: e.reciprocal(out=ND[:, 1, 512 * tt:512 * tt + 512],
                                                              in_=ND[:, 1, 512 * tt:512 * tt + 512]),
                         reads=["NDall"], writes=["NDall"])
                    S.op("dve", lambda e, tt=tt: e.tensor_tensor(out=OT[:, jj, 512 * tt:512 * tt + 512],
                                                                 in0=ND[:, 0, 512 * tt:512 * tt + 512],
                                                                 in1=ND[:, 1, 512 * tt:512 * tt + 512],
                                                                 op=ALU.mult),
                         reads=["NDall"], writes=[("OT", jj, tt)])

            steps = [(jj, g, k) for jj in range(4) for g in range(3) for k in range(16)]
            ctxs = {}

            def get_ctx(jj, g):
                if (jj, g) not in ctxs:
                    ctxs[(jj, g)] = begin_group(jj, g)
                return ctxs[(jj, g)]

            cur = S_stage(get_ctx(0, 0), 0)
            for i, (jj, g, k) in enumerate(steps):
                nxt = None
                if i + 1 < len(steps):
                    j2, g2, k2 = steps[i + 1]
                    nxt = S_stage(get_ctx(j2, g2), k2)
                PV_stage(get_ctx(jj, g), k, *cur)
                if g == 2 and k == 15:
                    end_jj(jj)
                cur = nxt
            S.fence()
            AR.off = m3
            Ym = AR.alloc((DC, 512))
            for tt in range(4):
                ti = tt + 1
                for dc in range(DC):
                    ps, psk = PS.next()

                    def mm(e, dc=dc, ps=ps, tt=tt):
                        for k in range(4):
                            ins = e.matmul(ps[:, :512], wo_sb[:, k, dc * 128:(dc + 1) * 128],
                                           OT[:, k, 512 * tt:512 * tt + 512], start=(k == 0), stop=(k == 3))
                        return ins

                    S.op("pe", mm, reads=["w_o"] + [("OT", k, tt) for k in range(4)], writes=[psk])
                    S.op("act", lambda e, dc=dc, ps=ps: e.activation(out=Ym[:, dc, :], in_=ps[:, :512], func=AF.Copy),
                         reads=[psk], writes=[("Ym", dc)])
                postnorm_add(ti, 11, lambda c: Ym[:, c, :], lambda c: ("Ym", c), False)
            AR.off = m

        if upto >= 5.25:
            ffn(3, 12, 13, [[1, 2], [3, 4]])
        if upto >= 5:
            S.fence()
            for c in range(DC):
                S.op("sp", lambda e, c=c: e.dma_start(out=hT_out[:, c, :], in_=H[:, c, HALO:]),
                     reads=[Hk(c, t) for t in range(5)], dma="stH")

        S.fence()
        final_waits = dict(S.fence_tok)

        sems = {}
        for e in ENGS:
            sems[("e", e)] = es.enter_context(nc.semaphore("sem_" + e))
        for k in S.dcnt:
            sems[("d", k)] = es.enter_context(nc.semaphore("dsem_" + k))
        engmap = {"pe": "tensor", "act": "scalar", "dve": "vector", "pool": "gpsimd", "sp": "sync"}
        with nc.Block() as block:
            def make(eng):
                def body(e):
                    for waits, fn, tok in S.q[eng]:
                        for sk, v in waits:
                            e.wait_ge(sems[sk], v)
                        ins = fn(e)
                        ins.then_inc(sems[tok[0]], 16 if tok[0][0] == "d" else 1)
                    if eng == "sp":
                        for sk, v in final_waits.items():
                            if v > 0:
                                e.wait_ge(sems[sk], v)
                return body

            for eng in ENGS:
                getattr(block, engmap[eng])(make(eng))
    return nc


def _fm(a):
    T, F = a.shape
    return np.ascontiguousarray(a.T.reshape(F // 128, 128, T).transpose(1, 0, 2))


def _dense(w):
    k, n = w.shape
    return np.ascontiguousarray(w.reshape(k // 128, 128, n).transpose(1, 0, 2))


def _ffn_w(gate, up, down):
    gate = gate.reshape(4, DC, 128, FC, 128)
    up = up.reshape(4, DC, 128, FC, 128)
    down = down.reshape(4, FC, 128, DC, 128)
    wg = np.ascontiguousarray(gate.transpose(0, 3, 2, 1, 4)).reshape(4, FC, 128, DC * 128)
    wu = np.ascontiguousarray(up.transpose(0, 3, 2, 1, 4)).reshape(4, FC, 128, DC * 128)
    wd = np.ascontiguousarray(down.transpose(0, 3, 2, 1, 4)).reshape(4, DC, 128, FC * 128)
    return wg, wu, wd


def _vecs(norm_gain, kv_gain, pool_scale):
    v = np.zeros((128, 14 * 8), np.float32)
    for i in range(6):
        v[:, i * 8:(i + 1) * 8] = norm_gain[0, i].reshape(8, 128).T
        v[:, (8 + i) * 8:(9 + i) * 8] = norm_gain[1, i].reshape(8, 128).T
    v[:, 48:56] = kv_gain.reshape(8, 128).T
    v[:, 56:64] = pool_scale.reshape(8, 128).T
    return v


_NC_CACHE = {}


def _get_nc():
    if "F" not in _NC_CACHE:
        _NC_CACHE["F"] = build()
    return _NC_CACHE["F"]


def _corr(is_seq_start):
    corr = np.ones((128, 4, 128), np.float32)
    if is_seq_start:
        t = np.arange(128)
        for g, w in enumerate((2, 4, 8, 16)):
            corr[:, g, :] = (w / np.minimum(t + 1, w)).astype(np.float32)[None, :]
    return corr


def make_in_maps(x, positions, norm_gain, ffn_w_gate, ffn_w_up, ffn_w_down, pool_w_in, pool_w_group, pool_scale,
                 pool_w_out, kv_norm_gain, w_k, w_v, attn_w_q, attn_w_o):
    x = np.asarray(x, np.float32)
    positions = np.asarray(positions, np.int32)
    bf = ml_dtypes.bfloat16
    p = np.arange(128)
    inv_freq = (np.float32(10000.0) ** (-(np.arange(0, 64, 2, dtype=np.float32)) / np.float32(64))).astype(np.float32)
    rconst = np.stack([inv_freq[p % 32], np.where((p % 64) < 32, -1.0, 1.0)], axis=1).astype(np.float32)
    partner = np.where((p % 64) < 32, p + 32, p - 32)
    permM = np.zeros((128, 128), np.float32)
    permM[partner, p] = 1.0
    permM = permM.astype(bf)
    wg, wu, wd = _ffn_w(np.asarray(ffn_w_gate, np.float32), np.asarray(ffn_w_up, np.float32),
                        np.asarray(ffn_w_down, np.float32))
    vecs = _vecs(np.asarray(norm_gain, np.float32), np.asarray(kv_norm_gain, np.float32),
                 np.asarray(pool_scale[0], np.float32))
    common = dict(
        wg=wg, wu=wu, wd=wd, vecs=vecs, rconst=rconst, permM=permM,
        w_in=_dense(np.asarray(pool_w_in[0], np.float32)), w_out=_dense(np.asarray(pool_w_out[0], np.float32)),
        w_grp=np.ascontiguousarray(np.asarray(pool_w_group[0], np.float32).reshape(4, 2, 128, 256).transpose(2, 0, 1, 3)),
        w_k=_dense(np.asarray(w_k, np.float32)), w_v=_dense(np.asarray(w_v, np.float32)),
        w_q=_dense(np.asarray(attn_w_q[0], np.float32)), w_o=_dense(np.asarray(attn_w_o[0], np.float32)),
        masks_id=np.eye(128, dtype=np.float32).astype(bf))
    kk = np.arange(128)[:, None]
    qq = np.arange(128)[None, :]
    mprev = np.where(kk >= qq, 0.0, NEG).astype(np.float32)
    mcur = np.where(kk <= qq, 0.0, NEG).astype(np.float32)
    mall = np.full((128, 128), NEG, np.float32)
    MN = np.concatenate([mprev, mcur, mprev, mcur], axis=1)
    MF0 = np.concatenate([mall, mcur, mall, mcur], axis=1)
    in_maps = []
    for c in range(8):
        b, q = divmod(c, 4)
        s0 = q * NT

        def xslice(start):
            xs = np.zeros((HC, D), np.float32)
            lo = start - HALO
            if start >= 0:
                a = max(lo, 0)
                xs[a - lo:] = x[b, a:start + NT]
            return _fm(xs)

        pp = s0 - NT
        pos_prev = positions[b:b + 1, pp:pp + NT] if q > 0 else positions[b:b + 1, 0:NT]
        d = dict(common)
        d.update(xT_prev=xslice(s0 - NT if q > 0 else -10 ** 9), xT_own=xslice(s0),
                 pos_prev=np.ascontiguousarray(pos_prev), pos_own=np.ascontiguousarray(positions[b:b + 1, s0:s0 + NT]),
                 corr_prev=_corr(q == 1), corr_own=_corr(q == 0),
                 masks=np.stack([MN, MF0 if q == 0 else MN], axis=1).astype(bf))
        in_maps.append(d)
    return in_maps


def kernel(x, positions, norm_gain, ffn_w_gate, ffn_w_up, ffn_w_down, pool_w_in, pool_w_group, pool_scale,
           pool_w_out, kv_norm_gain, w_k, w_v, attn_w_q, attn_w_o):
    in_maps = make_in_maps(x, positions, norm_gain, ffn_w_gate, ffn_w_up, ffn_w_down, pool_w_in, pool_w_group,
                           pool_scale, pool_w_out, kv_norm_gain, w_k, w_v, attn_w_q, attn_w_o)
    res = run_bass_kernel_spmd(_get_nc(), in_maps, core_ids=list(range(8))).results
    out = np.zeros((2, 8192, D), np.float32)
    for c in range(8):
        b, q = divmod(c, 4)
        hT = np.asarray(res[c]["hT"])
        out[b, q * NT:(q + 1) * NT] = hT.transpose(2, 1, 0).reshape(NT, D)
    return out
```

```python
import contextlib
import os
KSTEP = int(os.environ.get('KSTEP', '9'))
import numpy as np
import ml_dtypes
import concourse.bass as bass
import concourse.mybir as mybir
from concourse.bass_utils import run_bass_kernel_spmd

F32 = mybir.dt.float32
BF16 = mybir.dt.bfloat16
I32 = mybir.dt.int32
AF = mybir.ActivationFunctionType
ALU = mybir.AluOpType

D = 1024
DFF = 2816
NT = 2048
HALO = 128
HC = HALO + NT
DC = 8
FC = 22
QKV = 1536
EPS = 1e-6
NEG = -30000.0
NWG = 2
GROUPS = ((128, 1), (512, 4), (2048, 16))
ENGS = ("pe", "act", "dve", "pool", "sp")
SAME_ENGINE_FREE = ("pe", "sp")
MAGIC = 12582912.0
TWO_PI = 6.283185307179586
C1 = 6.28125
C2 = TWO_PI - C1

TILES = [(0, HALO)] + [(HALO + 512 * i, 512) for i in range(4)]


class Sched:
    def __init__(self):
        self.q = {e: [] for e in ENGS}
        self.cnt = {e: 0 for e in ENGS}
        self.dcnt = {}
        self.seen = {e: {} for e in ENGS}
        self.last_w = {}
        self.readers = {}
        self.fence_tok = None
        self.fence_done = set(ENGS)

    def fence(self):
        tok = {("e", e): self.cnt[e] for e in ENGS}
        for k, v in self.dcnt.items():
            tok[("d", k)] = v
        self.fence_tok = tok
        self.fence_done = set()

    def op(self, eng, fn, reads=(), writes=(), dma=None):
        waits = {}
        writes = list(writes) + [k for k in reads if isinstance(k, tuple) and k[0] == "ps" and k not in writes]

        def need(dep):
            if dep is None:
                return
            sk, val = dep
            if sk == ("e", eng) and eng in SAME_ENGINE_FREE:
                return
            if sk[0] == "d":
                val = self.dcnt[sk[1]]
            if val <= 0 or self.seen[eng].get(sk, 0) >= val:
                return
            if waits.get(sk, 0) < val:
                waits[sk] = val

        if eng not in self.fence_done:
            for sk, v in self.fence_tok.items():
                need((sk, v))
            self.fence_done.add(eng)
        for k in reads:
            need(self.last_w.get(k))
        for k in writes:
            need(self.last_w.get(k))
            for r in self.readers.get(k, ()):
                need(r)
        for sk, v in waits.items():
            self.seen[eng][sk] = v
        if dma is None:
            self.cnt[eng] += 1
            tok = (("e", eng), self.cnt[eng])
        else:
            self.dcnt[dma] = self.dcnt.get(dma, 0) + 16
            tok = (("d", dma), self.dcnt[dma])
        for k in writes:
            self.last_w[k] = tok
            self.readers[k] = []
        for k in reads:
            self.readers.setdefault(k, []).append(tok)
        self.q[eng].append((sorted(waits.items(), key=str), fn, tok))


class Arena:
    def __init__(self, ap, nwords):
        self.ap = ap
        self.n = nwords
        self.off = 0

    def alloc(self, free, dtype=F32):
        if isinstance(free, int):
            free = (free,)
        n = int(np.prod(free))
        words = n if dtype != BF16 else (n + 1) // 2
        words = (words + 7) // 8 * 8
        assert self.off + words <= self.n, ("arena overflow", self.off, words, self.n)
        v = self.ap[:, self.off:self.off + words]
        self.off += words
        if dtype == BF16:
            v = v.bitcast(BF16)
        elif dtype == I32:
            v = v.bitcast(I32)
        v = v[:, 0:n]
        if len(free) == 2:
            v = v.rearrange("p (a b) -> p a b", a=free[0])
        elif len(free) == 3:
            v = v.rearrange("p (a b c) -> p a b c", a=free[0], b=free[1])
        return v


class Pool:
    def __init__(self, name, bufs):
        self.name = name
        self.bufs = bufs
        self.i = -1

    def next(self):
        self.i = (self.i + 1) % len(self.bufs)
        return self.bufs[self.i], (self.name, self.i)


def build(upto=99, dbg=False):
    nc = bass.Bass("TRN2", target_bir_lowering=False)
    S = Sched()
    A_ = True

    def din(name, shape, dt=F32):
        return nc.dram_tensor(name, list(shape), dt, kind="ExternalInput").ap()

    def dout(name, shape, dt=F32):
        return nc.dram_tensor(name, list(shape), dt, kind="ExternalOutput").ap()

    xT_prev = din("xT_prev", [128, DC, HC])
    xT_own = din("xT_own", [128, DC, HC])
    wg = din("wg", [4, FC, 128, DC * 128])
    wu = din("wu", [4, FC, 128, DC * 128])
    wd = din("wd", [4, DC, 128, FC * 128])
    NV = 14 * 8
    vecs = din("vecs", [128, NV])
    pos_prev = din("pos_prev", [1, NT], I32)
    pos_own = din("pos_own", [1, NT], I32)
    rconst = din("rconst", [128, 2])
    permM = din("permM", [128, 128], BF16)
    w_in = din("w_in", [128, DC, D])
    w_grp = din("w_grp", [128, 4, 2, 256])
    w_out = din("w_out", [128, DC, D])
    w_k = din("w_k", [128, DC, QKV])
    w_v = din("w_v", [128, DC, QKV])
    corr_prev = din("corr_prev", [128, 4, 128])
    corr_own = din("corr_own", [128, 4, 128])
    w_q = din("w_q", [128, DC, QKV])
    w_o = din("w_o", [128, 4, D])
    masks = din("masks", [128, 2, 512], BF16)
    masks_id = din("masks_id", [128, 128], BF16)
    hT_out = dout("hT", [128, DC, NT])
    skind = "ExternalOutput" if dbg else "Internal"
    kin = [nc.dram_tensor("kin%d" % g, [4, 128, (GROUPS[g][1] + 16) * 128], BF16, kind=skind).ap() for g in range(3)]
    vin = [nc.dram_tensor("vin%d" % g, [4, 128, (GROUPS[g][1] + 16), 128], BF16, kind=skind).ap() for g in range(3)]
    tab_prev = nc.dram_tensor("tab_prev", [128, 2, NT], F32).ap()
    tab_own = nc.dram_tensor("tab_own", [128, 2, NT], F32).ap()

    es = contextlib.ExitStack()
    with es:
        AW = 53000
        arena_t = es.enter_context(nc.sbuf_tensor("arena", [128, AW], F32))
        AR = Arena(arena_t[:], AW)
        banks = [es.enter_context(nc.psum_tensor("bank%d" % i, [128, 512], F32)) for i in range(8)]
        PS = Pool("ps", [b[:] for b in banks])

        H = AR.alloc((DC, HC))
        G = AR.alloc(NV)
        RC = AR.alloc(2)
        onesD = AR.alloc(128, BF16)
        ones4D = AR.alloc(128, BF16)
        ones1 = AR.alloc(128, BF16)
        ident = AR.alloc(128, BF16)
        perm_sb = AR.alloc(128, BF16)
        epsb = AR.alloc(2)
        tabs = {}

        def Hk(c, ti):
            return ("H", c, ti)

        for c in range(DC):
            S.op("sp", lambda e, c=c: e.dma_start(out=H[:, c, :], in_=xT_prev[:, c, :]),
                 writes=[Hk(c, t) for t in range(5)], dma="ldx")
        S.op("sp", lambda e: e.dma_start(out=G, in_=vecs), writes=["G"], dma="ldc")
        S.op("sp", lambda e: e.dma_start(out=RC, in_=rconst), writes=["RC"], dma="ldc")
        S.op("sp", lambda e: e.dma_start(out=perm_sb, in_=permM), writes=["perm"], dma="ldc")
        S.op("dve", lambda e: e.memset(onesD, 1.0 / D), writes=["onesD"])
        S.op("dve", lambda e: e.memset(ones4D, 4.0 / D), writes=["ones4D"])
        S.op("dve", lambda e: e.memset(ones1, 1.0), writes=["ones1"])
        S.op("dve", lambda e: e.memset(epsb[:, 0:1], EPS), writes=["epsb"])
        S.op("dve", lambda e: e.memset(epsb[:, 1:2], 4 * EPS), writes=["epsb"])

        def rope_precompute(pos, dst):
            m = AR.off
            CT = AR.alloc(NT)
            ST = AR.alloc(NT)
            posi = AR.alloc(NT, I32)
            ang = AR.alloc(NT)
            t1 = AR.alloc(NT)
            t2 = AR.alloc(NT)
            S.op("sp", lambda e: e.dma_start(out=posi, in_=pos.to_broadcast([128, NT])), writes=["posi"], dma="ldp")
            S.op("dve", lambda e: e.tensor_copy(out=ang, in_=posi), reads=["posi"], writes=["ang"])
            S.op("dve", lambda e: e.tensor_scalar(ang, ang, RC[:, 0:1], None, ALU.mult), reads=["ang", "RC"],
                 writes=["ang"])
            for which, dst_sb, shift in (("s", ST, 0.0), ("c", CT, np.pi / 2)):
                S.op("dve", lambda e, shift=shift: e.tensor_scalar(t1, ang, float(shift), None, ALU.add),
                     reads=["ang"], writes=["t1"])
                S.op("dve", lambda e: e.tensor_scalar(t2, t1, 1.0 / TWO_PI, MAGIC, ALU.mult, ALU.add),
                     reads=["t1"], writes=["t2"])
                S.op("dve", lambda e: e.tensor_scalar(t2, t2, MAGIC, None, ALU.subtract), reads=["t2"], writes=["t2"])
                S.op("dve", lambda e: e.scalar_tensor_tensor(out=t1, in0=t2, scalar=-C1, in1=t1, op0=ALU.mult,
                                                             op1=ALU.add), reads=["t1", "t2"], writes=["t1"])
                S.op("dve", lambda e: e.scalar_tensor_tensor(out=t1, in0=t2, scalar=-C2, in1=t1, op0=ALU.mult,
                                                             op1=ALU.add), reads=["t1", "t2"], writes=["t1"])
                S.op("dve", lambda e: e.tensor_scalar(t1, t1, 3.1415925, -3.1415925, ALU.min, ALU.max),
                     reads=["t1"], writes=["t1"])
                if which == "s":
                    S.op("act", lambda e, dst_sb=dst_sb: e.activation(out=dst_sb, in_=t1, func=AF.Sin,
                                                                      scale=RC[:, 1:2]),
                         reads=["t1", "RC"], writes=["ptab" + which])
                else:
                    S.op("act", lambda e, dst_sb=dst_sb: e.activation(out=dst_sb, in_=t1, func=AF.Sin),
                         reads=["t1"], writes=["ptab" + which])
            S.op("sp", lambda e: e.dma_start(out=dst[:, 0, :], in_=CT), reads=["ptabc"], dma="stT")
            S.op("sp", lambda e: e.dma_start(out=dst[:, 1, :], in_=ST), reads=["ptabs"], dma="stT")
            S.fence()
            AR.off = m

        def rope_tables(src):
            CT = AR.alloc(NT)
            ST = AR.alloc(NT)
            tabs["CT"], tabs["ST"] = CT, ST
            S.op("sp", lambda e: e.dma_start(out=CT, in_=src[:, 0, :]), writes=["tabc"], dma="ldt")
            S.op("sp", lambda e: e.dma_start(out=ST, in_=src[:, 1, :]), writes=["tabs"], dma="ldt")

        rope_precompute(pos_prev, tab_prev)
        rope_precompute(pos_own, tab_own)

        sq_pool = None
        misc = {}
        NTMP = [2]

        def alloc_common():
            misc["sq"] = Pool("sq", [AR.alloc(512, BF16) for _ in range(2)])
            misc["rstd"] = Pool("rstd", [AR.alloc(512) for _ in range(2)])
            misc["tmp"] = Pool("tmp", [AR.alloc(512) for _ in range(NTMP[0])])

        def aslist(k):
            return list(k) if isinstance(k, list) else [k]

        def gcol(slot, c):
            return G[:, slot * 8 + c: slot * 8 + c + 1]

        def rms_stats(src_fn, src_keys, n, onesm, eps, sq_eng="act"):
            ps, psk = PS.next()
            for c in range(DC):
                sq, sqk = misc["sq"].next()
                if sq_eng == "act":
                    S.op("act", lambda e, c=c, sq=sq: e.activation(out=sq[:, :n], in_=src_fn(c), func=AF.Square),
                         reads=aslist(src_keys(c)), writes=[sqk])
                else:
                    S.op("dve", lambda e, c=c, sq=sq: e.tensor_tensor(out=sq[:, :n], in0=src_fn(c), in1=src_fn(c),
                                                                       op=ALU.mult),
                         reads=aslist(src_keys(c)), writes=[sqk])
                S.op("pe", lambda e, c=c, sq=sq, ps=ps: e.matmul(ps[:, :n], onesm, sq[:, :n], start=(c == 0),
                                                                  stop=(c == DC - 1)),
                     reads=[sqk, "onesD", "ones4D"], writes=[psk])
            rstd, rk = misc["rstd"].next()
            S.op("act", lambda e, ps=ps, rstd=rstd: e.activation(out=rstd[:, :n], in_=ps[:, :n], func=AF.Ln,
                                                                 bias=epsb[:, 1:2] if eps > 2e-6 else epsb[:, 0:1]),
                 reads=[psk, "epsb"], writes=[rk])
            S.op("act", lambda e, rstd=rstd: e.activation(out=rstd[:, :n], in_=rstd[:, :n], func=AF.Exp, scale=-0.5),
                 reads=[rk], writes=[rk])
            return rstd, rk

        def prenorm(ti, slot, dst_fn, dst_key):
            c0, n = TILES[ti]
            rstd, rk = rms_stats(lambda c: H[:, c, c0:c0 + n], lambda c: Hk(c, ti), n, onesD, EPS)
            for c in range(DC):
                S.op("dve", lambda e, c=c: e.scalar_tensor_tensor(out=dst_fn(c), in0=H[:, c, c0:c0 + n],
                                                                  scalar=gcol(slot, c), in1=rstd[:, :n],
                                                                  op0=ALU.mult, op1=ALU.mult),
                     reads=[Hk(c, ti), rk, "G"], writes=aslist(dst_key(c)))

        def postnorm_add(ti, slot, Y_fn, Y_key, half):
            c0, n = TILES[ti]
            rstd, rk = rms_stats(Y_fn, Y_key, n, ones4D if half else onesD, 4 * EPS if half else EPS, sq_eng="act")
            pend = None

            def add(c, tmp, tk):
                S.op("dve", lambda e: e.tensor_tensor(out=H[:, c, c0:c0 + n], in0=H[:, c, c0:c0 + n],
                                                      in1=tmp[:, :n], op=ALU.add),
                     reads=[tk, Hk(c, ti)], writes=[Hk(c, ti)])

            for c in range(DC):
                tmp, tk = misc["tmp"].next()
                S.op("dve", lambda e, c=c, tmp=tmp: e.scalar_tensor_tensor(out=tmp[:, :n], in0=Y_fn(c),
                                                                           scalar=gcol(slot, c), in1=rstd[:, :n],
                                                                           op0=ALU.mult, op1=ALU.mult),
                     reads=aslist(Y_key(c)) + [rk, "G"], writes=[tk])
                if pend is not None:
                    add(*pend)
                pend = (c, tmp, tk)
            add(*pend)

        def load_dense(dst, src, key, nsplit=1):
            a = dst.shape[1]
            step = (a + nsplit - 1) // nsplit
            for i in range(0, a, step):
                S.op("pool", lambda e, i=i: e.dma_start(out=dst[:, i:i + step], in_=src[:, i:i + step]),
                     writes=[key], dma="wdense")

        def blkkeys(name, idx, lc, n):
            return [(name, idx, b) for b in range(lc // 128, (lc + n + 127) // 128)]

        def interleave(main, side):
            nm, ns = len(main), len(side)
            si = 0
            for i, t in enumerate(main):
                t()
                want = ((i + 1) * ns) // nm
                while si < want:
                    side[si]()
                    si += 1
            while si < ns:
                side[si]()
                si += 1

        def ffn(j, slot_pre, slot_post, passes):
            S.fence()
            m = AR.off
            alloc_common()
            W = HALO + 1024
            xn = AR.alloc((DC, W), BF16)
            act = AR.alloc((FC, W), BF16)
            Y = AR.alloc((DC, W))
            sgp = Pool("sg", [AR.alloc(512) for _ in range(2)])
            wgu = [(AR.alloc((DC, 128), BF16), AR.alloc((DC, 128), BF16)) for _ in range(NWG)]
            wdn = [AR.alloc((FC, 128), BF16) for _ in range(2)]
            fcount = [0, 0]

            def pre_thunks(tl):
                p0 = TILES[tl[0]][0]
                out = []
                for ti in tl:
                    c0, n = TILES[ti]
                    lc = c0 - p0
                    out.append(lambda ti=ti, lc=lc, n=n: prenorm(
                        ti, slot_pre, lambda c: xn[:, c, lc:lc + n], lambda c: blkkeys("xn", c, lc, n)))
                return out

            def gateup_thunks(tl):
                p0 = TILES[tl[0]][0]

                def one(f):
                    s = fcount[0] % NWG
                    fcount[0] += 1
                    S.op("pool", lambda e: e.dma_start(out=wgu[s][0], in_=wg[j, f].rearrange(
                        "p (k n) -> p k n", k=DC)), writes=[("wg", s)], dma="wg%d" % s)
                    S.op("pool", lambda e: e.dma_start(out=wgu[s][1], in_=wu[j, f].rearrange(
                        "p (k n) -> p k n", k=DC)), writes=[("wu", s)], dma="wu%d" % s)
                    for ti in tl:
                        c0, n = TILES[ti]
                        lc = c0 - p0
                        pg, pgk = PS.next()
                        pu, puk = PS.next()

                        def mmg(e, lc=lc, n=n, pp=pg, which=0):
                            for k in range(DC):
                                ins = e.matmul(pp[:, :n], wgu[s][which][:, k, :], xn[:, k, lc:lc + n], start=(k == 0),
                                               stop=(k == DC - 1))
                            return ins

                        xk = [k_ for c in range(DC) for k_ in blkkeys("xn", c, lc, n)]
                        S.op("pe", mmg, reads=[("wg", s)] + xk, writes=[pgk])
                        S.op("pe", lambda e, lc=lc, n=n, pu=pu, mmg=mmg: mmg(e, lc, n, pu, 1),
                             reads=[("wu", s)] + xk, writes=[puk])
                        sg, sgk = sgp.next()
                        S.op("act", lambda e, sg=sg, pg=pg, n=n: e.activation(out=sg[:, :n], in_=pg[:, :n],
                                                                               func=AF.Silu),
                             reads=[pgk], writes=[sgk])
                        S.op("dve", lambda e, sg=sg, pu=pu, n=n, lc=lc: e.tensor_tensor(
                            out=act[:, f, lc:lc + n], in0=sg[:, :n], in1=pu[:, :n], op=ALU.mult),
                             reads=[sgk, puk], writes=blkkeys("act", f, lc, n))

                return [lambda f=f: one(f) for f in range(FC)]

            def down_thunks(tl):
                p0 = TILES[tl[0]][0]

                def one(dc):
                    s = fcount[1] % 2
                    fcount[1] += 1
                    S.op("pool", lambda e: e.dma_start(out=wdn[s], in_=wd[j, dc].rearrange(
                        "p (k n) -> p k n", k=FC)), writes=[("wd", s)], dma="wd%d" % s)
                    for ti in tl:
                        c0, n = TILES[ti]
                        lc = c0 - p0
                        py, pyk = PS.next()

                        def mmd(e, lc=lc, n=n, py=py):
                            for k in range(FC):
                                ins = e.matmul(py[:, :n], wdn[s][:, k, :], act[:, k, lc:lc + n], start=(k == 0),
                                               stop=(k == FC - 1))
                            return ins

                        S.op("pe", mmd, reads=[("wd", s)] + [k_ for f in range(FC) for k_ in blkkeys("act", f, lc, n)],
                             writes=[pyk])
                        S.op("act", lambda e, py=py, lc=lc, n=n: e.activation(out=Y[:, dc, lc:lc + n],
                                                                              in_=py[:, :n], func=AF.Copy),
                             reads=[pyk], writes=blkkeys("Y", dc, lc, n))

                return [lambda dc=dc: one(dc) for dc in range(DC)]

            def post_thunks(tl):
                p0 = TILES[tl[0]][0]
                out = []
                for ti in tl:
                    c0, n = TILES[ti]
                    lc = c0 - p0
                    out.append(lambda ti=ti, lc=lc, n=n: postnorm_add(
                        ti, slot_post, lambda c: Y[:, c, lc:lc + n], lambda c: blkkeys("Y", c, lc, n), True))
                return out

            def run(ths):
                for t in ths:
                    t()

            np_ = len(passes)
            run(pre_thunks(passes[0]))
            for p_i in range(np_):
                if p_i == 0:
                    run(gateup_thunks(passes[0]))
                if p_i + 1 < np_:
                    interleave(down_thunks(passes[p_i]), pre_thunks(passes[p_i + 1]))
                    interleave(gateup_thunks(passes[p_i + 1]), post_thunks(passes[p_i]))
                else:
                    run(down_thunks(passes[p_i]))
                    run(post_thunks(passes[p_i]))
            AR.off = m

        def rope_stage1(ps, psk, n):
            kb, kbk = misc["kb"].next()
            S.op("act", lambda e: e.activation(out=kb[:, :n], in_=ps[:, :n], func=AF.Copy), reads=[psk], writes=[kbk])
            return kb, kbk

        def rope_stage2(ps, psk, kb, kbk, c0t, n, dst_fn, dkey):
            CT_, ST_ = tabs["CT"], tabs["ST"]
            p2, p2k = PS.next()
            S.op("pe", lambda e: e.matmul(p2[:, :n], perm_sb, kb[:, :n], start=True, stop=True),
                 reads=[kbk, "perm"], writes=[p2k])
            t1, t1k = misc["tmp"].next()
            t2, t2k = misc["tmp"].next()
            S.op("dve", lambda e: e.tensor_tensor(out=t1[:, :n], in0=ps[:, :n], in1=CT_[:, c0t:c0t + n], op=ALU.mult),
                 reads=[psk, "tabc"], writes=[t1k])
            S.op("dve", lambda e: e.tensor_tensor(out=t2[:, :n], in0=p2[:, :n], in1=ST_[:, c0t:c0t + n], op=ALU.mult),
                 reads=[p2k, "tabs"], writes=[t2k])
            S.op("dve", lambda e: dst_fn(e, t1[:, :n], t2[:, :n]), reads=[t1k, t2k], writes=[dkey])

        def class_views(dst3, g, tt, full):
            d = GROUPS[g][1]
            base = dst3[:, 512 * tt:512 * tt + 512] if full else dst3
            if d == 1:
                return base, None
            if d == 4:
                return base.rearrange("p (r i) -> p r i", r=4), 4
            if full:
                return dst3.rearrange("p (r i) -> p r i", r=16)[:, :, 32 * tt:32 * tt + 32], 16
            return dst3.rearrange("p (r i) -> p r i", r=16), 16

        def proj_rope(xn_tile, xn_keys, wres, wkey, tt, dst_all, dname, full):
            pend = None
            for c in range(12):
                g = c // 4
                ps, psk = PS.next()

                def mm(e, c=c, ps=ps):
                    for k in range(DC):
                        ins = e.matmul(ps[:, :512], wres[:, k, c * 128:(c + 1) * 128], xn_tile[:, k, :],
                                       start=(k == 0), stop=(k == DC - 1))
                    return ins

                S.op("pe", mm, reads=[wkey] + xn_keys, writes=[psk])
                kb, kbk = rope_stage1(ps, psk, 512)
                ov, r = class_views(dst_all[:, c, :], g, tt, full)

                def fin(e, a, b, ov=ov, r=r):
                    if r is not None:
                        a = a.rearrange("p (i r) -> p r i", r=r)
                        b = b.rearrange("p (i r) -> p r i", r=r)
                    return e.tensor_tensor(out=ov, in0=a, in1=b, op=ALU.add)

                if pend is not None:
                    rope_stage2(*pend)
                pend = (ps, psk, kb, kbk, 512 * tt, 512, fin, (dname, c, tt if full else 0))
            rope_stage2(*pend)

        def layer0_pass(xsrc, possrc, corrsrc, prev):
            S.fence()
            if not prev:
                for c in range(DC):
                    S.op("sp", lambda e, c=c: e.dma_start(out=H[:, c, :], in_=xsrc[:, c, :]),
                         writes=[Hk(c, t) for t in range(5)], dma="ldx")
            if A_:
                if upto >= 1:
                    ffn(0, 0, 1, [[0, 1, 2], [3, 4]])

            if A_ and upto >= 2:
                S.fence()
                m = AR.off
                alloc_common()
                win_sb = AR.alloc((DC, D), BF16)
                wgr_sb = AR.alloc((4, 2, 256), BF16)
                wout_sb = AR.alloc((DC, D), BF16)
                corr_sb = AR.alloc((4, 128))
                load_dense(win_sb, w_in, "w_in", 2)
                S.op("pool", lambda e: e.dma_start(out=wgr_sb, in_=w_grp), writes=["w_grp"], dma="wdense")
                load_dense(wout_sb, w_out, "w_out", 2)
                S.op("sp", lambda e: e.dma_start(out=corr_sb, in_=corrsrc), writes=["corr"], dma="ldm")
                hm = AR.alloc((DC, 512), BF16)
                U = [AR.alloc((DC, 528)) for _ in range(2)]
                Ta = AR.alloc(528)
                Tb = AR.alloc(528)
                pTb = [AR.alloc((DC, 512), BF16) for _ in range(2)]
                yT = AR.alloc((DC, 512), BF16)
                Ym = AR.alloc((DC, 512))

                def stage_A(ti):
                    n = TILES[ti][1]
                    prenorm(ti, 2, lambda c: hm[:, c, :n], lambda c: ("hm", c))

                def stage_B(ti):
                    c0, n = TILES[ti]
                    cur = U[ti % 2]
                    nxt = U[(ti + 1) % 2]
                    ck = "U%d" % (ti % 2)
                    nk = "U%d" % ((ti + 1) % 2)
                    pT = pTb[ti % 2]
                    pk = "pT%d" % (ti % 2)
                    for c in range(DC):
                        ps, psk = PS.next()

                        def mm(e, c=c, ps=ps):
                            for k in range(DC):
                                ins = e.matmul(ps[:, :n], win_sb[:, k, c * 128:(c + 1) * 128], hm[:, k, :n],
                                               start=(k == 0), stop=(k == DC - 1))
                            return ins

                        S.op("pe", mm, reads=["w_in"] + [("hm", k) for k in range(DC)], writes=[psk])
                        if ti == 0:
                            S.op("act", lambda e, c=c, ps=ps: e.activation(out=nxt[:, c, 0:16],
                                                                           in_=ps[:, HALO - 16:HALO], func=AF.Copy),
                                 reads=[psk], writes=[(nk, c)])
                            continue
                        S.op("act", lambda e, c=c, ps=ps: e.activation(out=cur[:, c, 16:528], in_=ps[:, :512],
                                                                       func=AF.Copy),
                             reads=[psk], writes=[(ck, c)])
                        if ti < 4:
                            S.op("act", lambda e, c=c: e.activation(out=nxt[:, c, 0:16], in_=cur[:, c, 512:528],
                                                                    func=AF.Copy),
                                 reads=[(ck, c)], writes=[(nk, c)])
                        gi = c // 2
                        w = 2 << gi
                        Uc = cur[:, c, :]
                        src = Uc
                        srck = (ck, c)
                        sh = 1
                        bufs = [(Ta, "Ta"), (Tb, "Tb")]
                        bi = 0
                        while sh < w:
                            dstb, dk = bufs[bi]
                            S.op("dve", lambda e, src=src, dstb=dstb, sh=sh: e.tensor_tensor(
                                out=dstb[:, sh:528], in0=src[:, sh:528], in1=src[:, 0:528 - sh], op=ALU.add),
                                 reads=[srck], writes=[dk])
                            src, srck = dstb, dk
                            bi ^= 1
                            sh *= 2
                        if ti == 1:
                            S.op("dve", lambda e, src=src, gi=gi: e.tensor_tensor(out=src[:, 16:144], in0=src[:, 16:144],
                                                                                  in1=corr_sb[:, gi, :], op=ALU.mult),
                                 reads=[srck, "corr"], writes=[srck])
                        S.op("dve", lambda e, src=src, w=w, Uc=Uc, c=c: e.scalar_tensor_tensor(
                            out=pT[:, c, :], in0=src[:, 16:528], scalar=1.0 / w, in1=Uc[:, 16:528], op0=ALU.mult,
                            op1=ALU.subtract), reads=[srck, (ck, c)], writes=[(pk, c)])

                def stage_C(ti):
                    pT = pTb[ti % 2]
                    pk = "pT%d" % (ti % 2)
                    for dc in range(DC):
                        gi = dc // 2
                        ps, psk = PS.next()

                        def mm(e, dc=dc, gi=gi, ps=ps):
                            for cc in range(2):
                                ins = e.matmul(ps[:, :512], wgr_sb[:, gi, cc, (dc % 2) * 128:(dc % 2) * 128 + 128],
                                               pT[:, 2 * gi + cc, :], start=(cc == 0), stop=(cc == 1))
                            return ins

                        S.op("pe", mm, reads=["w_grp", (pk, 2 * gi), (pk, 2 * gi + 1)], writes=[psk])
                        S.op("act", lambda e, dc=dc, ps=ps: e.activation(out=yT[:, dc, :], in_=ps[:, :512], func=AF.Copy,
                                                                         scale=G[:, 56 + dc:57 + dc]),
                             reads=[psk, "G"], writes=[("yT", dc)])
                    for dc in range(DC):
                        ps, psk = PS.next()

                        def mm(e, dc=dc, ps=ps):
                            for k in range(DC):
                                ins = e.matmul(ps[:, :512], wout_sb[:, k, dc * 128:(dc + 1) * 128], yT[:, k, :],
                                               start=(k == 0), stop=(k == DC - 1))
                            return ins

                        S.op("pe", mm, reads=["w_out"] + [("yT", k) for k in range(DC)], writes=[psk])
                        S.op("act", lambda e, dc=dc, ps=ps: e.activation(out=Ym[:, dc, :], in_=ps[:, :512], func=AF.Copy),
                             reads=[psk], writes=[("Ym", dc)])
                    postnorm_add(ti, 3, lambda c: Ym[:, c, :], lambda c: ("Ym", c), False)

                stage_A(0)
                stage_B(0)
                stage_A(1)
                stage_B(1)
                for ti in range(1, 5):
                    if ti < 4:
                        stage_A(ti + 1)
                        stage_B(ti + 1)
                    stage_C(ti)
                AR.off = m

            if A_ and upto >= 3:
                ffn(1, 4, 5, [[1, 2], [3, 4]])

            if upto < 5 and not prev:
                S.fence()
                for c in range(DC):
                    S.op("sp", lambda e, c=c: e.dma_start(out=hT_out[:, c, :], in_=H[:, c, HALO:]),
                         reads=[Hk(c, t) for t in range(5)], dma="stH")
            if A_ and upto >= 4:
                S.fence()
                m = AR.off
                alloc_common()
                misc["kb"] = Pool("kb", [AR.alloc(512, BF16) for _ in range(2)])
                rope_tables(tab_prev if prev else tab_own)
                wk_sb = AR.alloc((DC, QKV), BF16)
                wv_sb = AR.alloc((DC, QKV), BF16)
                load_dense(wk_sb, w_k, "w_k", 2)
                load_dense(wv_sb, w_v, "w_v", 2)
                Kall = AR.alloc((12, 512), BF16)
                hkb = [AR.alloc((DC, 512), BF16) for _ in range(2)]
                hk16 = AR.alloc((DC, 512), BF16)
                vsb = Pool("vsb", [AR.alloc(512, BF16) for _ in range(3)])
                prenorm(1, 6, lambda c: hkb[0][:, c, :], lambda c: ("hk0", c))
                for tt in range(4 if upto >= 4.1 else 0):
                    ti = tt + 1
                    hk = hkb[tt % 2]
                    hkn = "hk%d" % (tt % 2)
                    if tt < 3:
                        prenorm(ti + 1, 6, lambda c, tt=tt: hkb[(tt + 1) % 2][:, c, :],
                                lambda c, tt=tt: ("hk%d" % ((tt + 1) % 2), c))
                    hkk = [(hkn, c) for c in range(DC)]
                    for c in range(DC if upto >= 4.3 else 0):
                        S.op("act", lambda e, c=c, hk=hk: e.activation(
                            out=hk16[:, c, :].rearrange("p (r i) -> p r i", r=16),
                            in_=hk[:, c, :].rearrange("p (i r) -> p r i", r=16), func=AF.Copy),
                             reads=[(hkn, c)], writes=[("hk16", c)])
                    proj_rope(hk, hkk, wk_sb, "w_k", tt, Kall, "Kall", False)
                    for g in range(3):
                        R = GROUPS[g][1]
                        kd = kin[g].rearrange("j p x -> p j x")
                        ksrc = Kall[:, 4 * g:4 * g + 4, :]
                        rk = [("Kall", c, 0) for c in range(4 * g, 4 * g + 4)]
                        if g < 2:
                            if prev and tt < 3:
                                continue
                            if prev and g == 0:
                                S.op("sp", lambda e, kd=kd, ksrc=ksrc: e.dma_start(out=kd[:, :, 0:128], in_=ksrc[:, :, 384:512]),
                                     reads=rk, dma="stK%d" % g)
                            elif prev:
                                S.op("sp", lambda e, kd=kd, ksrc=ksrc: e.dma_start(out=kd[:, :, 0:512], in_=ksrc), reads=rk, dma="stK%d" % g)
                            else:
                                o = R * 128 + 512 * tt
                                S.op("sp", lambda e, kd=kd, ksrc=ksrc, o=o: e.dma_start(out=kd[:, :, o:o + 512], in_=ksrc),
                                     reads=rk, dma="stK%d" % g)
                        else:
                            o = 0 if prev else 2048
                            for jj in range(4):
                                S.op("sp", lambda e, jj=jj, o=o, tt=tt: e.dma_start(
                                    out=kin[2][jj][:, o:o + 2048].rearrange("p (r i) -> p r i", r=16)[:, :, 32 * tt:32 * tt + 32],
                                    in_=Kall[:, 8 + jj, :].rearrange("p (r i) -> p r i", r=16)),
                                     reads=[("Kall", 8 + jj, 0)], dma="stK2%d" % jj)
                    for g in range(3 if upto >= 4.3 else 0):
                        d = GROUPS[g][1]
                        for b in range(4):
                            R = d
                            vd = vin[g].rearrange("j p b f -> p j b f")
                            if d == 1:
                                cols = lambda k, b=b, hk=hk: hk[:, k, 128 * b:128 * b + 128]
                                cls = [4 * tt + b]
                            elif d == 4:
                                cols = lambda k, b=b, hk=hk: hk[:, k, :].rearrange("p (i r) -> p r i", r=4)[:, b, :]
                                cls = [4 * tt + b]
                            else:
                                cols = lambda k, b=b: hk16[:, k, 128 * b:128 * b + 128]
                                cls = [4 * b + rr for rr in range(4)]
                            blks = [(k_ - (16 - R)) if prev else (R + k_) for k_ in cls]
                            if blks[0] < 0:
                                continue
                            ps, psk = PS.next()

                            def mm(e, cols=cols, ps=ps, g=g):
                                for k in range(DC):
                                    ins = e.matmul(ps[:, :512], cols(k), wv_sb[:, k, g * 512:(g + 1) * 512],
                                                   start=(k == 0), stop=(k == DC - 1))
                                return ins

                            S.op("pe", mm, reads=["w_v"] + hkk + [("hk16", c) for c in range(DC)], writes=[psk])
                            vb, vbk = vsb.next()
                            S.op("act", lambda e, vb=vb, ps=ps: e.activation(out=vb, in_=ps[:, :512], func=AF.Copy),
                                 reads=[psk], writes=[vbk])
                            if d == 16:
                                for rr in range(4):
                                    S.op("sp", lambda e, vb=vb, vd=vd, rr=rr, tt=tt, blk=blks[rr]: e.dma_start(
                                        out=vd[32 * tt:32 * tt + 32, :, blk, :],
                                        in_=vb[32 * rr:32 * rr + 32, :].rearrange("p (j f) -> p j f", j=4)), reads=[vbk], dma="stV%d" % vbk[1])
                            else:
                                S.op("sp", lambda e, vb=vb, vd=vd, blk=blks[0]: e.dma_start(
                                    out=vd[:, :, blk, :], in_=vb.rearrange("p (j f) -> p j f", j=4)), reads=[vbk], dma="stV%d" % vbk[1])
                AR.off = m

        layer0_pass(xT_prev, pos_prev, corr_prev, True)
        layer0_pass(xT_own, pos_own, corr_own, False)

        if upto >= 5:
            ffn(2, 8, 9, [[1, 2], [3, 4]])

        if upto >= 5.15:
            S.fence()
            m = AR.off
            alloc_common()
            misc["kb"] = Pool("kb", [AR.alloc(512, BF16) for _ in range(2)])
            Qall = AR.alloc((12, NT), BF16)
            m2 = AR.off
            rope_tables(tab_own)
            wq_sb = AR.alloc((DC, QKV), BF16)
            load_dense(wq_sb, w_q, "w_q", 2)
            hmqb = [AR.alloc((DC, 512), BF16) for _ in range(2)]
            prenorm(1, 10, lambda c: hmqb[0][:, c, :], lambda c: ("hmq0", c))
            for tt in range(4):
                ti = tt + 1
                if tt < 3:
                    prenorm(ti + 1, 10, lambda c, tt=tt: hmqb[(tt + 1) % 2][:, c, :],
                            lambda c, tt=tt: ("hmq%d" % ((tt + 1) % 2), c))
                proj_rope(hmqb[tt % 2], [("hmq%d" % (tt % 2), c) for c in range(DC)], wq_sb, "w_q", tt, Qall, "Qall",
                          True)
            S.fence()
            AR.off = m2
            mk = AR.alloc((2, 512), BF16)
            S.op("sp", lambda e: e.dma_start(out=mk, in_=masks), writes=["mk"], dma="ldm")
            S.op("sp", lambda e: e.dma_start(out=ident, in_=masks_id), writes=["ident"], dma="ldm")
            wo_sb = AR.alloc((4, D), BF16)
            load_dense(wo_sb, w_o, "w_o", 1)
            OT = AR.alloc((4, NT), BF16)
            m3 = AR.off
            ND = AR.alloc((2, NT))
            kbuf = [AR.alloc(32 * 128, BF16) for _ in range(2)]
            vbuf = [AR.alloc((32, 128), BF16) for _ in range(2)]
            ptp = Pool("pt", [AR.alloc(512, BF16) for _ in range(3)])
            Qk = [("Qall", c, tt) for c in range(12) for tt in range(4)]
            slot_ctr = [0]

            def begin_group(jj, g):
                R = GROUPS[g][1]
                nb = R + 16
                s_ = slot_ctr[0] % 2
                slot_ctr[0] += 1
                S.op("sp", lambda e: e.dma_start(out=kbuf[s_][:, :nb * 128], in_=kin[g][jj]),
                     writes=[("kbuf", s_)], dma="kb%d" % s_)
                S.op("sp", lambda e: e.dma_start(out=vbuf[s_][:, :nb, :], in_=vin[g][jj]),
                     writes=[("vbuf", s_)], dma="vb%d" % s_)
                return dict(jj=jj, g=g, R=R, s=s_, Qc=Qall[:, 4 * g + jj, :])

            def S_stage(ctx, k):
                jj, g, R, s_, Qc = ctx["jj"], ctx["g"], ctx["R"], ctx["s"], ctx["Qc"]
                first = k < R
                pt, ptk = ptp.next()
                for hh in range(2):
                    pss, pssk = PS.next()

                    def mms(e, pss=pss, hh=hh):
                        e.matmul(pss[:, :256], ident, mk[:, 1 if first else 0, 0:256], start=True, stop=False)
                        q = Qc[:, 128 * k:128 * k + 128]
                        lo = 64 * hh
                        for w_, kb_ in enumerate((k, k + R)):
                            ins = e.matmul(pss[:, w_ * 128:(w_ + 1) * 128],
                                           kbuf[s_][lo:lo + 64, kb_ * 128:(kb_ + 1) * 128], q[lo:lo + 64, :],
                                           start=False, stop=(w_ == 1))
                        return ins

                    S.op("pe", mms, reads=[("kbuf", s_), "mk", "ident"] + [("Qall", 4 * g + jj, tt) for tt in
                                                                            range(4)], writes=[pssk])
                    S.op("act", lambda e, pss=pss, hh=hh: e.activation(
                        out=pt[:, 256 * hh:256 * hh + 256], in_=pss[:, :256], func=AF.Exp, scale=0.125),
                         reads=[pssk], writes=[(ptk, hh)])
                return pt, ptk

            def PV_stage(ctx, k, pt, ptk):
                g, R, s_ = ctx["g"], ctx["R"], ctx["s"]
                pso, psok = PS.next()

                def mmo(e):
                    for hh in range(2):
                        lo = 64 * hh
                        for w_, kb_ in enumerate((k, k + R)):
                            e.matmul(pso[lo:lo + 64, 0:128], vbuf[s_][:, kb_, lo:lo + 64],
                                     pt[:, (2 * hh + w_) * 128:(2 * hh + w_ + 1) * 128], start=(w_ == 0),
                                     stop=(w_ == 1))
                        for w_ in range(2):
                            ins = e.matmul(pso[lo:lo + 64, 128:256], ones1[:, 0:64],
                                           pt[:, (2 * hh + w_) * 128:(2 * hh + w_ + 1) * 128], start=(w_ == 0),
                                           stop=(w_ == 1))
                    return ins

                S.op("pe", mmo, reads=[("vbuf", s_), (ptk, 0), (ptk, 1), "ones1"], writes=[psok])
                if R == 1:
                    ndv = ND[:, :, 128 * k:128 * k + 128]
                elif R == 4:
                    n_, r_ = k // 4, k % 4
                    ndv = ND[:, :, 512 * n_:512 * n_ + 512].rearrange("p x (i r) -> p x r i", r=4)[:, :, r_, :]
                else:
                    ndv = ND.rearrange("p x (i r) -> p x r i", r=16)[:, :, k, :]
                psv = pso[:, 0:256].rearrange("p (x i) -> p x i", x=2)
                if g == 0:
                    S.op("dve", lambda e: e.tensor_copy(out=ndv, in_=psv), reads=[psok], writes=["NDall"])
                else:
                    S.op("dve", lambda e: e.tensor_tensor(out=ndv, in0=psv, in1=ndv, op=ALU.add),
                         reads=[psok, "NDall"], writes=["NDall"])

            def end_jj(jj):
                for tt in range(4):
                    S.op("dve", lambda e, tt=tt: e.reciprocal(out=ND[:, 1, 512 * tt:512 * tt + 512],
                                                              in_=ND[:, 1, 512 * tt:512 * tt + 512]),
                         reads=["NDall"], writes=["NDall"])
                    S.op("dve", lambda e, tt=tt: e.tensor_tensor(out=OT[:, jj, 512 * tt:512 * tt + 512],
                                                                 in0=ND[:, 0, 512 * tt:512 * tt + 512],
                                                                 in1=ND[:, 1, 512 * tt:512 * tt + 512],
                                                                 op=ALU.mult),
                         reads=["NDall"], writes=[("OT", jj, tt)])

            steps = [(jj, g, k) for jj in range(4) for g in range(3) for k in range(16)]
            ctxs = {}

            def get_ctx(jj, g):
                if (jj, g) not in ctxs:
                    ctxs[(jj, g)] = begin_group(jj, g)
                return ctxs[(jj, g)]

            cur = S_stage(get_ctx(0, 0), 0)
            for i, (jj, g, k) in enumerate(steps):
                nxt = None
                if i + 1 < len(steps):
                    j2, g2, k2 = steps[i + 1]
                    nxt = S_stage(get_ctx(j2, g2), k2)
                PV_stage(get_ctx(jj, g), k, *cur)
                if g == 2 and k == 15:
                    end_jj(jj)
                cur = nxt
            S.fence()
            AR.off = m3
            Ym = AR.alloc((DC, 512))
            for tt in range(4):
                ti = tt + 1
                for dc in range(DC):
                    ps, psk = PS.next()

                    def mm(e, dc=dc, ps=ps, tt=tt):
                        for k in range(4):
                            ins = e.matmul(ps[:, :512], wo_sb[:, k, dc * 128:(dc + 1) * 128],
                                           OT[:, k, 512 * tt:512 * tt + 512], start=(k == 0), stop=(k == 3))
                        return ins

                    S.op("pe", mm, reads=["w_o"] + [("OT", k, tt) for k in range(4)], writes=[psk])
                    S.op("act", lambda e, dc=dc, ps=ps: e.activation(out=Ym[:, dc, :], in_=ps[:, :512], func=AF.Copy),
                         reads=[psk], writes=[("Ym", dc)])
                postnorm_add(ti, 11, lambda c: Ym[:, c, :], lambda c: ("Ym", c), False)
            AR.off = m

        if upto >= 5.25:
            ffn(3, 12, 13, [[1, 2], [3, 4]])
        if upto >= 5:
            S.fence()
            for c in range(DC):
                S.op("sp", lambda e, c=c: e.dma_start(out=hT_out[:, c, :], in_=H[:, c, HALO:]),
                     reads=[Hk(c, t) for t in range(5)], dma="stH")

        S.fence()
        final_waits = dict(S.fence_tok)

        sems = {}
        for e in ENGS:
            sems[("e", e)] = es.enter_context(nc.semaphore("sem_" + e))
        for k in S.dcnt:
            sems[("d", k)] = es.enter_context(nc.semaphore("dsem_" + k))
        engmap = {"pe": "tensor", "act": "scalar", "dve": "vector", "pool": "gpsimd", "sp": "sync"}
        with nc.Block() as block:
            def make(eng):
                def body(e):
                    for waits, fn, tok in S.q[eng]:
                        for sk, v in waits:
                            e.wait_ge(sems[sk], v)
                        ins = fn(e)
                        ins.then_inc(sems[tok[0]], 16 if tok[0][0] == "d" else 1)
                    if eng == "sp":
                        for sk, v in final_waits.items():
                            if v > 0:
                                e.wait_ge(sems[sk], v)
                return body

            for eng in ENGS:
                getattr(block, engmap[eng])(make(eng))
    return nc


def _fm(a):
    T, F = a.shape
    return np.ascontiguousarray(a.T.reshape(F // 128, 128, T).transpose(1, 0, 2))


def _dense(w):
    k, n = w.shape
    return np.ascontiguousarray(w.reshape(k // 128, 128, n).transpose(1, 0, 2))


def _ffn_w(gate, up, down):
    gate = gate.reshape(4, DC, 128, FC, 128)
    up = up.reshape(4, DC, 128, FC, 128)
    down = down.reshape(4, FC, 128, DC, 128)
    wg = np.ascontiguousarray(gate.transpose(0, 3, 2, 1, 4)).reshape(4, FC, 128, DC * 128)
    wu = np.ascontiguousarray(up.transpose(0, 3, 2, 1, 4)).reshape(4, FC, 128, DC * 128)
    wd = np.ascontiguousarray(down.transpose(0, 3, 2, 1, 4)).reshape(4, DC, 128, FC * 128)
    return wg, wu, wd


def _vecs(norm_gain, kv_gain, pool_scale):
    v = np.zeros((128, 14 * 8), np.float32)
    for i in range(6):
        v[:, i * 8:(i + 1) * 8] = norm_gain[0, i].reshape(8, 128).T
        v[:, (8 + i) * 8:(9 + i) * 8] = norm_gain[1, i].reshape(8, 128).T
    v[:, 48:56] = kv_gain.reshape(8, 128).T
    v[:, 56:64] = pool_scale.reshape(8, 128).T
    return v


_NC_CACHE = {}


def _get_nc():
    if "F" not in _NC_CACHE:
        _NC_CACHE["F"] = build()
    return _NC_CACHE["F"]


def _corr(is_seq_start):
    corr = np.ones((128, 4, 128), np.float32)
    if is_seq_start:
        t = np.arange(128)
        for g, w in enumerate((2, 4, 8, 16)):
            corr[:, g, :] = (w / np.minimum(t + 1, w)).astype(np.float32)[None, :]
    return corr


def make_in_maps(x, positions, norm_gain, ffn_w_gate, ffn_w_up, ffn_w_down, pool_w_in, pool_w_group, pool_scale,
                 pool_w_out, kv_norm_gain, w_k, w_v, attn_w_q, attn_w_o):
    x = np.asarray(x, np.float32)
    positions = np.asarray(positions, np.int32)
    bf = ml_dtypes.bfloat16
    p = np.arange(128)
    inv_freq = (np.float32(10000.0) ** (-(np.arange(0, 64, 2, dtype=np.float32)) / np.float32(64))).astype(np.float32)
    rconst = np.stack([inv_freq[p % 32], np.where((p % 64) < 32, -1.0, 1.0)], axis=1).astype(np.float32)
    partner = np.where((p % 64) < 32, p + 32, p - 32)
    permM = np.zeros((128, 128), np.float32)
    permM[partner, p] = 1.0
    permM = permM.astype(bf)
    wg, wu, wd = _ffn_w(np.asarray(ffn_w_gate, np.float32), np.asarray(ffn_w_up, np.float32),
                        np.asarray(ffn_w_down, np.float32))
    vecs = _vecs(np.asarray(norm_gain, np.float32), np.asarray(kv_norm_gain, np.float32),
                 np.asarray(pool_scale[0], np.float32))
    common = dict(
        wg=wg, wu=wu, wd=wd, vecs=vecs, rconst=rconst, permM=permM,
        w_in=_dense(np.asarray(pool_w_in[0], np.float32)), w_out=_dense(np.asarray(pool_w_out[0], np.float32)),
        w_grp=np.ascontiguousarray(np.asarray(pool_w_group[0], np.float32).reshape(4, 2, 128, 256).transpose(2, 0, 1, 3)),
        w_k=_dense(np.asarray(w_k, np.float32)), w_v=_dense(np.asarray(w_v, np.float32)),
        w_q=_dense(np.asarray(attn_w_q[0], np.float32)), w_o=_dense(np.asarray(attn_w_o[0], np.float32)),
        masks_id=np.eye(128, dtype=np.float32).astype(bf))
    kk = np.arange(128)[:, None]
    qq = np.arange(128)[None, :]
    mprev = np.where(kk >= qq, 0.0, NEG).astype(np.float32)
    mcur = np.where(kk <= qq, 0.0, NEG).astype(np.float32)
    mall = np.full((128, 128), NEG, np.float32)
    MN = np.concatenate([mprev, mcur, mprev, mcur], axis=1)
    MF0 = np.concatenate([mall, mcur, mall, mcur], axis=1)
    in_maps = []
    for c in range(8):
        b, q = divmod(c, 4)
        s0 = q * NT

        def xslice(start):
            xs = np.zeros((HC, D), np.float32)
            lo = start - HALO
            if start >= 0:
                a = max(lo, 0)
                xs[a - lo:] = x[b, a:start + NT]
            return _fm(xs)

        pp = s0 - NT
        pos_prev = positions[b:b + 1, pp:pp + NT] if q > 0 else positions[b:b + 1, 0:NT]
        d = dict(common)
        d.update(xT_prev=xslice(s0 - NT if q > 0 else -10 ** 9), xT_own=xslice(s0),
                 pos_prev=np.ascontiguousarray(pos_prev), pos_own=np.ascontiguousarray(positions[b:b + 1, s0:s0 + NT]),
                 corr_prev=_corr(q == 1), corr_own=_corr(q == 0),
                 masks=np.stack([MN, MF0 if q == 0 else MN], axis=1).astype(bf))
        in_maps.append(d)
    return in_maps


def kernel(x, positions, norm_gain, ffn_w_gate, ffn_w_up, ffn_w_down, pool_w_in, pool_w_group, pool_scale,
           pool_w_out, kv_norm_gain, w_k, w_v, attn_w_q, attn_w_o):
    in_maps = make_in_maps(x, positions, norm_gain, ffn_w_gate, ffn_w_up, ffn_w_down, pool_w_in, pool_w_group,
                           pool_scale, pool_w_out, kv_norm_gain, w_k, w_v, attn_w_q, attn_w_o)
    res = run_bass_kernel_spmd(_get_nc(), in_maps, core_ids=list(range(8))).results
    out = np.zeros((2, 8192, D), np.float32)
    for c in range(8):
        b, q = divmod(c, 4)
        hT = np.asarray(res[c]["hT"])
        out[b, q * NT:(q + 1) * NT] = hT.transpose(2, 1, 0).reshape(NT, D)
    return out
```

```python
import contextlib
import os
KSTEP = int(os.environ.get('KSTEP', '9'))
import numpy as np
import ml_dtypes
import concourse.bass as bass
import concourse.mybir as mybir
from concourse.bass_utils import run_bass_kernel_spmd

F32 = mybir.dt.float32
BF16 = mybir.dt.bfloat16
I32 = mybir.dt.int32
AF = mybir.ActivationFunctionType
ALU = mybir.AluOpType

D = 1024
DFF = 2816
NT = 2048
HALO = 128
HC = HALO + NT
DC = 8
FC = 22
QKV = 1536
EPS = 1e-6
NEG = -30000.0
NWG = 2
GROUPS = ((128, 1), (512, 4), (2048, 16))
ENGS = ("pe", "act", "dve", "pool", "sp")
SAME_ENGINE_FREE = ("pe", "sp")
MAGIC = 12582912.0
TWO_PI = 6.283185307179586
C1 = 6.28125
C2 = TWO_PI - C1

TILES = [(0, HALO)] + [(HALO + 512 * i, 512) for i in range(4)]


class Sched:
    def __init__(self):
        self.q = {e: [] for e in ENGS}
        self.cnt = {e: 0 for e in ENGS}
        self.dcnt = {}
        self.seen = {e: {} for e in ENGS}
        self.last_w = {}
        self.readers = {}
        self.fence_tok = None
        self.fence_done = set(ENGS)

    def fence(self):
        tok = {("e", e): self.cnt[e] for e in ENGS}
        for k, v in self.dcnt.items():
            tok[("d", k)] = v
        self.fence_tok = tok
        self.fence_done = set()

    def op(self, eng, fn, reads=(), writes=(), dma=None):
        waits = {}
        writes = list(writes) + [k for k in reads if isinstance(k, tuple) and k[0] == "ps" and k not in writes]

        def need(dep):
            if dep is None:
                return
            sk, val = dep
            if sk == ("e", eng) and eng in SAME_ENGINE_FREE:
                return
            if sk[0] == "d":
                val = self.dcnt[sk[1]]
            if val <= 0 or self.seen[eng].get(sk, 0) >= val:
                return
            if waits.get(sk, 0) < val:
                waits[sk] = val

        if eng not in self.fence_done:
            for sk, v in self.fence_tok.items():
                need((sk, v))
            self.fence_done.add(eng)
        for k in reads:
            need(self.last_w.get(k))
        for k in writes:
            need(self.last_w.get(k))
            for r in self.readers.get(k, ()):
                need(r)
        for sk, v in waits.items():
            self.seen[eng][sk] = v
        if dma is None:
            self.cnt[eng] += 1
            tok = (("e", eng), self.cnt[eng])
        else:
            self.dcnt[dma] = self.dcnt.get(dma, 0) + 16
            tok = (("d", dma), self.dcnt[dma])
        for k in writes:
            self.last_w[k] = tok
            self.readers[k] = []
        for k in reads:
            self.readers.setdefault(k, []).append(tok)
        self.q[eng].append((sorted(waits.items(), key=str), fn, tok))


class Arena:
    def __init__(self, ap, nwords):
        self.ap = ap
        self.n = nwords
        self.off = 0

    def alloc(self, free, dtype=F32):
        if isinstance(free, int):
            free = (free,)
        n = int(np.prod(free))
        words = n if dtype != BF16 else (n + 1) // 2
        words = (words + 7) // 8 * 8
        assert self.off + words <= self.n, ("arena overflow", self.off, words, self.n)
        v = self.ap[:, self.off:self.off + words]
        self.off += words
        if dtype == BF16:
            v = v.bitcast(BF16)
        elif dtype == I32:
            v = v.bitcast(I32)
        v = v[:, 0:n]
        if len(free) == 2:
            v = v.rearrange("p (a b) -> p a b", a=free[0])
        elif len(free) == 3:
            v = v.rearrange("p (a b c) -> p a b c", a=free[0], b=free[1])
        return v


class Pool:
    def __init__(self, name, bufs):
        self.name = name
        self.bufs = bufs
        self.i = -1

    def next(self):
        self.i = (self.i + 1) % len(self.bufs)
        return self.bufs[self.i], (self.name, self.i)


def build(upto=99, dbg=False):
    nc = bass.Bass("TRN2", target_bir_lowering=False)
    S = Sched()
    A_ = True

    def din(name, shape, dt=F32):
        return nc.dram_tensor(name, list(shape), dt, kind="ExternalInput").ap()

    def dout(name, shape, dt=F32):
        return nc.dram_tensor(name, list(shape), dt, kind="ExternalOutput").ap()

    xT_prev = din("xT_prev", [128, DC, HC])
    xT_own = din("xT_own", [128, DC, HC])
    wg = din("wg", [4, FC, 128, DC * 128])
    wu = din("wu", [4, FC, 128, DC * 128])
    wd = din("wd", [4, DC, 128, FC * 128])
    NV = 14 * 8
    vecs = din("vecs", [128, NV])
    pos_prev = din("pos_prev", [1, NT], I32)
    pos_own = din("pos_own", [1, NT], I32)
    rconst = din("rconst", [128, 2])
    permM = din("permM", [128, 128], BF16)
    w_in = din("w_in", [128, DC, D])
    w_grp = din("w_grp", [128, 4, 2, 256])
    w_out = din("w_out", [128, DC, D])
    w_k = din("w_k", [128, DC, QKV])
    w_v = din("w_v", [128, DC, QKV])
    corr_prev = din("corr_prev", [128, 4, 128])
    corr_own = din("corr_own", [128, 4, 128])
    w_q = din("w_q", [128, DC, QKV])
    w_o = din("w_o", [128, 4, D])
    masks = din("masks", [128, 2, 512], BF16)
    masks_id = din("masks_id", [128, 128], BF16)
    hT_out = dout("hT", [128, DC, NT])
    skind = "ExternalOutput" if dbg else "Internal"
    kin = [nc.dram_tensor("kin%d" % g, [4, 128, (GROUPS[g][1] + 16) * 128], BF16, kind=skind).ap() for g in range(3)]
    vin = [nc.dram_tensor("vin%d" % g, [4, 128, (GROUPS[g][1] + 16), 128], BF16, kind=skind).ap() for g in range(3)]
    tab_prev = nc.dram_tensor("tab_prev", [128, 2, NT], F32).ap()
    tab_own = nc.dram_tensor("tab_own", [128, 2, NT], F32).ap()

    es = contextlib.ExitStack()
    with es:
        AW = 53000
        arena_t = es.enter_context(nc.sbuf_tensor("arena", [128, AW], F32))
        AR = Arena(arena_t[:], AW)
        banks = [es.enter_context(nc.psum_tensor("bank%d" % i, [128, 512], F32)) for i in range(8)]
        PS = Pool("ps", [b[:] for b in banks])

        H = AR.alloc((DC, HC))
        G = AR.alloc(NV)
        RC = AR.alloc(2)
        onesD = AR.alloc(128, BF16)
        ones4D = AR.alloc(128, BF16)
        ones1 = AR.alloc(128, BF16)
        ident = AR.alloc(128, BF16)
        perm_sb = AR.alloc(128, BF16)
        epsb = AR.alloc(2)
        tabs = {}

        def Hk(c, ti):
            return ("H", c, ti)

        S.op("sp", lambda e: e.dma_start(out=G, in_=vecs), writes=["G"], dma="ldc")
        S.op("sp", lambda e: e.dma_start(out=RC, in_=rconst), writes=["RC"], dma="ldc")
        S.op("sp", lambda e: e.dma_start(out=perm_sb, in_=permM), writes=["perm"], dma="ldc")
        S.op("dve", lambda e: e.memset(onesD, 1.0 / D), writes=["onesD"])
        S.op("dve", lambda e: e.memset(ones4D, 4.0 / D), writes=["ones4D"])
        S.op("dve", lambda e: e.memset(ones1, 1.0), writes=["ones1"])
        S.op("dve", lambda e: e.memset(epsb[:, 0:1], EPS), writes=["epsb"])
        S.op("dve", lambda e: e.memset(epsb[:, 1:2], 4 * EPS), writes=["epsb"])

        def rope_precompute(pos, dst):
            m = AR.off
            CT = AR.alloc(NT)
            ST = AR.alloc(NT)
            posi = AR.alloc(NT, I32)
            ang = AR.alloc(NT)
            t1 = AR.alloc(NT)
            t2 = AR.alloc(NT)
            S.op("sp", lambda e: e.dma_start(out=posi, in_=pos.to_broadcast([128, NT])), writes=["posi"], dma="ldp")
            S.op("dve", lambda e: e.tensor_copy(out=ang, in_=posi), reads=["posi"], writes=["ang"])
            S.op("dve", lambda e: e.tensor_scalar(ang, ang, RC[:, 0:1], None, ALU.mult), reads=["ang", "RC"],
                 writes=["ang"])
            for which, dst_sb, shift in (("s", ST, 0.0), ("c", CT, np.pi / 2)):
                S.op("dve", lambda e, shift=shift: e.tensor_scalar(t1, ang, float(shift), None, ALU.add),
                     reads=["ang"], writes=["t1"])
                S.op("dve", lambda e: e.tensor_scalar(t2, t1, 1.0 / TWO_PI, MAGIC, ALU.mult, ALU.add),
                     reads=["t1"], writes=["t2"])
                S.op("dve", lambda e: e.tensor_scalar(t2, t2, MAGIC, None, ALU.subtract), reads=["t2"], writes=["t2"])
                S.op("dve", lambda e: e.scalar_tensor_tensor(out=t1, in0=t2, scalar=-C1, in1=t1, op0=ALU.mult,
                                                             op1=ALU.add), reads=["t1", "t2"], writes=["t1"])
                S.op("dve", lambda e: e.scalar_tensor_tensor(out=t1, in0=t2, scalar=-C2, in1=t1, op0=ALU.mult,
                                                             op1=ALU.add), reads=["t1", "t2"], writes=["t1"])
                S.op("dve", lambda e: e.tensor_scalar(t1, t1, 3.1415925, -3.1415925, ALU.min, ALU.max),
                     reads=["t1"], writes=["t1"])
                if which == "s":
                    S.op("act", lambda e, dst_sb=dst_sb: e.activation(out=dst_sb, in_=t1, func=AF.Sin,
                                                                      scale=RC[:, 1:2]),
                         reads=["t1", "RC"], writes=["ptab" + which])
                else:
                    S.op("act", lambda e, dst_sb=dst_sb: e.activation(out=dst_sb, in_=t1, func=AF.Sin),
                         reads=["t1"], writes=["ptab" + which])
            S.op("sp", lambda e: e.dma_start(out=dst[:, 0, :], in_=CT), reads=["ptabc"], dma="stT")
            S.op("sp", lambda e: e.dma_start(out=dst[:, 1, :], in_=ST), reads=["ptabs"], dma="stT")
            S.fence()
            AR.off = m

        def rope_tables(src):
            CT = AR.alloc(NT)
            ST = AR.alloc(NT)
            tabs["CT"], tabs["ST"] = CT, ST
            S.op("sp", lambda e: e.dma_start(out=CT, in_=src[:, 0, :]), writes=["tabc"], dma="ldt")
            S.op("sp", lambda e: e.dma_start(out=ST, in_=src[:, 1, :]), writes=["tabs"], dma="ldt")

        rope_precompute(pos_prev, tab_prev)
        rope_precompute(pos_own, tab_own)

        sq_pool = None
        misc = {}
        NTMP = [2]

        def alloc_common():
            misc["sq"] = Pool("sq", [AR.alloc(512, BF16) for _ in range(2)])
            misc["rstd"] = Pool("rstd", [AR.alloc(512) for _ in range(2)])
            misc["tmp"] = Pool("tmp", [AR.alloc(512) for _ in range(NTMP[0])])

        def aslist(k):
            return list(k) if isinstance(k, list) else [k]

        def gcol(slot, c):
            return G[:, slot * 8 + c: slot * 8 + c + 1]

        def rms_stats(src_fn, src_keys, n, onesm, eps, sq_eng="act"):
            ps, psk = PS.next()
            for c in range(DC):
                sq, sqk = misc["sq"].next()
                if sq_eng == "act":
                    S.op("act", lambda e, c=c, sq=sq: e.activation(out=sq[:, :n], in_=src_fn(c), func=AF.Square),
                         reads=aslist(src_keys(c)), writes=[sqk])
                else:
                    S.op("dve", lambda e, c=c, sq=sq: e.tensor_tensor(out=sq[:, :n], in0=src_fn(c), in1=src_fn(c),
                                                                       op=ALU.mult),
                         reads=aslist(src_keys(c)), writes=[sqk])
                S.op("pe", lambda e, c=c, sq=sq, ps=ps: e.matmul(ps[:, :n], onesm, sq[:, :n], start=(c == 0),
                                                                  stop=(c == DC - 1)),
                     reads=[sqk, "onesD", "ones4D"], writes=[psk])
            rstd, rk = misc["rstd"].next()
            S.op("act", lambda e, ps=ps, rstd=rstd: e.activation(out=rstd[:, :n], in_=ps[:, :n], func=AF.Ln,
                                                                 bias=epsb[:, 1:2] if eps > 2e-6 else epsb[:, 0:1]),
                 reads=[psk, "epsb"], writes=[rk])
            S.op("act", lambda e, rstd=rstd: e.activation(out=rstd[:, :n], in_=rstd[:, :n], func=AF.Exp, scale=-0.5),
                 reads=[rk], writes=[rk])
            return rstd, rk

        def prenorm(ti, slot, dst_fn, dst_key):
            c0, n = TILES[ti]
            rstd, rk = rms_stats(lambda c: H[:, c, c0:c0 + n], lambda c: Hk(c, ti), n, onesD, EPS)
            for c in range(DC):
                S.op("dve", lambda e, c=c: e.scalar_tensor_tensor(out=dst_fn(c), in0=H[:, c, c0:c0 + n],
                                                                  scalar=gcol(slot, c), in1=rstd[:, :n],
                                                                  op0=ALU.mult, op1=ALU.mult),
                     reads=[Hk(c, ti), rk, "G"], writes=aslist(dst_key(c)))

        def postnorm_add(ti, slot, Y_fn, Y_key, half):
            c0, n = TILES[ti]
            rstd, rk = rms_stats(Y_fn, Y_key, n, ones4D if half else onesD, 4 * EPS if half else EPS, sq_eng="act")
            for c in range(DC):
                tmp, tk = misc["tmp"].next()
                S.op("dve", lambda e, c=c, tmp=tmp: e.scalar_tensor_tensor(out=tmp[:, :n], in0=Y_fn(c),
                                                                           scalar=gcol(slot, c), in1=rstd[:, :n],
                                                                           op0=ALU.mult, op1=ALU.mult),
                     reads=aslist(Y_key(c)) + [rk, "G"], writes=[tk])
                S.op("dve", lambda e, c=c, tmp=tmp: e.tensor_tensor(out=H[:, c, c0:c0 + n], in0=H[:, c, c0:c0 + n],
                                                                    in1=tmp[:, :n], op=ALU.add),
                     reads=[tk, Hk(c, ti)], writes=[Hk(c, ti)])

        def load_dense_cols(dst, src, key, width):
            N = dst.shape[2]
            for c in range(N // width):
                S.op("pool", lambda e, c=c: e.dma_start(out=dst[:, :, c * width:(c + 1) * width],
                                                        in_=src[:, :, c * width:(c + 1) * width]),
                     writes=[(key, c)], dma="%s_%d" % (key, c))

        def load_dense(dst, src, key, nsplit=1):
            a = dst.shape[1]
            step = (a + nsplit - 1) // nsplit
            for i in range(0, a, step):
                S.op("pool", lambda e, i=i: e.dma_start(out=dst[:, i:i + step], in_=src[:, i:i + step]),
                     writes=[key], dma="wdense")

        def blkkeys(name, idx, lc, n):
            return [(name, idx, b) for b in range(lc // 128, (lc + n + 127) // 128)]

        def interleave(main, side):
            nm, ns = len(main), len(side)
            si = 0
            for i, t in enumerate(main):
                t()
                want = ((i + 1) * ns) // nm
                while si < want:
                    side[si]()
                    si += 1
            while si < ns:
                side[si]()
                si += 1

        def ffn(j, slot_pre, slot_post, passes):
            S.fence()
            m = AR.off
            alloc_common()
            W = HALO + 1024
            xn = AR.alloc((DC, W), BF16)
            act = AR.alloc((FC, W), BF16)
            Y = AR.alloc((DC, W))
            sgp = Pool("sg", [AR.alloc(512) for _ in range(2)])
            wgu = [(AR.alloc((DC, 128), BF16), AR.alloc((DC, 128), BF16)) for _ in range(NWG)]
            wdn = [AR.alloc((FC, 128), BF16) for _ in range(2)]
            fcount = [0, 0]

            def pre_thunks(tl):
                p0 = TILES[tl[0]][0]
                out = []
                for ti in tl:
                    c0, n = TILES[ti]
                    lc = c0 - p0
                    out.append(lambda ti=ti, lc=lc, n=n: prenorm(
                        ti, slot_pre, lambda c: xn[:, c, lc:lc + n], lambda c: blkkeys("xn", c, lc, n)))
                return out

            def gateup_thunks(tl):
                p0 = TILES[tl[0]][0]

                def one(f):
                    s = fcount[0] % NWG
                    fcount[0] += 1
                    S.op("pool", lambda e: e.dma_start(out=wgu[s][0], in_=wg[j, f].rearrange(
                        "p (k n) -> p k n", k=DC)), writes=[("wg", s)], dma="wg%d" % s)
                    S.op("pool", lambda e: e.dma_start(out=wgu[s][1], in_=wu[j, f].rearrange(
                        "p (k n) -> p k n", k=DC)), writes=[("wu", s)], dma="wu%d" % s)
                    for ti in tl:
                        c0, n = TILES[ti]
                        lc = c0 - p0
                        pg, pgk = PS.next()
                        pu, puk = PS.next()

                        def mmg(e, lc=lc, n=n, pp=pg, which=0):
                            for k in range(DC):
                                ins = e.matmul(pp[:, :n], wgu[s][which][:, k, :], xn[:, k, lc:lc + n], start=(k == 0),
                                               stop=(k == DC - 1))
                            return ins

                        xk = [k_ for c in range(DC) for k_ in blkkeys("xn", c, lc, n)]
                        S.op("pe", mmg, reads=[("wg", s)] + xk, writes=[pgk])
                        S.op("pe", lambda e, lc=lc, n=n, pu=pu, mmg=mmg: mmg(e, lc, n, pu, 1),
                             reads=[("wu", s)] + xk, writes=[puk])
                        sg, sgk = sgp.next()
                        S.op("act", lambda e, sg=sg, pg=pg, n=n: e.activation(out=sg[:, :n], in_=pg[:, :n],
                                                                               func=AF.Silu),
                             reads=[pgk], writes=[sgk])
                        S.op("dve", lambda e, sg=sg, pu=pu, n=n, lc=lc: e.tensor_tensor(
                            out=act[:, f, lc:lc + n], in0=sg[:, :n], in1=pu[:, :n], op=ALU.mult),
                             reads=[sgk, puk], writes=blkkeys("act", f, lc, n))

                return [lambda f=f: one(f) for f in range(FC)]

            def down_thunks(tl):
                p0 = TILES[tl[0]][0]

                def one(dc):
                    s = fcount[1] % 2
                    fcount[1] += 1
                    S.op("pool", lambda e: e.dma_start(out=wdn[s], in_=wd[j, dc].rearrange(
                        "p (k n) -> p k n", k=FC)), writes=[("wd", s)], dma="wd%d" % s)
                    for ti in tl:
                        c0, n = TILES[ti]
                        lc = c0 - p0
                        py, pyk = PS.next()

                        def mmd(e, lc=lc, n=n, py=py):
                            for k in range(FC):
                                ins = e.matmul(py[:, :n], wdn[s][:, k, :], act[:, k, lc:lc + n], start=(k == 0),
                                               stop=(k == FC - 1))
                            return ins

                        S.op("pe", mmd, reads=[("wd", s)] + [k_ for f in range(FC) for k_ in blkkeys("act", f, lc, n)],
                             writes=[pyk])
                        S.op("act", lambda e, py=py, lc=lc, n=n: e.activation(out=Y[:, dc, lc:lc + n],
                                                                              in_=py[:, :n], func=AF.Copy),
                             reads=[pyk], writes=blkkeys("Y", dc, lc, n))

                return [lambda dc=dc: one(dc) for dc in range(DC)]

            def post_thunks(tl):
                p0 = TILES[tl[0]][0]
                out = []
                for ti in tl:
                    c0, n = TILES[ti]
                    lc = c0 - p0
                    out.append(lambda ti=ti, lc=lc, n=n: postnorm_add(
                        ti, slot_post, lambda c: Y[:, c, lc:lc + n], lambda c: blkkeys("Y", c, lc, n), True))
                return out

            def run(ths):
                for t in ths:
                    t()

            np_ = len(passes)
            run(pre_thunks(passes[0]))
            for p_i in range(np_):
                if p_i == 0:
                    run(gateup_thunks(passes[0]))
                if p_i + 1 < np_:
                    interleave(down_thunks(passes[p_i]), pre_thunks(passes[p_i + 1]))
                    interleave(gateup_thunks(passes[p_i + 1]), post_thunks(passes[p_i]))
                else:
                    run(down_thunks(passes[p_i]))
                    run(post_thunks(passes[p_i]))
            AR.off = m

        def rope_stage1(ps, psk, n):
            kb, kbk = misc["kb"].next()
            S.op("act", lambda e: e.activation(out=kb[:, :n], in_=ps[:, :n], func=AF.Copy), reads=[psk], writes=[kbk])
            return kb, kbk

        def rope_stage2(ps, psk, kb, kbk, c0t, n, dst_fn, dkey):
            CT_, ST_ = tabs["CT"], tabs["ST"]
            p2, p2k = PS.next()
            S.op("pe", lambda e: e.matmul(p2[:, :n], perm_sb, kb[:, :n], start=True, stop=True),
                 reads=[kbk, "perm"], writes=[p2k])
            t1, t1k = misc["tmp"].next()
            t2, t2k = misc["tmp"].next()
            S.op("dve", lambda e: e.tensor_tensor(out=t1[:, :n], in0=ps[:, :n], in1=CT_[:, c0t:c0t + n], op=ALU.mult),
                 reads=[psk, "tabc"], writes=[t1k])
            S.op("dve", lambda e: e.tensor_tensor(out=t2[:, :n], in0=p2[:, :n], in1=ST_[:, c0t:c0t + n], op=ALU.mult),
                 reads=[p2k, "tabs"], writes=[t2k])
            S.op("dve", lambda e: dst_fn(e, t1[:, :n], t2[:, :n]), reads=[t1k, t2k], writes=[dkey])

        def class_views(dst3, g, tt, full):
            d = GROUPS[g][1]
            base = dst3[:, 512 * tt:512 * tt + 512] if full else dst3
            if d == 1:
                return base, None
            if d == 4:
                return base.rearrange("p (r i) -> p r i", r=4), 4
            if full:
                return dst3.rearrange("p (r i) -> p r i", r=16)[:, :, 32 * tt:32 * tt + 32], 16
            return dst3.rearrange("p (r i) -> p r i", r=16), 16

        def proj_rope(xn_tile, xn_keys, wres, wkey, tt, dst_all, dname, full):
            pend = None
            for c in range(12):
                g = c // 4
                ps, psk = PS.next()

                def mm(e, c=c, ps=ps):
                    for k in range(DC):
                        ins = e.matmul(ps[:, :512], wres[:, k, c * 128:(c + 1) * 128], xn_tile[:, k, :],
                                       start=(k == 0), stop=(k == DC - 1))
                    return ins

                S.op("pe", mm, reads=[(wkey, c)] + xn_keys, writes=[psk])
                kb, kbk = rope_stage1(ps, psk, 512)
                ov, r = class_views(dst_all[:, c, :], g, tt, full)

                def fin(e, a, b, ov=ov, r=r):
                    if r is not None:
                        a = a.rearrange("p (i r) -> p r i", r=r)
                        b = b.rearrange("p (i r) -> p r i", r=r)
                    return e.tensor_tensor(out=ov, in0=a, in1=b, op=ALU.add)

                if pend is not None:
                    rope_stage2(*pend)
                pend = (ps, psk, kb, kbk, 512 * tt, 512, fin, (dname, c, tt if full else 0))
            rope_stage2(*pend)

        def layer0_pass(xsrc, possrc, corrsrc, prev):
            S.fence()
            for c in range(DC):
                S.op("sp", lambda e, c=c: e.dma_start(out=H[:, c, :], in_=xsrc[:, c, :]),
                     writes=[Hk(c, t) for t in range(5)], dma="ldx")
            if A_:
                if upto >= 1:
                    ffn(0, 0, 1, [[0, 1, 2], [3, 4]])

            if A_ and upto >= 2:
                S.fence()
                m = AR.off
                alloc_common()
                win_sb = AR.alloc((DC, D), BF16)
                wgr_sb = AR.alloc((4, 2, 256), BF16)
                wout_sb = AR.alloc((DC, D), BF16)
                corr_sb = AR.alloc((4, 128))
                load_dense_cols(win_sb, w_in, "w_in", 128)
                S.op("pool", lambda e: e.dma_start(out=wgr_sb, in_=w_grp), writes=["w_grp"], dma="wdense")
                load_dense_cols(wout_sb, w_out, "w_out", 128)
                S.op("sp", lambda e: e.dma_start(out=corr_sb, in_=corrsrc), writes=["corr"], dma="ldm")
                hm = AR.alloc((DC, 512), BF16)
                U = [AR.alloc((DC, 528)) for _ in range(2)]
                Ta = AR.alloc(528)
                Tb = AR.alloc(528)
                pTb = [AR.alloc((DC, 512), BF16) for _ in range(2)]
                yT = AR.alloc((DC, 512), BF16)
                Ym = AR.alloc((DC, 512))

                def stage_A(ti):
                    n = TILES[ti][1]
                    prenorm(ti, 2, lambda c: hm[:, c, :n], lambda c: ("hm", c))

                def stage_B(ti):
                    c0, n = TILES[ti]
                    cur = U[ti % 2]
                    nxt = U[(ti + 1) % 2]
                    ck = "U%d" % (ti % 2)
                    nk = "U%d" % ((ti + 1) % 2)
                    pT = pTb[ti % 2]
                    pk = "pT%d" % (ti % 2)
                    for c in range(DC):
                        ps, psk = PS.next()

                        def mm(e, c=c, ps=ps):
                            for k in range(DC):
                                ins = e.matmul(ps[:, :n], win_sb[:, k, c * 128:(c + 1) * 128], hm[:, k, :n],
                                               start=(k == 0), stop=(k == DC - 1))
                            return ins

                        S.op("pe", mm, reads=[("w_in", c)] + [("hm", k) for k in range(DC)], writes=[psk])
                        if ti == 0:
                            S.op("act", lambda e, c=c, ps=ps: e.activation(out=nxt[:, c, 0:16],
                                                                           in_=ps[:, HALO - 16:HALO], func=AF.Copy),
                                 reads=[psk], writes=[(nk, c)])
                            continue
                        S.op("act", lambda e, c=c, ps=ps: e.activation(out=cur[:, c, 16:528], in_=ps[:, :512],
                                                                       func=AF.Copy),
                             reads=[psk], writes=[(ck, c)])
                        if ti < 4:
                            S.op("act", lambda e, c=c: e.activation(out=nxt[:, c, 0:16], in_=cur[:, c, 512:528],
                                                                    func=AF.Copy),
                                 reads=[(ck, c)], writes=[(nk, c)])
                        gi = c // 2
                        w = 2 << gi
                        Uc = cur[:, c, :]
                        src = Uc
                        srck = (ck, c)
                        sh = 1
                        bufs = [(Ta, "Ta"), (Tb, "Tb")]
                        bi = 0
                        while sh < w:
                            dstb, dk = bufs[bi]
                            S.op("dve", lambda e, src=src, dstb=dstb, sh=sh: e.tensor_tensor(
                                out=dstb[:, sh:528], in0=src[:, sh:528], in1=src[:, 0:528 - sh], op=ALU.add),
                                 reads=[srck], writes=[dk])
                            src, srck = dstb, dk
                            bi ^= 1
                            sh *= 2
                        if ti == 1:
                            S.op("dve", lambda e, src=src, gi=gi: e.tensor_tensor(out=src[:, 16:144], in0=src[:, 16:144],
                                                                                  in1=corr_sb[:, gi, :], op=ALU.mult),
                                 reads=[srck, "corr"], writes=[srck])
                        S.op("dve", lambda e, src=src, w=w, Uc=Uc, c=c: e.scalar_tensor_tensor(
                            out=pT[:, c, :], in0=src[:, 16:528], scalar=1.0 / w, in1=Uc[:, 16:528], op0=ALU.mult,
                            op1=ALU.subtract), reads=[srck, (ck, c)], writes=[(pk, c)])

                def stage_C(ti):
                    pT = pTb[ti % 2]
                    pk = "pT%d" % (ti % 2)
                    for dc in range(DC):
                        gi = dc // 2
                        ps, psk = PS.next()

                        def mm(e, dc=dc, gi=gi, ps=ps):
                            for cc in range(2):
                                ins = e.matmul(ps[:, :512], wgr_sb[:, gi, cc, (dc % 2) * 128:(dc % 2) * 128 + 128],
                                               pT[:, 2 * gi + cc, :], start=(cc == 0), stop=(cc == 1))
                            return ins

                        S.op("pe", mm, reads=["w_grp", (pk, 2 * gi), (pk, 2 * gi + 1)], writes=[psk])
                        S.op("act", lambda e, dc=dc, ps=ps: e.activation(out=yT[:, dc, :], in_=ps[:, :512], func=AF.Copy,
                                                                         scale=G[:, 56 + dc:57 + dc]),
                             reads=[psk, "G"], writes=[("yT", dc)])
                    for dc in range(DC):
                        ps, psk = PS.next()

                        def mm(e, dc=dc, ps=ps):
                            for k in range(DC):
                                ins = e.matmul(ps[:, :512], wout_sb[:, k, dc * 128:(dc + 1) * 128], yT[:, k, :],
                                               start=(k == 0), stop=(k == DC - 1))
                            return ins

                        S.op("pe", mm, reads=[("w_out", dc)] + [("yT", k) for k in range(DC)], writes=[psk])
                        S.op("act", lambda e, dc=dc, ps=ps: e.activation(out=Ym[:, dc, :], in_=ps[:, :512], func=AF.Copy),
                             reads=[psk], writes=[("Ym", dc)])
                    postnorm_add(ti, 3, lambda c: Ym[:, c, :], lambda c: ("Ym", c), False)

                stage_A(0)
                stage_B(0)
                stage_A(1)
                stage_B(1)
                for ti in range(1, 5):
                    if ti < 4:
                        stage_A(ti + 1)
                        stage_B(ti + 1)
                    stage_C(ti)
                AR.off = m

            if A_ and upto >= 3:
                ffn(1, 4, 5, [[1, 2], [3, 4]])

            if upto < 5 and not prev:
                S.fence()
                for c in range(DC):
                    S.op("sp", lambda e, c=c: e.dma_start(out=hT_out[:, c, :], in_=H[:, c, HALO:]),
                         reads=[Hk(c, t) for t in range(5)], dma="stH")
            if A_ and upto >= 4:
                S.fence()
                m = AR.off
                alloc_common()
                misc["kb"] = Pool("kb", [AR.alloc(512, BF16) for _ in range(2)])
                rope_tables(tab_prev if prev else tab_own)
                wk_sb = AR.alloc((DC, QKV), BF16)
                wv_sb = AR.alloc((DC, QKV), BF16)
                load_dense_cols(wk_sb, w_k, "w_k", 128)
                load_dense_cols(wv_sb, w_v, "w_v", 512)
                Kall = AR.alloc((12, 512), BF16)
                hkb = [AR.alloc((DC, 512), BF16) for _ in range(2)]
                hk16 = AR.alloc((DC, 512), BF16)
                vsb = Pool("vsb", [AR.alloc(512, BF16) for _ in range(3)])
                prenorm(1, 6, lambda c: hkb[0][:, c, :], lambda c: ("hk0", c))
                for tt in range(4 if upto >= 4.1 else 0):
                    ti = tt + 1
                    hk = hkb[tt % 2]
                    hkn = "hk%d" % (tt % 2)
                    if tt < 3:
                        prenorm(ti + 1, 6, lambda c, tt=tt: hkb[(tt + 1) % 2][:, c, :],
                                lambda c, tt=tt: ("hk%d" % ((tt + 1) % 2), c))
                    hkk = [(hkn, c) for c in range(DC)]
                    for c in range(DC if upto >= 4.3 else 0):
                        S.op("act", lambda e, c=c, hk=hk: e.activation(
                            out=hk16[:, c, :].rearrange("p (r i) -> p r i", r=16),
                            in_=hk[:, c, :].rearrange("p (i r) -> p r i", r=16), func=AF.Copy),
                             reads=[(hkn, c)], writes=[("hk16", c)])
                    proj_rope(hk, hkk, wk_sb, "w_k", tt, Kall, "Kall", False)
                    for g in range(3):
                        R = GROUPS[g][1]
                        kd = kin[g].rearrange("j p x -> p j x")
                        ksrc = Kall[:, 4 * g:4 * g + 4, :]
                        rk = [("Kall", c, 0) for c in range(4 * g, 4 * g + 4)]
                        if g < 2:
                            if prev and tt < 3:
                                continue
                            if prev and g == 0:
                                S.op("sp", lambda e, kd=kd, ksrc=ksrc: e.dma_start(out=kd[:, :, 0:128], in_=ksrc[:, :, 384:512]),
                                     reads=rk, dma="stK%d" % g)
                            elif prev:
                                S.op("sp", lambda e, kd=kd, ksrc=ksrc: e.dma_start(out=kd[:, :, 0:512], in_=ksrc), reads=rk, dma="stK%d" % g)
                            else:
                                o = R * 128 + 512 * tt
                                S.op("sp", lambda e, kd=kd, ksrc=ksrc, o=o: e.dma_start(out=kd[:, :, o:o + 512], in_=ksrc),
                                     reads=rk, dma="stK%d" % g)
                        else:
                            o = 0 if prev else 2048
                            for jj in range(4):
                                S.op("sp", lambda e, jj=jj, o=o, tt=tt: e.dma_start(
                                    out=kin[2][jj][:, o:o + 2048].rearrange("p (r i) -> p r i", r=16)[:, :, 32 * tt:32 * tt + 32],
                                    in_=Kall[:, 8 + jj, :].rearrange("p (r i) -> p r i", r=16)),
                                     reads=[("Kall", 8 + jj, 0)], dma="stK2%d" % jj)
                    for g in range(3 if upto >= 4.3 else 0):
                        d = GROUPS[g][1]
                        for b in range(4):
                            R = d
                            vd = vin[g].rearrange("j p b f -> p j b f")
                            if d == 1:
                                cols = lambda k, b=b, hk=hk: hk[:, k, 128 * b:128 * b + 128]
                                cls = [4 * tt + b]
                            elif d == 4:
                                cols = lambda k, b=b, hk=hk: hk[:, k, :].rearrange("p (i r) -> p r i", r=4)[:, b, :]
                                cls = [4 * tt + b]
                            else:
                                cols = lambda k, b=b: hk16[:, k, 128 * b:128 * b + 128]
                                cls = [4 * b + rr for rr in range(4)]
                            blks = [(k_ - (16 - R)) if prev else (R + k_) for k_ in cls]
                            if blks[0] < 0:
                                continue
                            ps, psk = PS.next()

                            def mm(e, cols=cols, ps=ps, g=g):
                                for k in range(DC):
                                    ins = e.matmul(ps[:, :512], cols(k), wv_sb[:, k, g * 512:(g + 1) * 512],
                                                   start=(k == 0), stop=(k == DC - 1))
                                return ins

                            S.op("pe", mm, reads=[("w_v", g)] + hkk + [("hk16", c) for c in range(DC)], writes=[psk])
                            vb, vbk = vsb.next()
                            S.op("act", lambda e, vb=vb, ps=ps: e.activation(out=vb, in_=ps[:, :512], func=AF.Copy),
                                 reads=[psk], writes=[vbk])
                            if d == 16:
                                for rr in range(4):
                                    S.op("sp", lambda e, vb=vb, vd=vd, rr=rr, tt=tt, blk=blks[rr]: e.dma_start(
                                        out=vd[32 * tt:32 * tt + 32, :, blk, :],
                                        in_=vb[32 * rr:32 * rr + 32, :].rearrange("p (j f) -> p j f", j=4)), reads=[vbk], dma="stV%d" % vbk[1])
                            else:
                                S.op("sp", lambda e, vb=vb, vd=vd, blk=blks[0]: e.dma_start(
                                    out=vd[:, :, blk, :], in_=vb.rearrange("p (j f) -> p j f", j=4)), reads=[vbk], dma="stV%d" % vbk[1])
                AR.off = m

        layer0_pass(xT_prev, pos_prev, corr_prev, True)
        layer0_pass(xT_own, pos_own, corr_own, False)

        if upto >= 5:
            ffn(2, 8, 9, [[1, 2], [3, 4]])

        if upto >= 5.15:
            S.fence()
            m = AR.off
            alloc_common()
            misc["kb"] = Pool("kb", [AR.alloc(512, BF16) for _ in range(2)])
            Qall = AR.alloc((12, NT), BF16)
            m2 = AR.off
            rope_tables(tab_own)
            wq_sb = AR.alloc((DC, QKV), BF16)
            load_dense_cols(wq_sb, w_q, "w_q", 128)
            hmqb = [AR.alloc((DC, 512), BF16) for _ in range(2)]
            prenorm(1, 10, lambda c: hmqb[0][:, c, :], lambda c: ("hmq0", c))
            for tt in range(4):
                ti = tt + 1
                if tt < 3:
                    prenorm(ti + 1, 10, lambda c, tt=tt: hmqb[(tt + 1) % 2][:, c, :],
                            lambda c, tt=tt: ("hmq%d" % ((tt + 1) % 2), c))
                proj_rope(hmqb[tt % 2], [("hmq%d" % (tt % 2), c) for c in range(DC)], wq_sb, "w_q", tt, Qall, "Qall",
                          True)
            S.fence()
            AR.off = m2
            mk = AR.alloc((2, 512), BF16)
            S.op("sp", lambda e: e.dma_start(out=mk, in_=masks), writes=["mk"], dma="ldm")
            S.op("sp", lambda e: e.dma_start(out=ident, in_=masks_id), writes=["ident"], dma="ldm")
            wo_sb = AR.alloc((4, D), BF16)
            load_dense_cols(wo_sb, w_o, "w_o", 128)
            OT = AR.alloc((4, NT), BF16)
            m3 = AR.off
            ND = AR.alloc((2, NT))
            kbuf = [AR.alloc(32 * 128, BF16) for _ in range(2)]
            vbuf = [AR.alloc((32, 128), BF16) for _ in range(2)]
            ptp = Pool("pt", [AR.alloc(512, BF16) for _ in range(3)])
            Qk = [("Qall", c, tt) for c in range(12) for tt in range(4)]
            slot_ctr = [0]

            def begin_group(jj, g):
                R = GROUPS[g][1]
                nb = R + 16
                s_ = slot_ctr[0] % 2
                slot_ctr[0] += 1
                S.op("sp", lambda e: e.dma_start(out=kbuf[s_][:, :nb * 128], in_=kin[g][jj]),
                     writes=[("kbuf", s_)], dma="kb%d" % s_)
                S.op("sp", lambda e: e.dma_start(out=vbuf[s_][:, :nb, :], in_=vin[g][jj]),
                     writes=[("vbuf", s_)], dma="vb%d" % s_)
                return dict(jj=jj, g=g, R=R, s=s_, Qc=Qall[:, 4 * g + jj, :])

            def S_stage(ctx, k):
                jj, g, R, s_, Qc = ctx["jj"], ctx["g"], ctx["R"], ctx["s"], ctx["Qc"]
                first = k < R
                pt, ptk = ptp.next()
                for hh in range(2):
                    pss, pssk = PS.next()

                    def mms(e, pss=pss, hh=hh):
                        e.matmul(pss[:, :256], ident, mk[:, 1 if first else 0, 0:256], start=True, stop=False)
                        q = Qc[:, 128 * k:128 * k + 128]
                        lo = 64 * hh
                        for w_, kb_ in enumerate((k, k + R)):
                            ins = e.matmul(pss[:, w_ * 128:(w_ + 1) * 128],
                                           kbuf[s_][lo:lo + 64, kb_ * 128:(kb_ + 1) * 128], q[lo:lo + 64, :],
                                           start=False, stop=(w_ == 1))
                        return ins

                    S.op("pe", mms, reads=[("kbuf", s_), "mk", "ident"] + [("Qall", 4 * g + jj, tt) for tt in
                                                                            range(4)], writes=[pssk])
                    S.op("act", lambda e, pss=pss, hh=hh: e.activation(
                        out=pt[:, 256 * hh:256 * hh + 256], in_=pss[:, :256], func=AF.Exp, scale=0.125),
                         reads=[pssk], writes=[(ptk, hh)])
                return pt, ptk

            def PV_stage(ctx, k, pt, ptk):
                g, R, s_ = ctx["g"], ctx["R"], ctx["s"]
                pso, psok = PS.next()

                def mmo(e):
                    for hh in range(2):
                        lo = 64 * hh
                        for w_, kb_ in enumerate((k, k + R)):
                            e.matmul(pso[lo:lo + 64, 0:128], vbuf[s_][:, kb_, lo:lo + 64],
                                     pt[:, (2 * hh + w_) * 128:(2 * hh + w_ + 1) * 128], start=(w_ == 0),
                                     stop=(w_ == 1))
                        for w_ in range(2):
                            ins = e.matmul(pso[lo:lo + 64, 128:256], ones1[:, 0:64],
                                           pt[:, (2 * hh + w_) * 128:(2 * hh + w_ + 1) * 128], start=(w_ == 0),
                                           stop=(w_ == 1))
                    return ins

                S.op("pe", mmo, reads=[("vbuf", s_), (ptk, 0), (ptk, 1), "ones1"], writes=[psok])
                if R == 1:
                    ndv = ND[:, :, 128 * k:128 * k + 128]
                elif R == 4:
                    n_, r_ = k // 4, k % 4
                    ndv = ND[:, :, 512 * n_:512 * n_ + 512].rearrange("p x (i r) -> p x r i", r=4)[:, :, r_, :]
                else:
                    ndv = ND.rearrange("p x (i r) -> p x r i", r=16)[:, :, k, :]
                psv = pso[:, 0:256].rearrange("p (x i) -> p x i", x=2)
                if g == 0:
                    S.op("dve", lambda e: e.tensor_copy(out=ndv, in_=psv), reads=[psok], writes=["NDall"])
                else:
                    S.op("dve", lambda e: e.tensor_tensor(out=ndv, in0=psv, in1=ndv, op=ALU.add),
                         reads=[psok, "NDall"], writes=["NDall"])

            def end_jj(jj):
                for tt in range(4):
                    S.op("dve", lambda e, tt=tt: e.reciprocal(out=ND[:, 1, 512 * tt:512 * tt + 512],
                                                              in_=ND[:, 1, 512 * tt:512 * tt + 512]),
                         reads=["NDall"], writes=["NDall"])
                    S.op("dve", lambda e, tt=tt: e.tensor_tensor(out=OT[:, jj, 512 * tt:512 * tt + 512],
                                                                 in0=ND[:, 0, 512 * tt:512 * tt + 512],
                                                                 in1=ND[:, 1, 512 * tt:512 * tt + 512],
                                                                 op=ALU.mult),
                         reads=["NDall"], writes=[("OT", jj, tt)])

            steps = [(jj, g, k) for jj in range(4) for g in range(3) for k in range(16)]
            ctxs = {}

            def get_ctx(jj, g):
                if (jj, g) not in ctxs:
                    ctxs[(jj, g)] = begin_group(jj, g)
                return ctxs[(jj, g)]

            cur = S_stage(get_ctx(0, 0), 0)
            for i, (jj, g, k) in enumerate(steps):
                nxt = None
                if i + 1 < len(steps):
                    j2, g2, k2 = steps[i + 1]
                    nxt = S_stage(get_ctx(j2, g2), k2)
                PV_stage(get_ctx(jj, g), k, *cur)
                if g == 2 and k == 15:
                    end_jj(jj)
                cur = nxt
            S.fence()
            AR.off = m3
            Ym = AR.alloc((DC, 512))
            for tt in range(4):
                ti = tt + 1
                for dc in range(DC):
                    ps, psk = PS.next()

                    def mm(e, dc=dc, ps=ps, tt=tt):
                        for k in range(4):
                            ins = e.matmul(ps[:, :512], wo_sb[:, k, dc * 128:(dc + 1) * 128],
                                           OT[:, k, 512 * tt:512 * tt + 512], start=(k == 0), stop=(k == 3))
                        return ins

                    S.op("pe", mm, reads=[("w_o", dc)] + [("OT", k, tt) for k in range(4)], writes=[psk])
                    S.op("act", lambda e, dc=dc, ps=ps: e.activation(out=Ym[:, dc, :], in_=ps[:, :512], func=AF.Copy),
                         reads=[psk], writes=[("Ym", dc)])
                postnorm_add(ti, 11, lambda c: Ym[:, c, :], lambda c: ("Ym", c), False)
            AR.off = m

        if upto >= 5.25:
            ffn(3, 12, 13, [[1, 2], [3, 4]])
        if upto >= 5:
            S.fence()
            for c in range(DC):
                S.op("sp", lambda e, c=c: e.dma_start(out=hT_out[:, c, :], in_=H[:, c, HALO:]),
                     reads=[Hk(c, t) for t in range(5)], dma="stH")

        S.fence()
        final_waits = dict(S.fence_tok)

        sems = {}
        for e in ENGS:
            sems[("e", e)] = es.enter_context(nc.semaphore("sem_" + e))
        for k in S.dcnt:
            sems[("d", k)] = es.enter_context(nc.semaphore("dsem_" + k))
        engmap = {"pe": "tensor", "act": "scalar", "dve": "vector", "pool": "gpsimd", "sp": "sync"}
        with nc.Block() as block:
            def make(eng):
                def body(e):
                    for waits, fn, tok in S.q[eng]:
                        for sk, v in waits:
                            e.wait_ge(sems[sk], v)
                        ins = fn(e)
                        ins.then_inc(sems[tok[0]], 16 if tok[0][0] == "d" else 1)
                    if eng == "sp":
                        for sk, v in final_waits.items():
                            if v > 0:
                                e.wait_ge(sems[sk], v)
                return body

            for eng in ENGS:
                getattr(block, engmap[eng])(make(eng))
    return nc


def _fm(a):
    T, F = a.shape
    return np.ascontiguousarray(a.T.reshape(F // 128, 128, T).transpose(1, 0, 2))


def _dense(w):
    k, n = w.shape
    return np.ascontiguousarray(w.reshape(k // 128, 128, n).transpose(1, 0, 2))


def _ffn_w(gate, up, down):
    gate = gate.reshape(4, DC, 128, FC, 128)
    up = up.reshape(4, DC, 128, FC, 128)
    down = down.reshape(4, FC, 128, DC, 128)
    wg = np.ascontiguousarray(gate.transpose(0, 3, 2, 1, 4)).reshape(4, FC, 128, DC * 128)
    wu = np.ascontiguousarray(up.transpose(0, 3, 2, 1, 4)).reshape(4, FC, 128, DC * 128)
    wd = np.ascontiguousarray(down.transpose(0, 3, 2, 1, 4)).reshape(4, DC, 128, FC * 128)
    return wg, wu, wd


def _vecs(norm_gain, kv_gain, pool_scale):
    v = np.zeros((128, 14 * 8), np.float32)
    for i in range(6):
        v[:, i * 8:(i + 1) * 8] = norm_gain[0, i].reshape(8, 128).T
        v[:, (8 + i) * 8:(9 + i) * 8] = norm_gain[1, i].reshape(8, 128).T
    v[:, 48:56] = kv_gain.reshape(8, 128).T
    v[:, 56:64] = pool_scale.reshape(8, 128).T
    return v


_NC_CACHE = {}


def _get_nc():
    if "F" not in _NC_CACHE:
        _NC_CACHE["F"] = build()
    return _NC_CACHE["F"]


def _corr(is_seq_start):
    corr = np.ones((128, 4, 128), np.float32)
    if is_seq_start:
        t = np.arange(128)
        for g, w in enumerate((2, 4, 8, 16)):
            corr[:, g, :] = (w / np.minimum(t + 1, w)).astype(np.float32)[None, :]
    return corr


def make_in_maps(x, positions, norm_gain, ffn_w_gate, ffn_w_up, ffn_w_down, pool_w_in, pool_w_group, pool_scale,
                 pool_w_out, kv_norm_gain, w_k, w_v, attn_w_q, attn_w_o):
    x = np.asarray(x, np.float32)
    positions = np.asarray(positions, np.int32)
    bf = ml_dtypes.bfloat16
    p = np.arange(128)
    inv_freq = (np.float32(10000.0) ** (-(np.arange(0, 64, 2, dtype=np.float32)) / np.float32(64))).astype(np.float32)
    rconst = np.stack([inv_freq[p % 32], np.where((p % 64) < 32, -1.0, 1.0)], axis=1).astype(np.float32)
    partner = np.where((p % 64) < 32, p + 32, p - 32)
    permM = np.zeros((128, 128), np.float32)
    permM[partner, p] = 1.0
    permM = permM.astype(bf)
    wg, wu, wd = _ffn_w(np.asarray(ffn_w_gate, np.float32), np.asarray(ffn_w_up, np.float32),
                        np.asarray(ffn_w_down, np.float32))
    vecs = _vecs(np.asarray(norm_gain, np.float32), np.asarray(kv_norm_gain, np.float32),
                 np.asarray(pool_scale[0], np.float32))
    common = dict(
        wg=wg, wu=wu, wd=wd, vecs=vecs, rconst=rconst, permM=permM,
        w_in=_dense(np.asarray(pool_w_in[0], np.float32)), w_out=_dense(np.asarray(pool_w_out[0], np.float32)),
        w_grp=np.ascontiguousarray(np.asarray(pool_w_group[0], np.float32).reshape(4, 2, 128, 256).transpose(2, 0, 1, 3)),
        w_k=_dense(np.asarray(w_k, np.float32)), w_v=_dense(np.asarray(w_v, np.float32)),
        w_q=_dense(np.asarray(attn_w_q[0], np.float32)), w_o=_dense(np.asarray(attn_w_o[0], np.float32)),
        masks_id=np.eye(128, dtype=np.float32).astype(bf))
    kk = np.arange(128)[:, None]
    qq = np.arange(128)[None, :]
    mprev = np.where(kk >= qq, 0.0, NEG).astype(np.float32)
    mcur = np.where(kk <= qq, 0.0, NEG).astype(np.float32)
    mall = np.full((128, 128), NEG, np.float32)
    MN = np.concatenate([mprev, mcur, mprev, mcur], axis=1)
    MF0 = np.concatenate([mall, mcur, mall, mcur], axis=1)
    in_maps = []
    for c in range(8):
        b, q = divmod(c, 4)
        s0 = q * NT

        def xslice(start):
            xs = np.zeros((HC, D), np.float32)
            lo = start - HALO
            if start >= 0:
                a = max(lo, 0)
                xs[a - lo:] = x[b, a:start + NT]
            return _fm(xs)

        pp = s0 - NT
        pos_prev = positions[b:b + 1, pp:pp + NT] if q > 0 else positions[b:b + 1, 0:NT]
        d = dict(common)
        d.update(xT_prev=xslice(s0 - NT if q > 0 else -10 ** 9), xT_own=xslice(s0),
                 pos_prev=np.ascontiguousarray(pos_prev), pos_own=np.ascontiguousarray(positions[b:b + 1, s0:s0 + NT]),
                 corr_prev=_corr(q == 1), corr_own=_corr(q == 0),
                 masks=np.stack([MN, MF0 if q == 0 else MN], axis=1).astype(bf))
        in_maps.append(d)
    return in_maps


def kernel(x, positions, norm_gain, ffn_w_gate, ffn_w_up, ffn_w_down, pool_w_in, pool_w_group, pool_scale,
           pool_w_out, kv_norm_gain, w_k, w_v, attn_w_q, attn_w_o):
    in_maps = make_in_maps(x, positions, norm_gain, ffn_w_gate, ffn_w_up, ffn_w_down, pool_w_in, pool_w_group,
                           pool_scale, pool_w_out, kv_norm_gain, w_k, w_v, attn_w_q, attn_w_o)
    res = run_bass_kernel_spmd(_get_nc(), in_maps, core_ids=list(range(8))).results
    out = np.zeros((2, 8192, D), np.float32)
    for c in range(8):
        b, q = divmod(c, 4)
        hT = np.asarray(res[c]["hT"])
        out[b, q * NT:(q + 1) * NT] = hT.transpose(2, 1, 0).reshape(NT, D)
    return out
```
